# Optimizing a Trainium2 kernel written in Bass

```python
import math
import jax, jax.numpy as jnp
from jax import lax
import numpy as np

D_MODEL = 4096
BATCH = 4
SEQ = 2048
DEPTH = 2
DEC_BATCH = 16
DEC_SEQ = 16
PAST_LEN = 4096

CHUNK = 64
Q_BLOCK = 128
MLA_HEADS = 16
MLA_NOPE = 128
MLA_ROPE = 64
MLA_V = 128
Q_LORA = 1024
KV_LORA = 512
ROPE_THETA = 10000.0
MLA_WIDTH = MLA_HEADS * MLA_V
MLA_SCALE = 1.0 / math.sqrt(MLA_NOPE + MLA_ROPE)
POOL_WINDOWS = (2, 4, 8, 16)
POOL_GROUP = D_MODEL // 16
POOL_WIDTH = POOL_GROUP * len(POOL_WINDOWS)
POOL_HIST = max(POOL_WINDOWS) - 1
SB_HEADS = 8
SB_HEAD_DIM = 128
SB_WIDTH = SB_HEADS * SB_HEAD_DIM
SB_SCALE = 1.0 / math.sqrt(SB_HEAD_DIM)
MIX_WIDTH = MLA_WIDTH + POOL_WIDTH + SB_WIDTH
D_FF = 11008
EPS = 1e-6
IN_SPLITS = (Q_LORA, KV_LORA, MLA_ROPE, POOL_WIDTH, SB_WIDTH, SB_WIDTH, SB_WIDTH)
IN_WIDTH = sum(IN_SPLITS)

kernel_name = "hybrid_mla_pool_stickbreak_streaming_step"


def rmsnorm(x, g):
    xf = x.astype(jnp.float32)
    y = xf * lax.rsqrt(jnp.mean(xf * xf, axis=-1, keepdims=True) + EPS)
    return y.astype(x.dtype) * g


def rope(x, pos):
    half = x.shape[-1] // 2
    inv = 1.0 / (ROPE_THETA ** (jnp.arange(half, dtype=jnp.float32) / half))
    ang = pos.astype(jnp.float32)[:, None] * inv[None, :]
    shape = (pos.shape[0],) + (1,) * (x.ndim - 3) + (half,)
    cos = jnp.cos(ang).reshape(shape)
    sin = jnp.sin(ang).reshape(shape)
    xf = x.astype(jnp.float32)
    x1, x2 = xf[..., :half], xf[..., half:]
    return jnp.concatenate([x1 * cos - x2 * sin, x2 * cos + x1 * sin], axis=-1).astype(x.dtype)


def swiglu(x, w_gate, w_up, w_down):
    return (jax.nn.silu(x @ w_gate) * (x @ w_up)) @ w_down


def mla_block(q_lat, q_pe, ckv, kpe, qpos, kpos):
    s = (jnp.einsum('bthc,bsc->bhts', q_lat, ckv)
         + jnp.einsum('bthr,bsr->bhts', q_pe, kpe)).astype(jnp.float32) * MLA_SCALE
    mask = (kpos[None, :] // CHUNK) <= (qpos[:, None] // CHUNK)
    s = jnp.where(mask, s, -jnp.inf)
    p = jax.nn.softmax(s, axis=-1).astype(ckv.dtype)
    return jnp.einsum('bhts,bsc->bthc', p, ckv)


def sb_block(q, k, v, qpos, kpos):
    z = jnp.einsum('bthd,bshd->bhts', q, k).astype(jnp.float32) * SB_SCALE
    mask = kpos[None, :] < qpos[:, None]
    l = jnp.where(mask, jax.nn.log_sigmoid(-z), 0.0)
    excl = lax.cumsum(l, axis=3, reverse=True) - l
    a = jnp.where(mask, jnp.exp(jax.nn.log_sigmoid(z) + excl), 0.0).astype(v.dtype)
    return jnp.einsum('bhts,bshd->bthd', a, v)


def pool_mix(u, hist, pos, w_pool, pool_scale):
    B, T, C = u.shape
    ext = jnp.concatenate([hist, u], axis=1).astype(jnp.float32)
    cs = jnp.concatenate([jnp.zeros((B, 1, C), jnp.float32), lax.cumsum(ext, axis=1)], axis=1)
    end = cs[:, POOL_HIST + 1:POOL_HIST + 1 + T]
    groups = []
    for g, w in enumerate(POOL_WINDOWS):
        c0, c1 = g * POOL_GROUP, (g + 1) * POOL_GROUP
        win = end[..., c0:c1] - cs[:, POOL_HIST + 1 - w:POOL_HIST + 1 - w + T, c0:c1]
        cnt = jnp.minimum(pos + 1, w).astype(jnp.float32)[None, :, None]
        groups.append(win / cnt - ext[:, POOL_HIST:, c0:c1])
    pooled = jnp.stack(groups, axis=2).astype(u.dtype)
    out = jnp.einsum('btgc,gcd->btgd', pooled, w_pool).reshape(B, T, POOL_WIDTH)
    return out * pool_scale


def trunk_layer(x, ckv_past, kpe_past, pool_hist, sbk_past, sbv_past,
                g_ffn1, w1_gate, w1_up, w1_down, g_mix, w_in, g_qnorm, w_uq,
                g_kvnorm, w_uk, w_uv, w_pool, pool_scale, g_mla_out, g_sb_out,
                w_out, g_ffn2, w2_gate, w2_up, w2_down):
    B, T, _ = x.shape
    P = ckv_past.shape[1]
    kpos = jnp.arange(P + T)
    pos = kpos[P:]
    x = x + 0.5 * swiglu(rmsnorm(x, g_ffn1), w1_gate, w1_up, w1_down)
    h = rmsnorm(x, g_mix)
    proj = h @ w_in
    c_q, c_kv, k_pe, u, q_sb, k_sb, v_sb = jnp.split(
        proj, np.cumsum(IN_SPLITS)[:-1].tolist(), axis=-1)
    q = (rmsnorm(c_q, g_qnorm) @ w_uq).reshape(B, T, MLA_HEADS, MLA_NOPE + MLA_ROPE)
    q_nope = q[..., :MLA_NOPE]
    q_pe = rope(q[..., MLA_NOPE:], pos)
    q_lat = jnp.einsum('bthd,chd->bthc', q_nope, w_uk)
    ckv_new = rmsnorm(c_kv, g_kvnorm)
    kpe_new = rope(k_pe, pos)
    ckv_all = jnp.concatenate([ckv_past, ckv_new], axis=1)
    kpe_all = jnp.concatenate([kpe_past, kpe_new], axis=1)
    q_sb = q_sb.reshape(B, T, SB_HEADS, SB_HEAD_DIM)
    k_sb = k_sb.reshape(B, T, SB_HEADS, SB_HEAD_DIM)
    v_sb = v_sb.reshape(B, T, SB_HEADS, SB_HEAD_DIM)
    k_all = jnp.concatenate([sbk_past, k_sb], axis=1)
    v_all = jnp.concatenate([sbv_past, v_sb], axis=1)
    mla_outs, sb_outs = [], []
    for s0 in range(0, T, Q_BLOCK):
        s1 = min(s0 + Q_BLOCK, T)
        kend = P + s1
        qp, kp = pos[s0:s1], kpos[:kend]
        mla_outs.append(mla_block(q_lat[:, s0:s1], q_pe[:, s0:s1],
                                  ckv_all[:, :kend], kpe_all[:, :kend], qp, kp))
        sb_outs.append(sb_block(q_sb[:, s0:s1], k_all[:, :kend], v_all[:, :kend], qp, kp))
    o_lat = jnp.concatenate(mla_outs, axis=1)
    o_mla = jnp.einsum('bthc,chd->bthd', o_lat, w_uv).reshape(B, T, MLA_WIDTH)
    o_sb = jnp.concatenate(sb_outs, axis=1).reshape(B, T, SB_WIDTH)
    o_pool = pool_mix(u, pool_hist, pos, w_pool, pool_scale)
    mixed = jnp.concatenate([rmsnorm(o_mla, g_mla_out), o_pool, rmsnorm(o_sb, g_sb_out)], axis=-1)
    x = x + mixed @ w_out
    x = x + 0.5 * swiglu(rmsnorm(x, g_ffn2), w2_gate, w2_up, w2_down)
    new_hist = jnp.concatenate([pool_hist, u], axis=1)[:, -POOL_HIST:]
    return x, ckv_new, kpe_new, new_hist, k_sb, v_sb


def setup_inputs(seed: int = 0) -> dict:
    key = jax.random.key(seed)
    ks = iter(jax.random.split(key, 40))
    f32 = jnp.float32

    def nrm(shape, scale=1.0):
        return jax.random.normal(next(ks), shape, f32) * scale

    def gain(shape):
        return 1.0 + 0.02 * jax.random.normal(next(ks), shape, f32)

    L = DEPTH
    return {
        "x_prompt": nrm((BATCH, SEQ, D_MODEL)),
        "x_sample": nrm((DEC_BATCH, DEC_SEQ, D_MODEL)),
        "cache_ckv": nrm((L, DEC_BATCH, PAST_LEN, KV_LORA)),
        "cache_kpe": nrm((L, DEC_BATCH, PAST_LEN, MLA_ROPE)),
        "state_pool": nrm((L, DEC_BATCH, POOL_HIST, POOL_WIDTH)),
        "cache_sb_k": nrm((L, DEC_BATCH, PAST_LEN, SB_HEADS, SB_HEAD_DIM)),
        "cache_sb_v": nrm((L, DEC_BATCH, PAST_LEN, SB_HEADS, SB_HEAD_DIM)),
        "g_ffn1": gain((L, D_MODEL)),
        "w1_gate": nrm((L, D_MODEL, D_FF), D_MODEL ** -0.5),
        "w1_up": nrm((L, D_MODEL, D_FF), D_MODEL ** -0.5),
        "w1_down": nrm((L, D_FF, D_MODEL), D_FF ** -0.5),
        "g_mix": gain((L, D_MODEL)),
        "w_in": nrm((L, D_MODEL, IN_WIDTH), D_MODEL ** -0.5),
        "g_qnorm": gain((L, Q_LORA)),
        "w_uq": nrm((L, Q_LORA, MLA_HEADS * (MLA_NOPE + MLA_ROPE)), Q_LORA ** -0.5),
        "g_kvnorm": gain((L, KV_LORA)),
        "w_uk": nrm((L, KV_LORA, MLA_HEADS, MLA_NOPE), KV_LORA ** -0.5),
        "w_uv": nrm((L, KV_LORA, MLA_HEADS, MLA_V), KV_LORA ** -0.5),
        "w_pool": nrm((L, len(POOL_WINDOWS), POOL_GROUP, POOL_GROUP), POOL_GROUP ** -0.5),
        "pool_scale": gain((L, POOL_WIDTH)),
        "g_mla_out": gain((L, MLA_WIDTH)),
        "g_sb_out": gain((L, SB_WIDTH)),
        "w_out": nrm((L, MIX_WIDTH, D_MODEL), MIX_WIDTH ** -0.5),
        "g_ffn2": gain((L, D_MODEL)),
        "w2_gate": nrm((L, D_MODEL, D_FF), D_MODEL ** -0.5),
        "w2_up": nrm((L, D_MODEL, D_FF), D_MODEL ** -0.5),
        "w2_down": nrm((L, D_FF, D_MODEL), D_FF ** -0.5),
        "g_final": gain((D_MODEL,)),
    }


def reference(x_prompt, x_sample, cache_ckv, cache_kpe, state_pool, cache_sb_k, cache_sb_v,
              g_ffn1, w1_gate, w1_up, w1_down, g_mix, w_in, g_qnorm, w_uq, g_kvnorm,
              w_uk, w_uv, w_pool, pool_scale, g_mla_out, g_sb_out, w_out, g_ffn2,
              w2_gate, w2_up, w2_down, g_final):
    B = x_prompt.shape[0]
    dt = x_prompt.dtype
    e_ckv = jnp.zeros((B, 0, KV_LORA), dt)
    e_kpe = jnp.zeros((B, 0, MLA_ROPE), dt)
    e_pool = jnp.zeros((B, POOL_HIST, POOL_WIDTH), dt)
    e_sb = jnp.zeros((B, 0, SB_HEADS, SB_HEAD_DIM), dt)
    hp, hs = x_prompt, x_sample
    p_ckv, p_kpe, p_pool, p_sbk, p_sbv = [], [], [], [], []
    s_ckv, s_kpe, s_pool, s_sbk, s_sbv = [], [], [], [], []
    for l in range(DEPTH):
        lw = (g_ffn1[l], w1_gate[l], w1_up[l], w1_down[l], g_mix[l], w_in[l], g_qnorm[l],
              w_uq[l], g_kvnorm[l], w_uk[l], w_uv[l], w_pool[l], pool_scale[l],
              g_mla_out[l], g_sb_out[l], w_out[l], g_ffn2[l], w2_gate[l], w2_up[l], w2_down[l])
        hp, a, b, c, d, e = trunk_layer(hp, e_ckv, e_kpe, e_pool, e_sb, e_sb, *lw)
        p_ckv.append(a); p_kpe.append(b); p_pool.append(c); p_sbk.append(d); p_sbv.append(e)
        hs, a, b, c, d, e = trunk_layer(hs, cache_ckv[l], cache_kpe[l], state_pool[l],
                                        cache_sb_k[l], cache_sb_v[l], *lw)
        s_ckv.append(a); s_kpe.append(b); s_pool.append(c); s_sbk.append(d); s_sbv.append(e)
    y_prompt = rmsnorm(hp, g_final)
    y_sample = rmsnorm(hs, g_final)
    return (y_prompt, y_sample,
            jnp.stack(p_ckv), jnp.stack(p_kpe), jnp.stack(p_pool), jnp.stack(p_sbk), jnp.stack(p_sbv),
            jnp.stack(s_ckv), jnp.stack(s_kpe), jnp.stack(s_pool), jnp.stack(s_sbk), jnp.stack(s_sbv))
```

```python
import math
from contextlib import ExitStack
import numpy as np
import ml_dtypes
import concourse.bass as bass
import concourse.mybir as mybir
from concourse.bass_utils import run_bass_kernel_spmd

F32 = mybir.dt.float32
BF16 = mybir.dt.bfloat16
AF = mybir.ActivationFunctionType
ALU = mybir.AluOpType
EPS = 1e-6
ROPE_THETA = 10000.0


class Cfg:
    def __init__(s, D=4096, DFF=11008, SEQ=2048, TP=512, PAST=4096, QL=1024, KVL=512, HM=16, HS=8,
                 PG=256, L=2, NS=2, DS=16):
        s.D, s.DFF, s.SEQ, s.TP, s.PAST, s.QL, s.KVL, s.HM, s.HS, s.PG, s.L, s.NS, s.DS = \
            D, DFF, SEQ, TP, PAST, QL, KVL, HM, HS, PG, L, NS, DS
        s.KC = D // 128; s.FC = DFF // 128; s.NPASS = SEQ // TP
        s.QLC = QL // 128; s.KVC = KVL // 128
        s.PW = 4 * PG; s.PGC = PG // 128; s.PWC = s.PW // 128
        s.MLAW = HM * 128; s.SBW = HS * 128
        s.MIX = s.MLAW + s.PW + s.SBW; s.MIXC = s.MIX // 128
        s.oq = 0; s.okv = QL; s.ope = QL + KVL; s.ou = s.ope + 64
        s.oqs = s.ou + s.PW; s.oks = s.oqs + s.SBW; s.ovs = s.oks + s.SBW
        s.INW = s.ovs + s.SBW
        s.NST = NS * DS
        s.NW = TP + s.NST
        s.KGB = TP // 128
        s.NG = 2
        s.HMG = HM // 2; s.HSG = HS // 2
        s.mla_scale = 1.0 / math.sqrt(192.0); s.sb_scale = 1.0 / math.sqrt(128.0)


class Q:
    def __init__(s, nc, eng, name):
        s.eng = eng; s.sem = nc.alloc_semaphore(name); s.n = 0; s.waited = {}


class Buf:
    def __init__(s, name):
        s.name = name; s.w = {}; s.r = {}; s.ds = {}


class DSem:
    def __init__(s, sem):
        s.sem = sem; s.n = 0


class Tr:
    def __init__(s, nc):
        s.nc = nc
        s.pe = Q(nc, nc.tensor, "q_pe"); s.act = Q(nc, nc.scalar, "q_act")
        s.dve = Q(nc, nc.vector, "q_dve"); s.pool = Q(nc, nc.gpsimd, "q_pool")
        s.sp = Q(nc, nc.sync, "q_sp")
        s.queues = [s.pe, s.act, s.dve, s.pool, s.sp]
        s.dbufs = []
        s.dpool = {}
        s.dall = []
        s.ninst = 0

    def _wait(s, q, ev):
        if ev[0] == 'e':
            sem, val, key = ev[1].sem, ev[2], id(ev[1])
        else:
            sem, val, key = ev[1].sem, 16 * ev[1].n, id(ev[1])
        if q.waited.get(key, 0) >= val:
            return
        q.eng.wait_ge(sem, val)
        q.waited[key] = val

    def _deps(s, q, reads, writes):
        for b in reads:
            for ev in b.w.values():
                s._wait(q, ev)
        for b in writes:
            for ev in b.w.values():
                s._wait(q, ev)
            for ev in b.r.values():
                s._wait(q, ev)

    def _reg(s, ev, key, reads, writes):
        for b in reads:
            b.r[key] = ev
        for b in writes:
            b.w = {key: ev}; b.r = {}

    def op(s, q, reads, writes, fn):
        s._deps(q, reads, writes)
        ins = fn()
        q.n += 1
        ins.then_inc(q.sem, 1)
        s._reg(('e', q, q.n), id(q), reads, writes)
        s.ninst += 1
        return ins

    def dma(s, q, out, in_, reads, writes, sb):
        qk = id(q)
        if qk not in sb.ds:
            pl = s.dpool.setdefault(qk, [])
            if pl:
                sb.ds[qk] = pl.pop()
            else:
                sb.ds[qk] = DSem(s.nc.alloc_semaphore("dsem%d" % len(s.dall))); s.dall.append(sb.ds[qk])
            s.dbufs.append((sb, qk))
        d = sb.ds[qk]
        s._deps(q, reads, writes)
        ins = q.eng.dma_start(out=out, in_=in_)
        d.n += 1
        ins.then_inc(d.sem, 16)
        s._reg(('d', d), id(d), reads, writes)
        s.ninst += 1

    def barrier(s):
        for q in s.queues:
            for q2 in s.queues:
                if q2 is not q and q2.n > 0:
                    s._wait(q, ('e', q2, q2.n))
            for d in s.dall:
                if d.n > 0:
                    s._wait(q, ('d', d))
        for (b, qk) in s.dbufs:
            s.dpool[qk].append(b.ds.pop(qk))
        s.dbufs = []


def build_program(c: Cfg):
    nc = bass.Bass("TRN2", target_bir_lowering=False)
    T = Tr(nc)
    pe, act, dve, pool, sp = T.pe, T.act, T.dve, T.pool, T.sp
    L, KC, FC, TP, NW, NST, DS = c.L, c.KC, c.FC, c.TP, c.NW, c.NST, c.DS

    def din(name, shape):
        return nc.dram_tensor(name, list(shape), F32, kind="ExternalInput").ap()

    def dout(name, shape):
        return nc.dram_tensor(name, list(shape), F32, kind="ExternalOutput").ap()

    xp = din("xp", [c.SEQ, c.D]); xs = din("xs", [NST, c.D])
    c_ckv = din("c_ckv", [L, c.NS, c.PAST, c.KVL]); c_kpe = din("c_kpe", [L, c.NS, c.PAST, 64])
    c_pool = din("c_pool", [L, c.NS, 15, c.PW])
    c_sbk = din("c_sbk", [L, c.NS, c.PAST, c.SBW]); c_sbv = din("c_sbv", [L, c.NS, c.PAST, c.SBW])
    g_ffn1 = din("g_ffn1", [L, c.D]); w1g = din("w1_gate", [L, c.D, c.DFF]); w1u = din("w1_up", [L, c.D, c.DFF])
    w1d = din("w1_down", [L, c.DFF, c.D]); g_mix = din("g_mix", [L, c.D]); w_in = din("w_in", [L, c.D, c.INW])
    g_qn = din("g_qnorm", [L, c.QL]); w_uq = din("w_uq", [L, c.QL, c.HM * 192])
    g_kvn = din("g_kvnorm", [L, c.KVL]); w_uk = din("w_uk", [L, c.KVL, c.HM * 128])
    w_uv = din("w_uv", [L, c.KVL, c.HM * 128]); w_pool = din("w_pool", [L, 4, c.PG, c.PG])
    pscale = din("pool_scale", [L, c.PW]); g_mo = din("g_mla_out", [L, c.MLAW]); g_so = din("g_sb_out", [L, c.SBW])
    w_out = din("w_out", [L, c.MIX, c.D]); g_ffn2 = din("g_ffn2", [L, c.D])
    w2g = din("w2_gate", [L, c.D, c.DFF]); w2u = din("w2_up", [L, c.D, c.DFF]); w2d = din("w2_down", [L, c.DFF, c.D])
    g_fin = din("g_final", [c.D])
    k_ident = din("k_ident", [128, 128]); k_ut = din("k_ut", [128, 128]); k_r = din("k_r", [128, 128])
    k_mmla = din("k_mmla", [c.KGB, 128, TP]); k_msb = din("k_msb", [c.KGB, 128, TP])
    k_cosq = din("k_cosq", [128, c.SEQ + NST]); k_sinq = din("k_sinq", [128, c.SEQ + NST])
    k_cosk = din("k_cosk", [c.SEQ + NST, 32]); k_sink = din("k_sink", [c.SEQ + NST, 32])
    k_invc = din("k_invc", [128, 4, 16])

    y_p = dout("y_p", [c.SEQ, c.D]); y_s = dout("y_s", [NST, c.D])
    o_ckv_p = dout("o_ckv_p", [L, c.SEQ, c.KVL]); o_kpe_p = dout("o_kpe_p", [L, c.SEQ, 64])
    o_pool_p = dout("o_pool_p", [L, 15, c.PW]); o_sbk_p = dout("o_sbk_p", [L, c.SEQ, c.SBW])
    o_sbv_p = dout("o_sbv_p", [L, c.SEQ, c.SBW])
    o_ckv_s = dout("o_ckv_s", [L, NST, c.KVL]); o_kpe_s = dout("o_kpe_s", [L, NST, 64])
    o_pool_s = dout("o_pool_s", [L, c.NS * 15, c.PW]); o_sbk_s = dout("o_sbk_s", [L, NST, c.SBW])
    o_sbv_s = dout("o_sbv_s", [L, NST, c.SBW])
    xscr = nc.dram_tensor("xscr", [128, KC, NW], F32).ap()
    d_xscr = Buf("xscr")
    d_kout = [Buf("kout%d" % l) for l in range(L)]
    d_yout = Buf("yout")

    es = ExitStack()

    uid = [0]

    def sb(name, shape, dt, stack=None):
        uid[0] += 1
        nm = "%s_%d" % (name, uid[0])
        t = (stack or es).enter_context(nc.sbuf_tensor(nm, list(shape), dt))
        return t, Buf(nm)

    PS = []
    for i in range(8):
        t = es.enter_context(nc.psum_tensor("ps%d" % i, [128, 512], F32))
        PS.append((t, Buf("ps%d" % i)))

    ident, b_ident = sb("ident", [128, 128], F32)
    identb, b_identb = sb("identb", [128, 128], BF16)
    ut, b_ut = sb("ut", [128, 128], F32)
    rr, b_rr = sb("rr", [128, 128], F32)
    onesf, b_onesf = sb("onesf", [128, 128], F32)
    onesb, b_onesb = sb("onesb", [128, 128], BF16)
    mmla, b_mmla = sb("mmla", [128, c.KGB, TP], BF16)
    msb, b_msb = sb("msb", [128, c.KGB, TP], BF16)
    invc, b_invc = sb("invc", [128, 4, 16], F32)
    gains, b_gains = sb("gains", [128, L, 3 * KC + c.QLC + c.HM + c.HS + c.PWC], F32)
    gfin, b_gfin = sb("gfin", [128, KC], F32)
    gkv, b_gkv = sb("gkv", [128, L, c.KVL], F32)
    uh, b_uh = sb("uh", [128, L, c.PWC, 15], F32)
    hm, b_hm = sb("hm", [128, max(KC, c.MIXC), NW], BF16)

    GO_F1, GO_MIX, GO_F2 = 0, KC, 2 * KC
    GO_QN = 3 * KC; GO_MO = GO_QN + c.QLC; GO_SO = GO_MO + c.HM; GO_PS = GO_SO + c.HS

    def load_consts():
        T.dma(sp, ident[:], k_ident[:, :], [], [b_ident], b_ident)
        T.dma(sp, ut[:], k_ut[:, :], [], [b_ut], b_ut)
        T.dma(sp, rr[:], k_r[:, :], [], [b_rr], b_rr)
        T.op(dve, [b_ident], [b_identb], lambda: nc.vector.tensor_copy(out=identb[:], in_=ident[:]))
        T.op(dve, [], [b_onesf], lambda: nc.vector.memset(onesf[:], 1.0))
        T.op(dve, [], [b_onesb], lambda: nc.vector.memset(onesb[:], 1.0))
        T.op(dve, [], [b_uh], lambda: nc.vector.memset(uh[:], 0.0))
        for kb in range(c.KGB):
            T.dma(sp, cst[:, 0, :], k_mmla[kb], [], [b_cst], b_cst)
            T.op(dve, [b_cst], [b_mmla], lambda: nc.vector.tensor_copy(out=mmla[:, kb, :], in_=cst[:, 0, :]))
            T.dma(sp, cst[:, 1, :], k_msb[kb], [], [b_cst], b_cst)
            T.op(dve, [b_cst], [b_msb], lambda: nc.vector.tensor_copy(out=msb[:, kb, :], in_=cst[:, 1, :]))
        T.dma(sp, invc[:], k_invc[:, :, :], [], [b_invc], b_invc)
        T.dma(sp, gfin[:], g_fin.rearrange("(k p) -> p k", p=128), [], [b_gfin], b_gfin)
        for l in range(L):
            for (src, off, n) in ((g_ffn1, GO_F1, KC), (g_mix, GO_MIX, KC), (g_ffn2, GO_F2, KC),
                                  (g_qn, GO_QN, c.QLC), (g_mo, GO_MO, c.HM), (g_so, GO_SO, c.HS),
                                  (pscale, GO_PS, c.PWC)):
                T.dma(sp, gains[:, l, off:off + n], src[l].rearrange("(k p) -> p k", p=128), [], [b_gains], b_gains)
            T.dma(sp, gkv[:, l, :], g_kvn[l].partition_broadcast(128), [], [b_gkv], b_gkv)

    with ExitStack() as st0:
        cst, b_cst = sb("cst", [128, 2, TP], F32, st0)
        with nc.allow_non_contiguous_dma(reason="tiny gain-vector loads"):
            load_consts()
        T.barrier()

    def pass_tiles(p):
        tl = [(0, TP)]
        if p == c.NPASS - 1:
            tl.append((TP, NST))
        return tl

    class Ring:
        def __init__(s, name, n, shape, stack):
            s.slots = [sb("%s%d" % (name, i), shape, BF16, stack) for i in range(n)]
            s.i = 0

        def next(s):
            t = s.slots[s.i % len(s.slots)]; s.i += 1
            return t

    def load_slab(ring, W2d, kchunks, c0, m):
        t, b = ring.next()
        src = W2d.rearrange("(kc p) f -> p kc f", p=128)[:, :, c0:c0 + m]
        T.dma(pool, t[:, :kchunks, :m], src, [], [b], b)
        return t, b

    def load_rows(ring, W2d, r0, ncols):
        t, b = ring.next()
        T.dma(pool, t[:, :ncols], W2d[r0:r0 + 128, 0:ncols], [], [b], b)
        return t, b

    small_slot = [0]

    def ps_small(w):
        i = small_slot[0] % 8; small_slot[0] += 1
        return PS[6][0][:, i * 64:i * 64 + w], PS[6][1]

    def mm_group(out_ap, b_out, pairs, reads):
        def fn():
            n = len(pairs); ins = None
            for i, (l_, r_) in enumerate(pairs):
                ins = nc.tensor.matmul(out_ap, lhsT=l_, rhs=r_, start=(i == 0), stop=(i == n - 1))
            return ins
        return T.op(pe, reads, [b_out], fn)

    def rstd_from_ps(ps_ap, b_ps, out_ap, b_o, dim, eng_alt=0):
        T.op(act, [b_ps], [b_o], lambda: nc.scalar.activation(out=out_ap, in_=ps_ap, func=AF.Sqrt,
                                                               bias=EPS, scale=1.0 / dim))
        T.op(dve, [b_o], [b_o], lambda: nc.vector.reciprocal(out=out_ap, in_=out_ap))

    def rmsnorm_feat(src, b_src, nch, tiles, goff, l, dst, b_dst, rstd, b_rstd, gt=None, sq=None, b_sq=None):
        gt = gains if gt is None else gt
        sq = dst if sq is None else sq; b_sq = b_dst if b_sq is None else b_sq
        for ti, (c0, w) in enumerate(tiles):
            for k in range(nch):
                T.op(act, [b_src], [b_sq], lambda: nc.scalar.activation(out=sq[:, k, c0:c0 + w], in_=src[:, k, c0:c0 + w],
                                                                      func=AF.Square))
            pst, bps = PS[7]
            mm_group(pst[:, :w], bps, [(onesb[:, :], sq[:, k, c0:c0 + w]) for k in range(nch)], [b_sq, b_onesb])
            rstd_from_ps(pst[:, :w], bps, rstd[:, c0:c0 + w], b_rstd, nch * 128)
            for k in range(nch):
                g_ap = gt[:, k:k + 1] if gt is gfin else gt[:, l, goff + k:goff + k + 1]
                eng, q = (nc.vector, dve)
                T.op(q, [b_src, b_rstd, b_gains, b_gfin], [b_dst],
                     lambda: eng.scalar_tensor_tensor(out=dst[:, k, c0:c0 + w], in0=src[:, k, c0:c0 + w],
                                                      scalar=g_ap, in1=rstd[:, c0:c0 + w],
                                                      op0=ALU.mult, op1=ALU.mult))

    def ffn(l, Wg, Wu, Wd, x, b_x, tiles, ringA, ringD, actb, silb):
        G = 3
        groups = [list(range(g0, min(g0 + G, FC))) for g0 in range(0, FC, G)]
        gi = 0
        for grp in groups:
            acts = []
            dslabs = []
            for j in grp:
                sg, bsg = load_slab(ringA, Wg[l], KC, j * 128, 128)
                su, bsu = load_slab(ringA, Wu[l], KC, j * 128, 128)
                at, b_at = actb[gi % len(actb)]; gi += 1
                acts.append((at, b_at))
                for ti, (c0, w) in enumerate(tiles):
                    if ti == 0:
                        gps, bg = PS[j % 2]; ups, bu = PS[2 + j % 2]
                        gp, up = gps[:, :w], ups[:, :w]
                    else:
                        gp, bg = ps_small(w); up, bu = ps_small(w)
                    mm_group(gp, bg, [(sg[:, k, :], hm[:, k, c0:c0 + w]) for k in range(KC)], [bsg, b_hm])
                    mm_group(up, bu, [(su[:, k, :], hm[:, k, c0:c0 + w]) for k in range(KC)], [bsu, b_hm])
                    st, b_st = silb[(j + ti) % len(silb)]
                    T.op(act, [bg], [b_st], lambda: nc.scalar.activation(out=st[:, :w], in_=gp, func=AF.Silu))
                    T.op(dve, [b_st, bu], [b_at], lambda: nc.vector.tensor_tensor(out=at[:, c0:c0 + w], in0=st[:, :w],
                                                                                 in1=up, op=ALU.mult))
            for j in grp:
                dslabs.append(load_rows(ringD, Wd[l], j * 128, c.D))
            for m in range(KC):
                for ti, (c0, w) in enumerate(tiles):
                    if ti == 0:
                        dps, bd = PS[4 + m % 2]; dp = dps[:, :w]
                    else:
                        dp, bd = ps_small(w)
                    mm_group(dp, bd, [(dslabs[i][0][:, m * 128:(m + 1) * 128], acts[i][0][:, c0:c0 + w])
                                      for i in range(len(grp))],
                             [dslabs[i][1] for i in range(len(grp))] + [a[1] for a in acts])
                    T.op(dve, [bd, b_x], [b_x], lambda: nc.vector.scalar_tensor_tensor(
                        out=x[:, m, c0:c0 + w], in0=dp, scalar=0.5, in1=x[:, m, c0:c0 + w],
                        op0=ALU.mult, op1=ALU.add))

    def load_x(p, x, b_x, tiles, stack):
        xin, b_xin = sb("xin", [128, c.D], F32, stack)
        for (c0, w) in tiles:
            for tb in range((w + 127) // 128):
                n = min(128, w - tb * 128)
                src = xp[p * TP + tb * 128: p * TP + tb * 128 + n, :] if c0 == 0 else xs[0:n, :]
                T.dma(sp, xin[:n, :], src, [], [b_xin], b_xin)
                for k0 in range(0, KC, 4):
                    kk = min(4, KC - k0)
                    pst, bps = PS[7]
                    def fn():
                        ins = None
                        for j in range(kk):
                            ins = nc.tensor.transpose(out=pst[:, j * 128:j * 128 + n],
                                                      in_=xin[:n, (k0 + j) * 128:(k0 + j + 1) * 128],
                                                      identity=ident[:n, :n])
                        return ins
                    T.op(pe, [b_xin, b_ident], [bps], fn)
                    T.op(act, [bps], [b_x], lambda: nc.scalar.copy(
                        out=x[:, k0:k0 + kk, c0 + tb * 128:c0 + tb * 128 + n],
                        in_=pst[:, :kk * 128].rearrange("p (j t) -> p j t", t=128)[:, :, :n]))

    def store_y(p, x, b_x, tiles, stack):
        yo, b_yo = sb("yo", [128, 2, 512], F32, stack)
        cnt = 0
        for (c0, w) in tiles:
            for tb in range((w + 127) // 128):
                n = min(128, w - tb * 128)
                for k0 in range(0, KC, 4):
                    kk = min(4, KC - k0)
                    pst, bps = PS[7]
                    def fn():
                        ins = None
                        for j in range(kk):
                            ins = nc.tensor.transpose(out=pst[:n, j * 128:(j + 1) * 128],
                                                      in_=x[:, k0 + j, c0 + tb * 128:c0 + tb * 128 + n],
                                                      identity=ident[:, :])
                        return ins
                    T.op(pe, [b_x, b_ident], [bps], fn)
                    s_ = cnt % 2; cnt += 1
                    T.op(act, [bps], [b_yo], lambda: nc.scalar.copy(out=yo[:n, s_, :kk * 128], in_=pst[:n, :kk * 128]))
                    dst = (y_p[p * TP + tb * 128: p * TP + tb * 128 + n, k0 * 128:(k0 + kk) * 128] if c0 == 0
                           else y_s[0:n, k0 * 128:(k0 + kk) * 128])
                    T.dma(sp, dst, yo[:n, s_, :kk * 128], [b_yo], [d_yout], b_yo)

    def proj_feat(ring, W2d, kch, c0, m, src, b_src, tiles, big_bank, evac):
        s_, bs = load_slab(ring, W2d, kch, c0, m)
        for ti, (t0, w) in enumerate(tiles):
            if ti == 0:
                pst, bps = PS[big_bank]; pa = pst[:m, :w]
            else:
                pa, bps = ps_small(w); pa = pa[:m, :]
            mm_group(pa, bps, [(s_[:, k, :m], src[:, k, t0:t0 + w]) for k in range(kch)], [bs, b_src])
            evac(pa, bps, t0, w)

    def copy_evac(dst_fn, b_dst, alt=[0]):
        def ev(pa, bps, t0, w):
            alt[0] += 1
            if alt[0] % 2:
                T.op(act, [bps], [b_dst], lambda: nc.scalar.copy(out=dst_fn(t0, w), in_=pa))
            else:
                T.op(dve, [bps], [b_dst], lambda: nc.vector.tensor_copy(out=dst_fn(t0, w), in_=pa))
        return ev

    def mix_q(l, p, tiles, ring, qn, b_qn, qp, b_qp, qs, b_qs, stack):
        cq, b_cq = sb("cq", [128, c.QLC, NW], F32, stack)
        cqn, b_cqn = sb("cqn", [128, c.QLC, NW], BF16, stack)
        rs, b_rs = sb("rs_q", [128, NW], F32, stack)
        tq, b_tq = sb("tq", [128, 2, TP], F32, stack)
        cs, b_cs = sb("cs_q", [128, 2, NW], F32, stack)
        W = w_in[l]
        for j in range(c.QLC):
            proj_feat(ring, W, KC, c.oq + j * 128, 128, hm, b_hm, tiles, j % 2,
                      copy_evac(lambda t0, w, j=j: cq[:, j, t0:t0 + w], b_cq))
        rmsnorm_feat(cq, b_cq, c.QLC, tiles, GO_QN, l, cqn, b_cqn, rs, b_rs)
        for h in range(c.HM):
            proj_feat(ring, w_uq[l], c.QLC, h * 192, 128, cqn, b_cqn, tiles, h % 2,
                      copy_evac(lambda t0, w, h=h: qn[:, h, t0:t0 + w], b_qn))
        T.dma(sp, cs[:, 0, :TP], k_cosq[:, p * TP:(p + 1) * TP], [], [b_cs], b_cs)
        T.dma(sp, cs[:, 1, :TP], k_sinq[:, p * TP:(p + 1) * TP], [], [b_cs], b_cs)
        if len(tiles) > 1:
            T.dma(sp, cs[:, 0, TP:NW], k_cosq[:, c.SEQ:c.SEQ + NST], [], [b_cs], b_cs)
            T.dma(sp, cs[:, 1, TP:NW], k_sinq[:, c.SEQ:c.SEQ + NST], [], [b_cs], b_cs)
        for hp in range(c.HM // 2):
            t_, bt = ring.next()
            for e in range(2):
                src = w_uq[l].rearrange("(kc p) f -> p kc f", p=128)[:, :, (2 * hp + e) * 192 + 128:(2 * hp + e) * 192 + 192]
                T.dma(pool, t_[:, :c.QLC, e * 64:(e + 1) * 64], src, [], [bt], bt)
            for ti, (t0, w) in enumerate(tiles):
                if ti == 0:
                    pst, bps = PS[hp % 2]; pa = pst[:, :w]
                else:
                    pa, bps = ps_small(w)
                mm_group(pa, bps, [(t_[:, k, :], cqn[:, k, t0:t0 + w]) for k in range(c.QLC)], [bt, b_cqn])
                T.op(act, [bps], [b_tq], lambda: nc.scalar.copy(out=tq[:, 0, :w], in_=pa))
                if ti == 0:
                    ps2, bp2 = PS[2 + hp % 2]; pb = ps2[:, :w]
                else:
                    pb, bp2 = ps_small(w)
                mm_group(pb, bp2, [(rr[:, :], tq[:, 0, :w])], [b_rr, b_tq])
                T.op(dve, [b_tq, b_cs], [b_tq], lambda: nc.vector.tensor_tensor(out=tq[:, 0, :w], in0=tq[:, 0, :w],
                                                                              in1=cs[:, 0, t0:t0 + w], op=ALU.mult))
                T.op(dve, [bp2, b_cs], [b_tq], lambda: nc.vector.tensor_tensor(out=tq[:, 1, :w], in0=pb,
                                                                              in1=cs[:, 1, t0:t0 + w], op=ALU.mult))
                T.op(dve, [b_tq], [b_qp], lambda: nc.vector.tensor_tensor(out=qp[:, hp, t0:t0 + w], in0=tq[:, 0, :w],
                                                                          in1=tq[:, 1, :w], op=ALU.add))
        for h in range(c.HS):
            proj_feat(ring, W, KC, c.oqs + h * 128, 128, hm, b_hm, tiles, h % 2,
                      copy_evac(lambda t0, w, h=h: qs[:, h, t0:t0 + w], b_qs))

    def tok_blocks(tiles):
        bl = []
        for (t0, w) in tiles:
            for tb in range((w + 127) // 128):
                bl.append((t0 + tb * 128, min(128, w - tb * 128), t0 != 0))
        return bl

    def mix_k(l, p, tiles, ring, stack):
        NB = c.KGB + 1
        stg, b_stg = sb("stg", [128, NB, 512], F32, stack)
        sm, b_sm = sb("sm_k", [128, NB, 8], F32, stack)
        ck, b_ck = sb("ck", [128, NB, 2, 32], F32, stack)
        junk, b_junk = sb("junk", [128, 512], F32, stack)
        tr_, b_tr = sb("tr_k", [128, 4, 32], F32, stack)
        W = w_in[l]
        blocks = tok_blocks(tiles)
        for bi, (t0, n, is_s) in enumerate(blocks):
            r0 = (c.SEQ + t0 - TP) if is_s else (p * TP + t0)
            T.dma(sp, ck[:n, bi, 0, :], k_cosk[r0:r0 + n, :], [], [b_ck], b_ck)
            T.dma(sp, ck[:n, bi, 1, :], k_sink[r0:r0 + n, :], [], [b_ck], b_ck)

        def tok_proj(c0, width):
            for j0 in range(0, width, 128):
                m = min(128, width - j0)
                s_, bs = load_slab(ring, W, KC, c0 + j0, m)
                for bi, (t0, n, is_s) in enumerate(blocks):
                    pa, bps = ps_small(64) if False else (PS[bi % 2][0][:n, :m], PS[bi % 2][1])
                    mm_group(pa, bps, [(hm[:, k, t0:t0 + n], s_[:, k, :m]) for k in range(KC)], [bs, b_hm])
                    if (bi + j0 // 128) % 2:
                        T.op(act, [bps], [b_stg], lambda: nc.scalar.copy(out=stg[:n, bi, j0:j0 + m], in_=pa))
                    else:
                        T.op(dve, [bps], [b_stg], lambda: nc.vector.tensor_copy(out=stg[:n, bi, j0:j0 + m], in_=pa))

        def out_rows(dst_p, dst_s, col0, width):
            for bi, (t0, n, is_s) in enumerate(blocks):
                if is_s:
                    dst = dst_s[l, t0 - TP:t0 - TP + n, col0:col0 + width]
                else:
                    dst = dst_p[l, p * TP + t0:p * TP + t0 + n, col0:col0 + width]
                T.dma(sp, dst, stg[:n, bi, :width], [b_stg], [d_kout[l]], b_stg)

        tok_proj(c.okv, c.KVL)
        for bi, (t0, n, is_s) in enumerate(blocks):
            T.op(act, [b_stg], [b_junk, b_sm], lambda: nc.scalar.activation(
                out=junk[:n, :c.KVL], in_=stg[:n, bi, :c.KVL], func=AF.Square, accum_out=sm[:n, bi, 0:1]))
            T.op(act, [b_sm], [b_sm], lambda: nc.scalar.activation(out=sm[:n, bi, 1:2], in_=sm[:n, bi, 0:1],
                                                                  func=AF.Sqrt, bias=EPS, scale=1.0 / c.KVL))
            T.op(dve, [b_sm], [b_sm], lambda: nc.vector.reciprocal(out=sm[:n, bi, 2:3], in_=sm[:n, bi, 1:2]))
            T.op(dve, [b_stg, b_sm, b_gkv], [b_stg], lambda: nc.vector.scalar_tensor_tensor(
                out=stg[:n, bi, :c.KVL], in0=stg[:n, bi, :c.KVL], scalar=sm[:n, bi, 2:3], in1=gkv[:n, l, :],
                op0=ALU.mult, op1=ALU.mult))
        out_rows(o_ckv_p, o_ckv_s, 0, c.KVL)
        tok_proj(c.ope, 64)
        for bi, (t0, n, is_s) in enumerate(blocks):
            x1 = stg[:n, bi, 0:32]; x2 = stg[:n, bi, 32:64]
            cs_, sn_ = ck[:n, bi, 0, :], ck[:n, bi, 1, :]
            for (o_, a_, b2_) in ((0, x1, cs_), (1, x2, sn_), (2, x2, cs_), (3, x1, sn_)):
                T.op(dve, [b_stg, b_ck], [b_tr], lambda: nc.vector.tensor_tensor(out=tr_[:n, o_, :], in0=a_, in1=b2_,
                                                                              op=ALU.mult))
            T.op(dve, [b_tr], [b_stg], lambda: nc.vector.tensor_tensor(out=x1, in0=tr_[:n, 0, :], in1=tr_[:n, 1, :],
                                                                      op=ALU.subtract))
            T.op(dve, [b_tr], [b_stg], lambda: nc.vector.tensor_tensor(out=x2, in0=tr_[:n, 2, :], in1=tr_[:n, 3, :],
                                                                      op=ALU.add))
        out_rows(o_kpe_p, o_kpe_s, 0, 64)
        for (off, dp_, ds_) in ((c.oks, o_sbk_p, o_sbk_s), (c.ovs, o_sbv_p, o_sbv_s)):
            for c0 in range(0, c.SBW, 512):
                wd = min(512, c.SBW - c0)
                tok_proj(off + c0, wd)
                out_rows(dp_, ds_, c0, wd)

    def mix_pool(l, p, tiles, ring, stack):
        PWC, PGC = c.PWC, c.PGC
        EW = 15 + TP
        u, b_u = sb("u", [128, PWC, EW], F32, stack)
        us, b_us = sb("u_s", [128, c.NS, PWC, 15 + DS], F32, stack)
        ta, b_ta = sb("pl_a", [128, PGC, EW], F32, stack)
        tb_, b_tb = sb("pl_b", [128, PGC, EW], F32, stack)
        pld, b_pld = sb("pld", [128, PWC, NW], BF16, stack)
        wp, b_wp = sb("wp", [128, 4, PGC, c.PG], BF16, stack)
        hs, b_hs = sb("hs", [16, c.PW], F32, stack)
        ho, b_ho = sb("ho", [16, c.PW], F32, stack)
        W = w_in[l]
        for g in range(4):
            T.dma(pool, wp[:, g, :, :], w_pool[l, g].rearrange("(cc p) d -> p cc d", p=128), [], [b_wp], b_wp)
        T.op(dve, [b_uh], [b_u], lambda: nc.vector.tensor_copy(out=u[:, :, 0:15], in_=uh[:, l, :, :]))
        for j in range(PWC):
            def ev(pa, bps, t0, w, j=j):
                if t0 == 0:
                    T.op(act, [bps], [b_u], lambda: nc.scalar.copy(out=u[:, j, 15:15 + w], in_=pa))
                else:
                    for s_ in range(c.NS):
                        T.op(act, [bps], [b_us], lambda: nc.scalar.copy(out=us[:, s_, j, 15:15 + DS],
                                                                       in_=pa[:, s_ * DS:(s_ + 1) * DS]))
            proj_feat(ring, W, KC, c.ou + j * 128, 128, hm, b_hm, tiles, j % 2, ev)
        has_s = len(tiles) > 1
        if has_s:
            for s_ in range(c.NS):
                T.dma(sp, hs[:15, :], c_pool[l, s_], [], [b_hs], b_hs)
                for j in range(PWC):
                    pa, bps = ps_small(16)
                    T.op(pe, [b_hs, b_ident], [bps], lambda: nc.tensor.transpose(
                        out=pa[:, :15], in_=hs[:15, j * 128:(j + 1) * 128], identity=ident[:15, :15]))
                    T.op(act, [bps], [b_us], lambda: nc.scalar.copy(out=us[:, s_, j, 0:15], in_=pa[:, :15]))
        T.op(dve, [b_u], [b_uh], lambda: nc.vector.tensor_copy(out=uh[:, l, :, :], in_=u[:, :, TP:TP + 15]))

        def pool_one(ext_fn, b_ext, ew, out_fn, fix):
            for g, wdw in enumerate((2, 4, 8, 16)):
                ch0 = g * PGC
                e = ext_fn(ch0, PGC)
                cur, b_cur = e, b_ext
                sh = 1; tgl = 0; lo = 0
                while sh < wdw:
                    dst, b_dst = (ta, b_ta) if tgl == 0 else (tb_, b_tb)
                    d_ap = dst[:, :, :ew]
                    T.op(dve, [b_cur], [b_dst], lambda: nc.vector.tensor_tensor(
                        out=d_ap[:, :, lo + sh:ew], in0=cur[:, :, lo + sh:ew], in1=cur[:, :, lo:ew - sh], op=ALU.add))
                    cur, b_cur = d_ap, b_dst
                    lo += sh; sh *= 2; tgl ^= 1
                o = out_fn(ch0, PGC)
                T.op(dve, [b_cur, b_ext], [b_pld], lambda: nc.vector.scalar_tensor_tensor(
                    out=o, in0=cur[:, :, 15:ew], scalar=1.0 / wdw, in1=e[:, :, 15:ew], op0=ALU.mult, op1=ALU.subtract))
                if fix:
                    for cc in range(PGC):
                        T.op(dve, [b_cur, b_invc], [b_tr2], lambda: nc.vector.tensor_tensor(
                            out=tr2[:, 0:16], in0=cur[:, cc, 15:31], in1=invc[:, g, :], op=ALU.mult))
                        T.op(dve, [b_tr2, b_ext], [b_pld], lambda: nc.vector.tensor_tensor(
                            out=o[:, cc, 0:16], in0=tr2[:, 0:16], in1=e[:, cc, 15:31], op=ALU.subtract))

        tr2, b_tr2 = sb("tr2", [128, 16], F32, stack)
        pool_one(lambda ch0, n: u[:, ch0:ch0 + n, :], b_u, EW, lambda ch0, n: pld[:, ch0:ch0 + n, 0:TP], p == 0)
        if has_s:
            for s_ in range(c.NS):
                pool_one(lambda ch0, n: us[:, s_, ch0:ch0 + n, :], b_us, 15 + DS,
                         lambda ch0, n: pld[:, ch0:ch0 + n, TP + s_ * DS:TP + (s_ + 1) * DS], False)
        for g in range(4):
            for dc in range(PGC):
                for ti, (t0, w) in enumerate(tiles):
                    if ti == 0:
                        pst, bps = PS[(g + dc) % 2]; pa = pst[:, :w]
                    else:
                        pa, bps = ps_small(w)
                    mm_group(pa, bps, [(wp[:, g, cc, dc * 128:(dc + 1) * 128], pld[:, g * PGC + cc, t0:t0 + w])
                                       for cc in range(PGC)], [b_wp, b_pld])
                    ch = g * PGC + dc
                    T.op(act, [bps, b_gains], [b_hm], lambda: nc.scalar.activation(
                        out=hm[:, c.HM + ch, t0:t0 + w], in_=pa, func=AF.Copy,
                        scale=gains[:, l, GO_PS + ch:GO_PS + ch + 1]))
        outs = []
        if p == c.NPASS - 1:
            outs.append((lambda j: u[:, j, TP:TP + 15], b_u, o_pool_p[l]))
            for s_ in range(c.NS):
                outs.append((lambda j, s_=s_: us[:, s_, j, DS:DS + 15], b_us, o_pool_s[l, s_ * 15:(s_ + 1) * 15, :]))
        for (srcf, b_s, dst) in outs:
            for j in range(PWC):
                pa, bps = PS[7][0][:, :128], PS[7][1]
                T.op(pe, [b_s, b_ident], [bps], lambda: nc.tensor.transpose(out=pa[:15, :128], in_=srcf(j),
                                                                           identity=ident[:, :]))
                T.op(act, [bps], [b_ho], lambda: nc.scalar.copy(out=ho[:15, j * 128:(j + 1) * 128], in_=pa[:15, :128]))
            T.dma(sp, dst, ho[:15, :], [b_ho], [d_yout], b_ho)

    def attention(l, qn, b_qn, qp, b_qp, qs, b_qs, qc0, N, segs, diag_last, stack_bufs, ssq):
        B = stack_bufs
        (ssq_m, ssq_s) = ssq
        for g in range(c.NG):
            hm0 = g * c.HMG; hs0 = g * c.HSG
            T.dma(pool, B.wuk[:, :, :], w_uk[l].rearrange("(kc p) f -> p kc f", p=128)[:, :, hm0 * 128:(hm0 + c.HMG) * 128],
                  [], [B.b_wuk], B.b_wuk)
            T.dma(pool, B.wuv[:, :, :], w_uv[l].rearrange("(kc p) f -> p kc f", p=128)[:, :, hm0 * 128:(hm0 + c.HMG) * 128],
                  [], [B.b_wuv], B.b_wuv)
            T.op(dve, [], [B.b_accm], lambda: nc.vector.memset(B.accm[:, :, :], 0.0))
            T.op(dve, [], [B.b_accd], lambda: nc.vector.memset(B.accd[:, :, :], 0.0))
            T.op(pool, [], [B.b_accs], lambda: nc.gpsimd.memset(B.accs[:, :, :], 0.0))
            T.op(pool, [], [B.b_spsum], lambda: nc.gpsimd.memset(B.spsum[:, :, :], 0.0))
            groups = []
            for (srcs, ntok, deps) in segs:
                for k0 in range(0, ntok, TP):
                    groups.append((srcs, k0, min(TP, ntok - k0), deps))
            first = True
            for gi_ in range(len(groups) - 1, -1, -1):
                srcs, k0, nk, deps = groups[gi_]
                diag = diag_last and gi_ == len(groups) - 1
                nkb = (nk + 127) // 128
                blk = [(kb, min(128, nk - kb * 128)) for kb in range(nkb)]
                for kb, n in blk:
                    r0 = k0 + kb * 128
                    T.dma(pool, B.rckv[:n, kb, :], srcs["ckv"][r0:r0 + n, :], deps, [B.b_rckv], B.b_rckv)
                    for e in range(2):
                        T.dma(pool, B.rkpe[:n, kb, e * 64:(e + 1) * 64], srcs["kpe"][r0:r0 + n, :], deps, [B.b_rkpe], B.b_rkpe)
                    T.dma(pool, B.rsbk[:n, kb, :], srcs["sbk"][r0:r0 + n, hs0 * 128:(hs0 + c.HSG) * 128], deps,
                          [B.b_rsbk], B.b_rsbk)
                    T.dma(pool, B.vsb[:n, kb, :], srcs["sbv"][r0:r0 + n, hs0 * 128:(hs0 + c.HSG) * 128], deps,
                          [B.b_vsb], B.b_vsb)
                tpb = PS[4][0][:].bitcast(BF16)
                bt4 = PS[4][1]
                def transp(src_fn, b_src, dst_ap_fn, b_dst, alt):
                    def fn():
                        ins = None
                        for kb, n in blk:
                            ins = nc.tensor.transpose(out=tpb[:, kb * 128:kb * 128 + n], in_=src_fn(kb, n),
                                                      identity=identb[:n, :n])
                        return ins
                    T.op(pe, [b_src, b_identb], [bt4], fn)
                    if alt % 2:
                        T.op(act, [bt4], [b_dst], lambda: nc.scalar.copy(out=dst_ap_fn(), in_=tpb[:, :nk]))
                    else:
                        T.op(dve, [bt4], [b_dst], lambda: nc.vector.tensor_copy(out=dst_ap_fn(), in_=tpb[:, :nk]))
                for cc in range(c.KVC):
                    transp(lambda kb, n, cc=cc: B.rckv[:n, kb, cc * 128:(cc + 1) * 128], B.b_rckv,
                           lambda cc=cc: B.ckvT[:, cc, :nk], B.b_ckvT, cc)
                transp(lambda kb, n: B.rkpe[:n, kb, :], B.b_rkpe, lambda: B.kpeT[:, :nk], B.b_kpeT, 1)
                for hh in range(c.HSG):
                    transp(lambda kb, n, hh=hh: B.rsbk[:n, kb, hh * 128:(hh + 1) * 128], B.b_rsbk,
                           lambda hh=hh: B.ksbT[:, hh, :nk], B.b_ksbT, hh)
                for hh in range(c.HMG):
                    pst, bps = PS[5]
                    mm_group(pst[:, :nk], bps, [(B.wuk[:, cc, hh * 128:(hh + 1) * 128], B.ckvT[:, cc, :nk])
                                                for cc in range(c.KVC)], [B.b_wuk, B.b_ckvT])
                    if hh % 2:
                        T.op(act, [bps], [B.b_kn], lambda: nc.scalar.copy(out=B.kn[:, hh, :nk], in_=pst[:, :nk]))
                    else:
                        T.op(dve, [bps], [B.b_kn], lambda: nc.vector.tensor_copy(out=B.kn[:, hh, :nk], in_=pst[:, :nk]))
                VW = c.HMG * 128
                for kb, n in blk:
                    for v0 in range(0, VW, 512):
                        vw = min(512, VW - v0)
                        pst, bps = PS[5]
                        mm_group(pst[:n, :vw], bps, [(B.ckvT[:, cc, kb * 128:kb * 128 + n], B.wuv[:, cc, v0:v0 + vw])
                                                     for cc in range(c.KVC)], [B.b_wuv, B.b_ckvT])
                        if kb % 2:
                            T.op(act, [bps], [B.b_vm], lambda: nc.scalar.copy(out=B.vm[:n, kb, v0:v0 + vw], in_=pst[:n, :vw]))
                        else:
                            T.op(dve, [bps], [B.b_vm], lambda: nc.vector.tensor_copy(out=B.vm[:n, kb, v0:v0 + vw],
                                                                                     in_=pst[:n, :vw]))
                for hh in range(c.HMG):
                    h = hm0 + hh
                    po = 64 * (h % 2)
                    ops_, bo = PS[2]; dps, bdn = PS[3]
                    pts = []
                    for kb, n in blk:
                        sps, bs_ = PS[kb % 2]
                        mm_group(sps[:n, :N], bs_,
                                 [(B.kn[:, hh, kb * 128:kb * 128 + n], qn[:, h, qc0:qc0 + N]),
                                  (B.kpeT[po:po + 64, kb * 128:kb * 128 + n], qp[po:po + 64, h // 2, qc0:qc0 + N])],
                                 [B.b_kn, B.b_kpeT, b_qn, b_qp])
                        pt, b_pt = B.pt[kb % 2]
                        T.op(act, [bs_], [b_pt], lambda: nc.scalar.activation(out=pt[:n, :N], in_=sps[:n, :N], func=AF.Exp,
                                                                             scale=c.mla_scale))
                        if diag:
                            T.op(dve, [b_pt, b_mmla], [b_pt], lambda: nc.vector.tensor_tensor(
                                out=pt[:n, :N], in0=pt[:n, :N], in1=mmla[:n, kb, :N], op=ALU.mult))
                        pts.append((pt, b_pt, kb, n))
                        mm_group(ops_[:, :N], bo, [(B.vm[:n, kb, hh * 128:(hh + 1) * 128], pt[:n, :N])], [B.b_vm, b_pt])
                        T.op(dve, [bo, B.b_accm], [B.b_accm], lambda: nc.vector.tensor_tensor(
                            out=B.accm[:, hh, :N], in0=ops_[:, :N], in1=B.accm[:, hh, :N], op=ALU.add))
                        mm_group(dps[:, :N], bdn, [(onesb[:n, :], pt[:n, :N])], [b_onesb, b_pt])
                        T.op(pool if False else dve, [bdn, B.b_accd], [B.b_accd], lambda: nc.vector.tensor_tensor(
                            out=B.accd[:, hh, :N], in0=dps[:, :N], in1=B.accd[:, hh, :N], op=ALU.add))
                for hh in range(c.HSG):
                    h = hs0 + hh
                    for kb, n in reversed(blk):
                        zps, bz = PS[kb % 2]
                        mm_group(zps[:n, :N], bz, [(B.ksbT[:, hh, kb * 128:kb * 128 + n], qs[:, h, qc0:qc0 + N])],
                                 [B.b_ksbT, b_qs])
                        T.op(act, [bz], [B.b_e], lambda: nc.scalar.activation(out=B.e[:n, :N], in_=zps[:n, :N], func=AF.Exp,
                                                                             scale=c.sb_scale))
                        T.op(act, [B.b_e], [B.b_sp], lambda: nc.scalar.activation(out=B.sp[:n, :N], in_=B.e[:n, :N],
                                                                                  func=AF.Ln, bias=1.0, scale=1.0))
                        if diag:
                            T.op(dve, [B.b_sp, b_msb], [B.b_sp], lambda: nc.vector.tensor_tensor(
                                out=B.sp[:n, :N], in0=B.sp[:n, :N], in1=msb[:n, kb, :N], op=ALU.mult))
                        ips, bi_ = PS[3]
                        pairs = [(ut[:n, :n], B.sp[:n, :N])]
                        rd = [b_ut, B.b_sp]
                        if not (first and kb == blk[-1][0]):
                            pairs.append((onesf[:, :n], B.spsum[:, hh, :N])); rd += [b_onesf, B.b_spsum]
                        mm_group(ips[:n, :N], bi_, pairs, rd)
                        T.op(act, [bi_], [B.b_ex], lambda: nc.scalar.activation(out=B.ex[:n, :N], in_=ips[:n, :N],
                                                                               func=AF.Exp, scale=-1.0))
                        at, b_at = B.pt[kb % 2]
                        if diag:
                            T.op(dve, [B.b_e, B.b_ex], [B.b_e], lambda: nc.vector.tensor_tensor(
                                out=B.e[:n, :N], in0=B.e[:n, :N], in1=B.ex[:n, :N], op=ALU.mult))
                            T.op(dve, [B.b_e, b_msb], [b_at], lambda: nc.vector.tensor_tensor(
                                out=at[:n, :N], in0=B.e[:n, :N], in1=msb[:n, kb, :N], op=ALU.mult))
                        else:
                            T.op(dve, [B.b_e, B.b_ex], [b_at], lambda: nc.vector.tensor_tensor(
                                out=at[:n, :N], in0=B.e[:n, :N], in1=B.ex[:n, :N], op=ALU.mult))
                        T.op(pool, [B.b_sp, B.b_spsum], [B.b_spsum], lambda: nc.gpsimd.tensor_tensor(
                            out=B.spsum[:n, hh, :N], in0=B.spsum[:n, hh, :N], in1=B.sp[:n, :N], op=ALU.add))
                        ops_, bo = PS[2]
                        mm_group(ops_[:, :N], bo, [(B.vsb[:n, kb, hh * 128:(hh + 1) * 128], at[:n, :N])], [B.b_vsb, b_at])
                        T.op(dve, [bo, B.b_accs], [B.b_accs], lambda: nc.vector.tensor_tensor(
                            out=B.accs[:, hh, :N], in0=ops_[:, :N], in1=B.accs[:, hh, :N], op=ALU.add))
                first = False
            for hh in range(c.HMG):
                T.op(dve, [B.b_accd], [B.b_accd], lambda: nc.vector.reciprocal(out=B.accd[:, hh, :N], in_=B.accd[:, hh, :N]))
                T.op(dve, [B.b_accd, B.b_accm], [B.b_accm], lambda: nc.vector.tensor_tensor(
                    out=B.accm[:, hh, :N], in0=B.accm[:, hh, :N], in1=B.accd[:, hh, :N], op=ALU.mult))
                T.op(act, [B.b_accm], [B.b_sqb], lambda: nc.scalar.activation(out=B.sqb[:, hh, :N], in_=B.accm[:, hh, :N],
                                                                              func=AF.Square))
            pst, bps = PS[7]
            mm_group(pst[:, :N], bps, [(onesb[:, :], B.sqb[:, hh, :N]) for hh in range(c.HMG)], [b_onesb, B.b_sqb])
            T.op(dve, [bps, B.b_ssq], [B.b_ssq], lambda: nc.vector.tensor_tensor(
                out=ssq_m[:, qc0:qc0 + N], in0=pst[:, :N], in1=ssq_m[:, qc0:qc0 + N], op=ALU.add))
            for hh in range(c.HSG):
                T.op(act, [B.b_accs], [B.b_sqb], lambda: nc.scalar.activation(out=B.sqb[:, hh, :N], in_=B.accs[:, hh, :N],
                                                                              func=AF.Square))
            mm_group(pst[:, :N], bps, [(onesb[:, :], B.sqb[:, hh, :N]) for hh in range(c.HSG)], [b_onesb, B.b_sqb])
            T.op(dve, [bps, B.b_ssq], [B.b_ssq], lambda: nc.vector.tensor_tensor(
                out=ssq_s[:, qc0:qc0 + N], in0=pst[:, :N], in1=ssq_s[:, qc0:qc0 + N], op=ALU.add))
            for hh in range(c.HMG):
                T.op(act, [B.b_accm], [b_hm], lambda: nc.scalar.copy(out=hm[:, hm0 + hh, qc0:qc0 + N], in_=B.accm[:, hh, :N]))
            for hh in range(c.HSG):
                T.op(act, [B.b_accs], [b_hm], lambda: nc.scalar.copy(out=hm[:, c.HM + c.PWC + hs0 + hh, qc0:qc0 + N],
                                                                    in_=B.accs[:, hh, :N]))

    class NS_:
        pass

    def attn_stage(l, p, tiles, qn, b_qn, qp, b_qp, qs, b_qs, stack):
        B = NS_()
        KGB = c.KGB
        def mk(name, shape, dt):
            t, b = sb(name, shape, dt, stack); setattr(B, name, t); setattr(B, "b_" + name, b)
        mk("wuk", [128, c.KVC, c.HMG * 128], BF16); mk("wuv", [128, c.KVC, c.HMG * 128], BF16)
        mk("rckv", [128, KGB, c.KVL], BF16); mk("rkpe", [128, KGB, 128], BF16)
        mk("rsbk", [128, KGB, c.HSG * 128], BF16); mk("vsb", [128, KGB, c.HSG * 128], BF16)
        mk("ckvT", [128, c.KVC, TP], BF16); mk("kpeT", [128, TP], BF16); mk("ksbT", [128, c.HSG, TP], BF16)
        mk("kn", [128, c.HMG, TP], BF16); mk("vm", [128, KGB, c.HMG * 128], BF16)
        mk("accm", [128, c.HMG, TP], F32); mk("accd", [128, c.HMG, TP], F32)
        mk("accs", [128, c.HSG, TP], F32); mk("spsum", [128, c.HSG, TP], F32)
        mk("e", [128, TP], F32); mk("sp", [128, TP], F32); mk("ex", [128, TP], F32)
        B.sqb, B.b_sqb = B.kn, B.b_kn
        mk("ssq", [128, 2, NW], F32)
        B.pt = [sb("pt%d" % i, [128, TP], BF16, stack) for i in range(2)]
        T.op(dve, [], [B.b_ssq], lambda: nc.vector.memset(B.ssq[:, :, :], 0.0))
        ssq = (B.ssq[:, 0, :], B.ssq[:, 1, :])
        srcs = {"ckv": o_ckv_p[l], "kpe": o_kpe_p[l], "sbk": o_sbk_p[l], "sbv": o_sbv_p[l]}
        attention(l, qn, b_qn, qp, b_qp, qs, b_qs, 0, TP, [(srcs, (p + 1) * TP, [d_kout[l]])], True, B, ssq)
        if len(tiles) > 1:
            for s_ in range(c.NS):
                past = {"ckv": c_ckv[l, s_], "kpe": c_kpe[l, s_], "sbk": c_sbk[l, s_], "sbv": c_sbv[l, s_]}
                new = {"ckv": o_ckv_s[l, s_ * DS:(s_ + 1) * DS, :], "kpe": o_kpe_s[l, s_ * DS:(s_ + 1) * DS, :],
                       "sbk": o_sbk_s[l, s_ * DS:(s_ + 1) * DS, :], "sbv": o_sbv_s[l, s_ * DS:(s_ + 1) * DS, :]}
                attention(l, qn, b_qn, qp, b_qp, qs, b_qs, TP + s_ * DS, DS,
                          [(past, c.PAST, []), (new, DS, [d_kout[l]])], True, B, ssq)
        for (which, nh, goff, ch0) in ((0, c.HM, GO_MO, 0), (1, c.HS, GO_SO, c.HM + c.PWC)):
            for (t0, w) in tiles:
                T.op(act, [B.b_ssq], [B.b_ssq], lambda: nc.scalar.activation(
                    out=B.ssq[:, which, t0:t0 + w], in_=B.ssq[:, which, t0:t0 + w], func=AF.Sqrt, bias=EPS,
                    scale=1.0 / (nh * 128)))
                T.op(dve, [B.b_ssq], [B.b_ssq], lambda: nc.vector.reciprocal(out=B.ssq[:, which, t0:t0 + w],
                                                                            in_=B.ssq[:, which, t0:t0 + w]))
                for hh in range(nh):
                    eng, q = (nc.vector, dve)
                    T.op(q, [B.b_ssq, b_gains, b_hm], [b_hm], lambda: eng.scalar_tensor_tensor(
                        out=hm[:, ch0 + hh, t0:t0 + w], in0=hm[:, ch0 + hh, t0:t0 + w],
                        scalar=gains[:, l, goff + hh:goff + hh + 1], in1=B.ssq[:, which, t0:t0 + w],
                        op0=ALU.mult, op1=ALU.mult))

    for p in range(c.NPASS):
        tiles = pass_tiles(p)
        for l in range(L):
            with ExitStack() as st:
                x, b_x = sb("x", [128, KC, NW], F32, st)
                if l == 0:
                    with ExitStack() as st2:
                        load_x(p, x, b_x, tiles, st2)
                        T.barrier()
                else:
                    for (t0, w) in tiles:
                        T.dma(sp, x[:, :, t0:t0 + w], xscr[:, :, t0:t0 + w], [d_xscr], [b_x], b_x)
                ringA = Ring("ra", 4, [128, KC, 128], st); ringD = Ring("rd", 4, [128, c.D], st)
                actb = [sb("act%d" % i, [128, NW], BF16, st) for i in range(6)]
                silb = [sb("sil%d" % i, [128, TP], F32, st) for i in range(2)]
                rstd, b_rstd = sb("rstd", [128, NW], F32, st)
                rmsnorm_feat(x, b_x, KC, tiles, GO_F1, l, hm, b_hm, rstd, b_rstd)
                ffn(l, w1g, w1u, w1d, x, b_x, tiles, ringA, ringD, actb, silb)
                rmsnorm_feat(x, b_x, KC, tiles, GO_MIX, l, hm, b_hm, rstd, b_rstd)
                for (t0, w) in tiles:
                    T.dma(sp, xscr[:, :, t0:t0 + w], x[:, :, t0:t0 + w], [b_x], [d_xscr], b_x)
                T.barrier()
            with ExitStack() as st:
                qn, b_qn = sb("qn", [128, c.HM, NW], BF16, st)
                qp, b_qp = sb("qp", [128, c.HM // 2, NW], BF16, st)
                qs, b_qs = sb("qs", [128, c.HS, NW], BF16, st)
                with ExitStack() as st2:
                    ring = Ring("rq", 4, [128, KC, 128], st2)
                    mix_q(l, p, tiles, ring, qn, b_qn, qp, b_qp, qs, b_qs, st2)
                    T.barrier()
                with ExitStack() as st2:
                    ring = Ring("rk", 4, [128, KC, 128], st2)
                    mix_k(l, p, tiles, ring, st2)
                    mix_pool(l, p, tiles, ring, st2)
                    T.barrier()
                with ExitStack() as st2:
                    attn_stage(l, p, tiles, qn, b_qn, qp, b_qp, qs, b_qs, st2)
                    T.barrier()
            with ExitStack() as st:
                x, b_x = sb("x", [128, KC, NW], F32, st)
                for (t0, w) in tiles:
                    T.dma(sp, x[:, :, t0:t0 + w], xscr[:, :, t0:t0 + w], [d_xscr], [b_x], b_x)
                with ExitStack() as st2:
                    ring = Ring("rw", 4, [128, c.MIXC, 128], st2)
                    for m in range(KC):
                        def ev(pa, bps, t0, w, m=m):
                            T.op(dve, [bps, b_x], [b_x], lambda: nc.vector.tensor_tensor(
                                out=x[:, m, t0:t0 + w], in0=pa, in1=x[:, m, t0:t0 + w], op=ALU.add))
                        proj_feat(ring, w_out[l], c.MIXC, m * 128, 128, hm, b_hm, tiles, m % 2, ev)
                    T.barrier()
                with ExitStack() as st2:
                    ringA = Ring("ra", 4, [128, KC, 128], st2); ringD = Ring("rd", 4, [128, c.D], st2)
                    actb = [sb("act%d" % i, [128, NW], BF16, st2) for i in range(6)]
                    silb = [sb("sil%d" % i, [128, TP], F32, st2) for i in range(2)]
                    rstd, b_rstd = sb("rstd", [128, NW], F32, st2)
                    rmsnorm_feat(x, b_x, KC, tiles, GO_F2, l, hm, b_hm, rstd, b_rstd)
                    ffn(l, w2g, w2u, w2d, x, b_x, tiles, ringA, ringD, actb, silb)
                    if l == L - 1:
                        rmsnorm_feat(x, b_x, KC, tiles, 0, l, x, b_x, rstd, b_rstd, gt=gfin, sq=hm, b_sq=b_hm)
                    T.barrier()
                if l == L - 1:
                    with ExitStack() as st2:
                        store_y(p, x, b_x, tiles, st2)
                        T.barrier()
                else:
                    for (t0, w) in tiles:
                        T.dma(sp, xscr[:, :, t0:t0 + w], x[:, :, t0:t0 + w], [b_x], [d_xscr], b_x)
                    T.barrier()
    T.barrier()
    es.close()
    return nc, T


def host_consts(c: Cfg):
    k = {}
    k["k_ident"] = np.eye(128, dtype=np.float32)
    j = np.arange(128)
    k["k_ut"] = (j[:, None] >= j[None, :]).astype(np.float32)
    r = np.zeros((64, 64), np.float32)
    for d in range(32):
        r[d, d + 32] = -1.0; r[d + 32, d] = 1.0
    rt = r.T.copy()
    R128 = np.zeros((128, 128), np.float32); R128[:64, :64] = rt; R128[64:, 64:] = rt
    k["k_r"] = R128
    kk = np.arange(128)[:, None]; q = np.arange(c.TP)[None, :]
    mm = np.zeros((c.KGB, 128, c.TP), np.float32); ms = np.zeros((c.KGB, 128, c.TP), np.float32)
    for kb in range(c.KGB):
        key = kb * 128 + kk
        mm[kb] = ((key // 64) <= (q // 64)).astype(np.float32)
        ms[kb] = (key < q).astype(np.float32)
    k["k_mmla"] = mm; k["k_msb"] = ms
    half = 32
    inv = (1.0 / (np.float32(ROPE_THETA) ** (np.arange(half, dtype=np.float32) / np.float32(half)))).astype(np.float32)
    pos = np.concatenate([np.arange(c.SEQ), c.PAST + np.tile(np.arange(c.DS), c.NS)]).astype(np.float32)
    ang = (pos[:, None] * inv[None, :]).astype(np.float32)
    cos = np.cos(ang).astype(np.float32); sin = np.sin(ang).astype(np.float32)
    k["k_cosk"] = cos; k["k_sink"] = sin
    cq = np.concatenate([cos, cos], axis=1).T
    sq = np.concatenate([sin, sin], axis=1).T
    k["k_cosq"] = np.ascontiguousarray(np.concatenate([cq, cq], axis=0), dtype=np.float32)
    k["k_sinq"] = np.ascontiguousarray(np.concatenate([sq, sq], axis=0), dtype=np.float32)
    ic = np.zeros((128, 4, 16), np.float32)
    for g, w in enumerate((2, 4, 8, 16)):
        ic[:, g, :] = 1.0 / np.minimum(np.arange(16) + 1, w).astype(np.float32)
    k["k_invc"] = ic
    return k


_CACHE = {}


def run(c: Cfg, inputs, n_cores=8, runner=None):
    key = (c.D, c.DFF, c.SEQ, c.TP, c.PAST)
    if key not in _CACHE:
        _CACHE[key] = build_program(c)
    nc, T = _CACHE[key]
    f = lambda a: np.ascontiguousarray(np.asarray(a, dtype=np.float32))
    I = {k: f(v) for k, v in inputs.items()}
    L = c.L
    shared = {
        "g_ffn1": I["g_ffn1"], "w1_gate": I["w1_gate"], "w1_up": I["w1_up"], "w1_down": I["w1_down"],
        "g_mix": I["g_mix"], "w_in": I["w_in"], "g_qnorm": I["g_qnorm"], "w_uq": I["w_uq"], "g_kvnorm": I["g_kvnorm"],
        "w_uk": I["w_uk"].reshape(L, c.KVL, c.HM * 128), "w_uv": I["w_uv"].reshape(L, c.KVL, c.HM * 128),
        "w_pool": I["w_pool"], "pool_scale": I["pool_scale"], "g_mla_out": I["g_mla_out"], "g_sb_out": I["g_sb_out"],
        "w_out": I["w_out"], "g_ffn2": I["g_ffn2"], "w2_gate": I["w2_gate"], "w2_up": I["w2_up"], "w2_down": I["w2_down"],
        "g_final": I["g_final"],
    }
    shared.update(host_consts(c))
    nb = I["x_prompt"].shape[0]
    in_maps = []
    for core in range(n_cores):
        s0 = core * c.NS
        m = dict(shared)
        m["xp"] = I["x_prompt"][core % nb]
        m["xs"] = f(I["x_sample"][s0:s0 + c.NS].reshape(c.NST, c.D))
        m["c_ckv"] = f(I["cache_ckv"][:, s0:s0 + c.NS]); m["c_kpe"] = f(I["cache_kpe"][:, s0:s0 + c.NS])
        m["c_pool"] = f(I["state_pool"][:, s0:s0 + c.NS])
        m["c_sbk"] = f(I["cache_sb_k"][:, s0:s0 + c.NS].reshape(L, c.NS, c.PAST, c.SBW))
        m["c_sbv"] = f(I["cache_sb_v"][:, s0:s0 + c.NS].reshape(L, c.NS, c.PAST, c.SBW))
        in_maps.append(m)
    if runner is None:
        res = run_bass_kernel_spmd(nc, in_maps, core_ids=list(range(n_cores))).results
    else:
        res = runner(nc, in_maps)
    R = res
    st = lambda name, cores: np.stack([R[i][name] for i in cores], axis=0)
    pc = list(range(nb)); ac = list(range(n_cores))
    nsq = n_cores * c.NS
    y_prompt = st("y_p", pc)
    y_sample = st("y_s", ac).reshape(nsq, c.DS, c.D)
    ckv_p = st("o_ckv_p", pc).transpose(1, 0, 2, 3)
    kpe_p = st("o_kpe_p", pc).transpose(1, 0, 2, 3)
    pool_p = st("o_pool_p", pc).transpose(1, 0, 2, 3)
    sbk_p = st("o_sbk_p", pc).transpose(1, 0, 2, 3).reshape(L, nb, c.SEQ, c.HS, 128)
    sbv_p = st("o_sbv_p", pc).transpose(1, 0, 2, 3).reshape(L, nb, c.SEQ, c.HS, 128)
    def samp(name, rows, width):
        a = st(name, ac)
        a = a.reshape(n_cores, L, c.NS, rows, width).transpose(1, 0, 2, 3, 4).reshape(L, nsq, rows, width)
        return a
    ckv_s = samp("o_ckv_s", c.DS, c.KVL); kpe_s = samp("o_kpe_s", c.DS, 64)
    pool_s = samp("o_pool_s", 15, c.PW)
    sbk_s = samp("o_sbk_s", c.DS, c.SBW).reshape(L, nsq, c.DS, c.HS, 128)
    sbv_s = samp("o_sbv_s", c.DS, c.SBW).reshape(L, nsq, c.DS, c.HS, 128)
    outs = (y_prompt, y_sample, ckv_p, kpe_p, pool_p, sbk_p, sbv_p, ckv_s, kpe_s, pool_s, sbk_s, sbv_s)
    return tuple(np.ascontiguousarray(o, dtype=np.float32) for o in outs)


def kernel(**inputs):
    return run(Cfg(), inputs)
```

```python
import math
from contextlib import ExitStack
import numpy as np
import ml_dtypes
import concourse.bass as bass
import concourse.mybir as mybir
from concourse.bass_utils import run_bass_kernel_spmd

F32 = mybir.dt.float32
BF16 = mybir.dt.bfloat16
AF = mybir.ActivationFunctionType
ALU = mybir.AluOpType
EPS = 1e-6
ROPE_THETA = 10000.0


class Cfg:
    def __init__(s, D=4096, DFF=11008, SEQ=2048, TP=512, PAST=4096, QL=1024, KVL=512, HM=16, HS=8,
                 PG=256, L=2, NS=2, DS=16):
        s.D, s.DFF, s.SEQ, s.TP, s.PAST, s.QL, s.KVL, s.HM, s.HS, s.PG, s.L, s.NS, s.DS = \
            D, DFF, SEQ, TP, PAST, QL, KVL, HM, HS, PG, L, NS, DS
        s.KC = D // 128; s.FC = DFF // 128; s.NPASS = SEQ // TP
        s.QLC = QL // 128; s.KVC = KVL // 128
        s.PW = 4 * PG; s.PGC = PG // 128; s.PWC = s.PW // 128
        s.MLAW = HM * 128; s.SBW = HS * 128
        s.MIX = s.MLAW + s.PW + s.SBW; s.MIXC = s.MIX // 128
        s.oq = 0; s.okv = QL; s.ope = QL + KVL; s.ou = s.ope + 64
        s.oqs = s.ou + s.PW; s.oks = s.oqs + s.SBW; s.ovs = s.oks + s.SBW
        s.INW = s.ovs + s.SBW
        s.NST = NS * DS
        s.NW = TP + s.NST
        s.KGB = TP // 128
        s.NG = 2
        s.HMG = HM // 2; s.HSG = HS // 2
        s.mla_scale = 1.0 / math.sqrt(192.0); s.sb_scale = 1.0 / math.sqrt(128.0)


def chunk_lists(c):
    ch = {}
    ch["ffn"] = [[(j * 128, 128, 0)] for j in range(c.FC)]
    win = []
    for j in range(c.QLC): win.append([(c.oq + j * 128, 128, 0)])
    for j in range(c.KVC): win.append([(c.okv + j * 128, 128, 0)])
    win.append([(c.ope, 64, 0)])
    for off, n in ((c.ou, c.PWC), (c.oqs, c.HS), (c.oks, c.HS), (c.ovs, c.HS)):
        for j in range(n): win.append([(off + j * 128, 128, 0)])
    ch["w_in"] = win
    ch["w_out"] = [[(j * 128, 128, 0)] for j in range(c.KC)]
    wuq = [[(h * 192, 128, 0)] for h in range(c.HM)]
    for hp in range(c.HM // 2):
        wuq.append([((2 * hp) * 192 + 128, 64, 0), ((2 * hp + 1) * 192 + 128, 64, 64)])
    ch["w_uq"] = wuq
    return ch


def pack_slabs(W, chunks):
    L_, K_, F_ = W.shape
    kc = K_ // 128
    Wv = W.reshape(L_, kc, 128, F_)
    out = np.zeros((L_, len(chunks), 128, kc, 128), np.float32)
    for i, pieces in enumerate(chunks):
        for (c0, m, d0) in pieces:
            out[:, i, :, :, d0:d0 + m] = Wv[:, :, :, c0:c0 + m].transpose(0, 2, 1, 3)
    return out


class Q:
    def __init__(s, nc, eng, name):
        s.eng = eng; s.sem = nc.alloc_semaphore(name); s.n = 0; s.waited = {}


class Buf:
    def __init__(s, name):
        s.name = name; s.w = {}; s.r = {}; s.ds = {}


class DSem:
    def __init__(s, sem):
        s.sem = sem; s.n = 0


class Tr:
    def __init__(s, nc):
        s.nc = nc
        s.pe = Q(nc, nc.tensor, "q_pe"); s.act = Q(nc, nc.scalar, "q_act")
        s.dve = Q(nc, nc.vector, "q_dve"); s.pool = Q(nc, nc.gpsimd, "q_pool")
        s.sp = Q(nc, nc.sync, "q_sp")
        s.queues = [s.pe, s.act, s.dve, s.pool, s.sp]
        s.dbufs = []
        s.dpool = {}
        s.dall = []
        s.ninst = 0

    def _wait(s, q, ev):
        if ev[0] == 'e':
            sem, val, key = ev[1].sem, ev[2], id(ev[1])
        else:
            sem, val, key = ev[1].sem, 16 * ev[1].n, id(ev[1])
        if q.waited.get(key, 0) >= val:
            return
        q.eng.wait_ge(sem, val)
        q.waited[key] = val

    def _deps(s, q, reads, writes):
        for b in reads:
            for ev in b.w.values():
                s._wait(q, ev)
        for b in writes:
            for ev in b.w.values():
                s._wait(q, ev)
            for ev in b.r.values():
                s._wait(q, ev)

    def _reg(s, ev, key, reads, writes):
        for b in reads:
            b.r[key] = ev
        for b in writes:
            b.w = {key: ev}; b.r = {}

    def op(s, q, reads, writes, fn):
        s._deps(q, reads, writes)
        ins = fn()
        q.n += 1
        ins.then_inc(q.sem, 1)
        s._reg(('e', q, q.n), id(q), reads, writes)
        s.ninst += 1
        return ins

    def dma(s, q, out, in_, reads, writes, sb):
        qk = id(q)
        if qk not in sb.ds:
            pl = s.dpool.setdefault(qk, [])
            if pl:
                sb.ds[qk] = pl.pop()
            else:
                sb.ds[qk] = DSem(s.nc.alloc_semaphore("dsem%d" % len(s.dall))); s.dall.append(sb.ds[qk])
            s.dbufs.append((sb, qk))
        d = sb.ds[qk]
        s._deps(q, reads, writes)
        ins = q.eng.dma_start(out=out, in_=in_)
        d.n += 1
        ins.then_inc(d.sem, 16)
        s._reg(('d', d), id(d), reads, writes)
        s.ninst += 1

    def barrier(s):
        for q in s.queues:
            for q2 in s.queues:
                if q2 is not q and q2.n > 0:
                    s._wait(q, ('e', q2, q2.n))
            for d in s.dall:
                if d.n > 0:
                    s._wait(q, ('d', d))
        for (b, qk) in s.dbufs:
            s.dpool[qk].append(b.ds.pop(qk))
        s.dbufs = []


def build_program(c: Cfg):
    nc = bass.Bass("TRN2", target_bir_lowering=False)
    T = Tr(nc)
    pe, act, dve, pool, sp = T.pe, T.act, T.dve, T.pool, T.sp
    L, KC, FC, TP, NW, NST, DS = c.L, c.KC, c.FC, c.TP, c.NW, c.NST, c.DS

    def din(name, shape):
        return nc.dram_tensor(name, list(shape), F32, kind="ExternalInput").ap()

    def dout(name, shape):
        return nc.dram_tensor(name, list(shape), F32, kind="ExternalOutput").ap()

    xp = din("xp", [c.SEQ, c.D]); xs = din("xs", [NST, c.D])
    c_ckv = din("c_ckv", [L, c.NS, c.PAST, c.KVL]); c_kpe = din("c_kpe", [L, c.NS, c.PAST, 64])
    c_pool = din("c_pool", [L, c.NS, 15, c.PW])
    c_sbk = din("c_sbk", [L, c.NS, c.PAST, c.SBW]); c_sbv = din("c_sbv", [L, c.NS, c.PAST, c.SBW])
    g_ffn1 = din("g_ffn1", [L, c.D]); w1g = din("w1_gate", [L, FC, 128, KC, 128]); w1u = din("w1_up", [L, FC, 128, KC, 128])
    w1d = din("w1_down", [L, c.DFF, c.D]); g_mix = din("g_mix", [L, c.D]); CH = chunk_lists(c)
    w_in = din("w_in", [L, len(CH["w_in"]), 128, KC, 128])
    g_qn = din("g_qnorm", [L, c.QL]); w_uq = din("w_uq", [L, len(CH["w_uq"]), 128, c.QLC, 128])
    g_kvn = din("g_kvnorm", [L, c.KVL]); w_uk = din("w_uk", [L, c.KVL, c.HM * 128])
    w_uv = din("w_uv", [L, c.KVL, c.HM * 128]); w_pool = din("w_pool", [L, 4, c.PG, c.PG])
    pscale = din("pool_scale", [L, c.PW]); g_mo = din("g_mla_out", [L, c.MLAW]); g_so = din("g_sb_out", [L, c.SBW])
    w_out = din("w_out", [L, KC, 128, c.MIXC, 128]); g_ffn2 = din("g_ffn2", [L, c.D])
    w2g = din("w2_gate", [L, FC, 128, KC, 128]); w2u = din("w2_up", [L, FC, 128, KC, 128]); w2d = din("w2_down", [L, c.DFF, c.D])
    g_fin = din("g_final", [c.D])
    k_ident = din("k_ident", [128, 128]); k_ut = din("k_ut", [128, 128]); k_r = din("k_r", [128, 128])
    k_mmla = din("k_mmla", [c.KGB, 128, TP]); k_msb = din("k_msb", [c.KGB, 128, TP])
    k_cosq = din("k_cosq", [128, c.SEQ + NST]); k_sinq = din("k_sinq", [128, c.SEQ + NST])
    k_cosk = din("k_cosk", [c.SEQ + NST, 32]); k_sink = din("k_sink", [c.SEQ + NST, 32])
    k_invc = din("k_invc", [128, 4, 16])

    y_p = dout("y_p", [c.SEQ, c.D]); y_s = dout("y_s", [NST, c.D])
    o_ckv_p = dout("o_ckv_p", [L, c.SEQ, c.KVL]); o_kpe_p = dout("o_kpe_p", [L, c.SEQ, 64])
    o_pool_p = dout("o_pool_p", [L, 15, c.PW]); o_sbk_p = dout("o_sbk_p", [L, c.SEQ, c.SBW])
    o_sbv_p = dout("o_sbv_p", [L, c.SEQ, c.SBW])
    o_ckv_s = dout("o_ckv_s", [L, NST, c.KVL]); o_kpe_s = dout("o_kpe_s", [L, NST, 64])
    o_pool_s = dout("o_pool_s", [L, c.NS * 15, c.PW]); o_sbk_s = dout("o_sbk_s", [L, NST, c.SBW])
    o_sbv_s = dout("o_sbv_s", [L, NST, c.SBW])
    xscr = nc.dram_tensor("xscr", [128, KC, NW], F32).ap()
    d_xscr = Buf("xscr")
    d_kout = [Buf("kout%d" % l) for l in range(L)]
    d_yout = Buf("yout")

    es = ExitStack()

    uid = [0]

    def sb(name, shape, dt, stack=None):
        uid[0] += 1
        nm = "%s_%d" % (name, uid[0])
        t = (stack or es).enter_context(nc.sbuf_tensor(nm, list(shape), dt))
        return t, Buf(nm)

    PS = []
    for i in range(8):
        t = es.enter_context(nc.psum_tensor("ps%d" % i, [128, 512], F32))
        PS.append((t, Buf("ps%d" % i)))

    ident, b_ident = sb("ident", [128, 128], F32)
    identb, b_identb = sb("identb", [128, 128], BF16)
    ut, b_ut = sb("ut", [128, 128], F32)
    rr, b_rr = sb("rr", [128, 128], F32)
    onesf, b_onesf = sb("onesf", [128, 128], F32)
    onesb, b_onesb = sb("onesb", [128, 128], BF16)
    mmla, b_mmla = sb("mmla", [128, c.KGB, TP], BF16)
    msb, b_msb = sb("msb", [128, c.KGB, TP], BF16)
    invc, b_invc = sb("invc", [128, 4, 16], F32)
    gains, b_gains = sb("gains", [128, L, 3 * KC + c.QLC + c.HM + c.HS + c.PWC], F32)
    gfin, b_gfin = sb("gfin", [128, KC], F32)
    gkv, b_gkv = sb("gkv", [128, L, c.KVL], F32)
    uh, b_uh = sb("uh", [128, L, c.PWC, 15], F32)
    hm, b_hm = sb("hm", [128, max(KC, c.MIXC), NW], BF16)

    GO_F1, GO_MIX, GO_F2 = 0, KC, 2 * KC
    GO_QN = 3 * KC; GO_MO = GO_QN + c.QLC; GO_SO = GO_MO + c.HM; GO_PS = GO_SO + c.HS

    def load_consts():
        T.dma(sp, ident[:], k_ident[:, :], [], [b_ident], b_ident)
        T.dma(sp, ut[:], k_ut[:, :], [], [b_ut], b_ut)
        T.dma(sp, rr[:], k_r[:, :], [], [b_rr], b_rr)
        T.op(dve, [b_ident], [b_identb], lambda: nc.vector.tensor_copy(out=identb[:], in_=ident[:]))
        T.op(dve, [], [b_onesf], lambda: nc.vector.memset(onesf[:], 1.0))
        T.op(dve, [], [b_onesb], lambda: nc.vector.memset(onesb[:], 1.0))
        T.op(dve, [], [b_uh], lambda: nc.vector.memset(uh[:], 0.0))
        for kb in range(c.KGB):
            T.dma(sp, cst[:, 0, :], k_mmla[kb], [], [b_cst], b_cst)
            T.op(dve, [b_cst], [b_mmla], lambda: nc.vector.tensor_copy(out=mmla[:, kb, :], in_=cst[:, 0, :]))
            T.dma(sp, cst[:, 1, :], k_msb[kb], [], [b_cst], b_cst)
            T.op(dve, [b_cst], [b_msb], lambda: nc.vector.tensor_copy(out=msb[:, kb, :], in_=cst[:, 1, :]))
        T.dma(sp, invc[:], k_invc[:, :, :], [], [b_invc], b_invc)
        T.dma(sp, gfin[:], g_fin.rearrange("(k p) -> p k", p=128), [], [b_gfin], b_gfin)
        for l in range(L):
            for (src, off, n) in ((g_ffn1, GO_F1, KC), (g_mix, GO_MIX, KC), (g_ffn2, GO_F2, KC),
                                  (g_qn, GO_QN, c.QLC), (g_mo, GO_MO, c.HM), (g_so, GO_SO, c.HS),
                                  (pscale, GO_PS, c.PWC)):
                T.dma(sp, gains[:, l, off:off + n], src[l].rearrange("(k p) -> p k", p=128), [], [b_gains], b_gains)
            T.dma(sp, gkv[:, l, :], g_kvn[l].partition_broadcast(128), [], [b_gkv], b_gkv)

    with ExitStack() as st0:
        cst, b_cst = sb("cst", [128, 2, TP], F32, st0)
        with nc.allow_non_contiguous_dma(reason="tiny gain-vector loads"):
            load_consts()
        T.barrier()

    def pass_tiles(p):
        tl = [(0, TP)]
        if p == c.NPASS - 1:
            tl.append((TP, NST))
        return tl

    class Ring:
        def __init__(s, name, n, shape, stack):
            s.slots = [sb("%s%d" % (name, i), shape, BF16, stack) for i in range(n)]
            s.i = 0

        def next(s):
            t = s.slots[s.i % len(s.slots)]; s.i += 1
            return t

    CHIDX = {nm: {tuple(pc): i for i, pc in enumerate(lst)} for nm, lst in CH.items()}

    def load_slab(ring, Wl, kchunks, c0, m, kind="ffn"):
        t, b = ring.next()
        idx = CHIDX[kind][((c0, m, 0),)]
        T.dma(pool, t[:, :kchunks, :], Wl[idx], [], [b], b)
        return t, b

    def load_rows(ring, W2d, r0, ncols):
        t, b = ring.next()
        T.dma(pool, t[:, :ncols], W2d[r0:r0 + 128, 0:ncols], [], [b], b)
        return t, b

    small_slot = [0]

    def ps_small(w):
        i = small_slot[0] % 8; small_slot[0] += 1
        return PS[6][0][:, i * 64:i * 64 + w], PS[6][1]

    def mm_group(out_ap, b_out, pairs, reads):
        def fn():
            n = len(pairs); ins = None
            for i, (l_, r_) in enumerate(pairs):
                ins = nc.tensor.matmul(out_ap, lhsT=l_, rhs=r_, start=(i == 0), stop=(i == n - 1))
            return ins
        return T.op(pe, reads, [b_out], fn)

    def rstd_from_ps(ps_ap, b_ps, out_ap, b_o, dim, eng_alt=0):
        T.op(act, [b_ps], [b_o], lambda: nc.scalar.activation(out=out_ap, in_=ps_ap, func=AF.Sqrt,
                                                               bias=EPS, scale=1.0 / dim))
        T.op(dve, [b_o], [b_o], lambda: nc.vector.reciprocal(out=out_ap, in_=out_ap))

    def rmsnorm_feat(src, b_src, nch, tiles, goff, l, dst, b_dst, rstd, b_rstd, gt=None, sq=None, b_sq=None):
        gt = gains if gt is None else gt
        sq = dst if sq is None else sq; b_sq = b_dst if b_sq is None else b_sq
        for ti, (c0, w) in enumerate(tiles):
            for k in range(nch):
                T.op(act, [b_src], [b_sq], lambda: nc.scalar.activation(out=sq[:, k, c0:c0 + w], in_=src[:, k, c0:c0 + w],
                                                                      func=AF.Square))
            pst, bps = PS[7]
            mm_group(pst[:, :w], bps, [(onesb[:, :], sq[:, k, c0:c0 + w]) for k in range(nch)], [b_sq, b_onesb])
            rstd_from_ps(pst[:, :w], bps, rstd[:, c0:c0 + w], b_rstd, nch * 128)
            for k in range(nch):
                g_ap = gt[:, k:k + 1] if gt is gfin else gt[:, l, goff + k:goff + k + 1]
                eng, q = (nc.vector, dve)
                T.op(q, [b_src, b_rstd, b_gains, b_gfin], [b_dst],
                     lambda: eng.scalar_tensor_tensor(out=dst[:, k, c0:c0 + w], in0=src[:, k, c0:c0 + w],
                                                      scalar=g_ap, in1=rstd[:, c0:c0 + w],
                                                      op0=ALU.mult, op1=ALU.mult))

    def ffn(l, Wg, Wu, Wd, x, b_x, tiles, ringA, ringD, actb, silb):
        G = 3
        groups = [list(range(g0, min(g0 + G, FC))) for g0 in range(0, FC, G)]
        gi = 0
        for grp in groups:
            acts = []
            dslabs = []
            for j in grp:
                sg, bsg = load_slab(ringA, Wg[l], KC, j * 128, 128)
                su, bsu = load_slab(ringA, Wu[l], KC, j * 128, 128)
                at, b_at = actb[gi % len(actb)]; gi += 1
                acts.append((at, b_at))
                for ti, (c0, w) in enumerate(tiles):
                    if ti == 0:
                        gps, bg = PS[j % 2]; ups, bu = PS[2 + j % 2]
                        gp, up = gps[:, :w], ups[:, :w]
                    else:
                        gp, bg = ps_small(w); up, bu = ps_small(w)
                    mm_group(gp, bg, [(sg[:, k, :], hm[:, k, c0:c0 + w]) for k in range(KC)], [bsg, b_hm])
                    mm_group(up, bu, [(su[:, k, :], hm[:, k, c0:c0 + w]) for k in range(KC)], [bsu, b_hm])
                    st, b_st = silb[(j + ti) % len(silb)]
                    T.op(act, [bg], [b_st], lambda: nc.scalar.activation(out=st[:, :w], in_=gp, func=AF.Silu))
                    T.op(dve, [b_st, bu], [b_at], lambda: nc.vector.tensor_tensor(out=at[:, c0:c0 + w], in0=st[:, :w],
                                                                                 in1=up, op=ALU.mult))
            for j in grp:
                dslabs.append(load_rows(ringD, Wd[l], j * 128, c.D))
            for m in range(KC):
                for ti, (c0, w) in enumerate(tiles):
                    if ti == 0:
                        dps, bd = PS[4 + m % 2]; dp = dps[:, :w]
                    else:
                        dp, bd = ps_small(w)
                    mm_group(dp, bd, [(dslabs[i][0][:, m * 128:(m + 1) * 128], acts[i][0][:, c0:c0 + w])
                                      for i in range(len(grp))],
                             [dslabs[i][1] for i in range(len(grp))] + [a[1] for a in acts])
                    T.op(dve, [bd, b_x], [b_x], lambda: nc.vector.scalar_tensor_tensor(
                        out=x[:, m, c0:c0 + w], in0=dp, scalar=0.5, in1=x[:, m, c0:c0 + w],
                        op0=ALU.mult, op1=ALU.add))

    def load_x(p, x, b_x, tiles, stack):
        xin, b_xin = sb("xin", [128, c.D], F32, stack)
        for (c0, w) in tiles:
            for tb in range((w + 127) // 128):
                n = min(128, w - tb * 128)
                src = xp[p * TP + tb * 128: p * TP + tb * 128 + n, :] if c0 == 0 else xs[0:n, :]
                T.dma(sp, xin[:n, :], src, [], [b_xin], b_xin)
                for k0 in range(0, KC, 4):
                    kk = min(4, KC - k0)
                    pst, bps = PS[7]
                    def fn():
                        ins = None
                        for j in range(kk):
                            ins = nc.tensor.transpose(out=pst[:, j * 128:j * 128 + n],
                                                      in_=xin[:n, (k0 + j) * 128:(k0 + j + 1) * 128],
                                                      identity=ident[:n, :n])
                        return ins
                    T.op(pe, [b_xin, b_ident], [bps], fn)
                    T.op(act, [bps], [b_x], lambda: nc.scalar.copy(
                        out=x[:, k0:k0 + kk, c0 + tb * 128:c0 + tb * 128 + n],
                        in_=pst[:, :kk * 128].rearrange("p (j t) -> p j t", t=128)[:, :, :n]))

    def store_y(p, x, b_x, tiles, stack):
        yo, b_yo = sb("yo", [128, 2, 512], F32, stack)
        cnt = 0
        for (c0, w) in tiles:
            for tb in range((w + 127) // 128):
                n = min(128, w - tb * 128)
                for k0 in range(0, KC, 4):
                    kk = min(4, KC - k0)
                    pst, bps = PS[7]
                    def fn():
                        ins = None
                        for j in range(kk):
                            ins = nc.tensor.transpose(out=pst[:n, j * 128:(j + 1) * 128],
                                                      in_=x[:, k0 + j, c0 + tb * 128:c0 + tb * 128 + n],
                                                      identity=ident[:, :])
                        return ins
                    T.op(pe, [b_x, b_ident], [bps], fn)
                    s_ = cnt % 2; cnt += 1
                    T.op(act, [bps], [b_yo], lambda: nc.scalar.copy(out=yo[:n, s_, :kk * 128], in_=pst[:n, :kk * 128]))
                    dst = (y_p[p * TP + tb * 128: p * TP + tb * 128 + n, k0 * 128:(k0 + kk) * 128] if c0 == 0
                           else y_s[0:n, k0 * 128:(k0 + kk) * 128])
                    T.dma(sp, dst, yo[:n, s_, :kk * 128], [b_yo], [d_yout], b_yo)

    def proj_feat(ring, W2d, kch, c0, m, src, b_src, tiles, big_bank, evac, kind="w_in"):
        s_, bs = load_slab(ring, W2d, kch, c0, m, kind)
        for ti, (t0, w) in enumerate(tiles):
            if ti == 0:
                pst, bps = PS[big_bank]; pa = pst[:m, :w]
            else:
                pa, bps = ps_small(w); pa = pa[:m, :]
            mm_group(pa, bps, [(s_[:, k, :m], src[:, k, t0:t0 + w]) for k in range(kch)], [bs, b_src])
            evac(pa, bps, t0, w)

    def copy_evac(dst_fn, b_dst, alt=[0]):
        def ev(pa, bps, t0, w):
            alt[0] += 1
            if alt[0] % 2:
                T.op(act, [bps], [b_dst], lambda: nc.scalar.copy(out=dst_fn(t0, w), in_=pa))
            else:
                T.op(dve, [bps], [b_dst], lambda: nc.vector.tensor_copy(out=dst_fn(t0, w), in_=pa))
        return ev

    def mix_q(l, p, tiles, ring, qn, b_qn, qp, b_qp, qs, b_qs, stack):
        cq, b_cq = sb("cq", [128, c.QLC, NW], F32, stack)
        cqn, b_cqn = sb("cqn", [128, c.QLC, NW], BF16, stack)
        rs, b_rs = sb("rs_q", [128, NW], F32, stack)
        tq, b_tq = sb("tq", [128, 2, TP], F32, stack)
        cs, b_cs = sb("cs_q", [128, 2, NW], F32, stack)
        W = w_in[l]
        for j in range(c.QLC):
            proj_feat(ring, W, KC, c.oq + j * 128, 128, hm, b_hm, tiles, j % 2,
                      copy_evac(lambda t0, w, j=j: cq[:, j, t0:t0 + w], b_cq))
        rmsnorm_feat(cq, b_cq, c.QLC, tiles, GO_QN, l, cqn, b_cqn, rs, b_rs)
        for h in range(c.HM):
            proj_feat(ring, w_uq[l], c.QLC, h * 192, 128, cqn, b_cqn, tiles, h % 2,
                      copy_evac(lambda t0, w, h=h: qn[:, h, t0:t0 + w], b_qn), kind="w_uq")
        T.dma(sp, cs[:, 0, :TP], k_cosq[:, p * TP:(p + 1) * TP], [], [b_cs], b_cs)
        T.dma(sp, cs[:, 1, :TP], k_sinq[:, p * TP:(p + 1) * TP], [], [b_cs], b_cs)
        if len(tiles) > 1:
            T.dma(sp, cs[:, 0, TP:NW], k_cosq[:, c.SEQ:c.SEQ + NST], [], [b_cs], b_cs)
            T.dma(sp, cs[:, 1, TP:NW], k_sinq[:, c.SEQ:c.SEQ + NST], [], [b_cs], b_cs)
        for hp in range(c.HM // 2):
            t_, bt = ring.next()
            T.dma(pool, t_[:, :c.QLC, :], w_uq[l][c.HM + hp], [], [bt], bt)
            for ti, (t0, w) in enumerate(tiles):
                if ti == 0:
                    pst, bps = PS[hp % 2]; pa = pst[:, :w]
                else:
                    pa, bps = ps_small(w)
                mm_group(pa, bps, [(t_[:, k, :], cqn[:, k, t0:t0 + w]) for k in range(c.QLC)], [bt, b_cqn])
                T.op(act, [bps], [b_tq], lambda: nc.scalar.copy(out=tq[:, 0, :w], in_=pa))
                if ti == 0:
                    ps2, bp2 = PS[2 + hp % 2]; pb = ps2[:, :w]
                else:
                    pb, bp2 = ps_small(w)
                mm_group(pb, bp2, [(rr[:, :], tq[:, 0, :w])], [b_rr, b_tq])
                T.op(dve, [b_tq, b_cs], [b_tq], lambda: nc.vector.tensor_tensor(out=tq[:, 0, :w], in0=tq[:, 0, :w],
                                                                              in1=cs[:, 0, t0:t0 + w], op=ALU.mult))
                T.op(dve, [bp2, b_cs], [b_tq], lambda: nc.vector.tensor_tensor(out=tq[:, 1, :w], in0=pb,
                                                                              in1=cs[:, 1, t0:t0 + w], op=ALU.mult))
                T.op(dve, [b_tq], [b_qp], lambda: nc.vector.tensor_tensor(out=qp[:, hp, t0:t0 + w], in0=tq[:, 0, :w],
                                                                          in1=tq[:, 1, :w], op=ALU.add))
        for h in range(c.HS):
            proj_feat(ring, W, KC, c.oqs + h * 128, 128, hm, b_hm, tiles, h % 2,
                      copy_evac(lambda t0, w, h=h: qs[:, h, t0:t0 + w], b_qs))

    def tok_blocks(tiles):
        bl = []
        for (t0, w) in tiles:
            for tb in range((w + 127) // 128):
                bl.append((t0 + tb * 128, min(128, w - tb * 128), t0 != 0))
        return bl

    def mix_k(l, p, tiles, ring, stack):
        NB = c.KGB + 1
        stg, b_stg = sb("stg", [128, NB, 512], F32, stack)
        sm, b_sm = sb("sm_k", [128, NB, 8], F32, stack)
        ck, b_ck = sb("ck", [128, NB, 2, 32], F32, stack)
        junk, b_junk = sb("junk", [128, 512], F32, stack)
        tr_, b_tr = sb("tr_k", [128, 4, 32], F32, stack)
        W = w_in[l]
        blocks = tok_blocks(tiles)
        for bi, (t0, n, is_s) in enumerate(blocks):
            r0 = (c.SEQ + t0 - TP) if is_s else (p * TP + t0)
            T.dma(sp, ck[:n, bi, 0, :], k_cosk[r0:r0 + n, :], [], [b_ck], b_ck)
            T.dma(sp, ck[:n, bi, 1, :], k_sink[r0:r0 + n, :], [], [b_ck], b_ck)

        def tok_proj(c0, width):
            for j0 in range(0, width, 128):
                m = min(128, width - j0)
                s_, bs = load_slab(ring, W, KC, c0 + j0, m, "w_in")
                for bi, (t0, n, is_s) in enumerate(blocks):
                    pa, bps = ps_small(64) if False else (PS[bi % 2][0][:n, :m], PS[bi % 2][1])
                    mm_group(pa, bps, [(hm[:, k, t0:t0 + n], s_[:, k, :m]) for k in range(KC)], [bs, b_hm])
                    if (bi + j0 // 128) % 2:
                        T.op(act, [bps], [b_stg], lambda: nc.scalar.copy(out=stg[:n, bi, j0:j0 + m], in_=pa))
                    else:
                        T.op(dve, [bps], [b_stg], lambda: nc.vector.tensor_copy(out=stg[:n, bi, j0:j0 + m], in_=pa))

        def out_rows(dst_p, dst_s, col0, width):
            for bi, (t0, n, is_s) in enumerate(blocks):
                if is_s:
                    dst = dst_s[l, t0 - TP:t0 - TP + n, col0:col0 + width]
                else:
                    dst = dst_p[l, p * TP + t0:p * TP + t0 + n, col0:col0 + width]
                T.dma(sp, dst, stg[:n, bi, :width], [b_stg], [d_kout[l]], b_stg)

        tok_proj(c.okv, c.KVL)
        for bi, (t0, n, is_s) in enumerate(blocks):
            T.op(act, [b_stg], [b_junk, b_sm], lambda: nc.scalar.activation(
                out=junk[:n, :c.KVL], in_=stg[:n, bi, :c.KVL], func=AF.Square, accum_out=sm[:n, bi, 0:1]))
            T.op(act, [b_sm], [b_sm], lambda: nc.scalar.activation(out=sm[:n, bi, 1:2], in_=sm[:n, bi, 0:1],
                                                                  func=AF.Sqrt, bias=EPS, scale=1.0 / c.KVL))
            T.op(dve, [b_sm], [b_sm], lambda: nc.vector.reciprocal(out=sm[:n, bi, 2:3], in_=sm[:n, bi, 1:2]))
            T.op(dve, [b_stg, b_sm, b_gkv], [b_stg], lambda: nc.vector.scalar_tensor_tensor(
                out=stg[:n, bi, :c.KVL], in0=stg[:n, bi, :c.KVL], scalar=sm[:n, bi, 2:3], in1=gkv[:n, l, :],
                op0=ALU.mult, op1=ALU.mult))
        out_rows(o_ckv_p, o_ckv_s, 0, c.KVL)
        tok_proj(c.ope, 64)
        for bi, (t0, n, is_s) in enumerate(blocks):
            x1 = stg[:n, bi, 0:32]; x2 = stg[:n, bi, 32:64]
            cs_, sn_ = ck[:n, bi, 0, :], ck[:n, bi, 1, :]
            for (o_, a_, b2_) in ((0, x1, cs_), (1, x2, sn_), (2, x2, cs_), (3, x1, sn_)):
                T.op(dve, [b_stg, b_ck], [b_tr], lambda: nc.vector.tensor_tensor(out=tr_[:n, o_, :], in0=a_, in1=b2_,
                                                                              op=ALU.mult))
            T.op(dve, [b_tr], [b_stg], lambda: nc.vector.tensor_tensor(out=x1, in0=tr_[:n, 0, :], in1=tr_[:n, 1, :],
                                                                      op=ALU.subtract))
            T.op(dve, [b_tr], [b_stg], lambda: nc.vector.tensor_tensor(out=x2, in0=tr_[:n, 2, :], in1=tr_[:n, 3, :],
                                                                      op=ALU.add))
        out_rows(o_kpe_p, o_kpe_s, 0, 64)
        for (off, dp_, ds_) in ((c.oks, o_sbk_p, o_sbk_s), (c.ovs, o_sbv_p, o_sbv_s)):
            for c0 in range(0, c.SBW, 512):
                wd = min(512, c.SBW - c0)
                tok_proj(off + c0, wd)
                out_rows(dp_, ds_, c0, wd)

    def mix_pool(l, p, tiles, ring, stack):
        PWC, PGC = c.PWC, c.PGC
        EW = 15 + TP
        u, b_u = sb("u", [128, PWC, EW], F32, stack)
        us, b_us = sb("u_s", [128, c.NS, PWC, 15 + DS], F32, stack)
        ta, b_ta = sb("pl_a", [128, PGC, EW], F32, stack)
        tb_, b_tb = sb("pl_b", [128, PGC, EW], F32, stack)
        pld, b_pld = sb("pld", [128, PWC, NW], BF16, stack)
        wp, b_wp = sb("wp", [128, 4, PGC, c.PG], BF16, stack)
        hs, b_hs = sb("hs", [16, c.PW], F32, stack)
        ho, b_ho = sb("ho", [16, c.PW], F32, stack)
        W = w_in[l]
        for g in range(4):
            T.dma(pool, wp[:, g, :, :], w_pool[l, g].rearrange("(cc p) d -> p cc d", p=128), [], [b_wp], b_wp)
        T.op(dve, [b_uh], [b_u], lambda: nc.vector.tensor_copy(out=u[:, :, 0:15], in_=uh[:, l, :, :]))
        for j in range(PWC):
            def ev(pa, bps, t0, w, j=j):
                if t0 == 0:
                    T.op(act, [bps], [b_u], lambda: nc.scalar.copy(out=u[:, j, 15:15 + w], in_=pa))
                else:
                    for s_ in range(c.NS):
                        T.op(act, [bps], [b_us], lambda: nc.scalar.copy(out=us[:, s_, j, 15:15 + DS],
                                                                       in_=pa[:, s_ * DS:(s_ + 1) * DS]))
            proj_feat(ring, W, KC, c.ou + j * 128, 128, hm, b_hm, tiles, j % 2, ev)
        has_s = len(tiles) > 1
        if has_s:
            for s_ in range(c.NS):
                T.dma(sp, hs[:15, :], c_pool[l, s_], [], [b_hs], b_hs)
                for j in range(PWC):
                    pa, bps = ps_small(16)
                    T.op(pe, [b_hs, b_ident], [bps], lambda: nc.tensor.transpose(
                        out=pa[:, :15], in_=hs[:15, j * 128:(j + 1) * 128], identity=ident[:15, :15]))
                    T.op(act, [bps], [b_us], lambda: nc.scalar.copy(out=us[:, s_, j, 0:15], in_=pa[:, :15]))
        T.op(dve, [b_u], [b_uh], lambda: nc.vector.tensor_copy(out=uh[:, l, :, :], in_=u[:, :, TP:TP + 15]))

        def pool_one(ext_fn, b_ext, ew, out_fn, fix):
            for g, wdw in enumerate((2, 4, 8, 16)):
                ch0 = g * PGC
                e = ext_fn(ch0, PGC)
                cur, b_cur = e, b_ext
                sh = 1; tgl = 0; lo = 0
                while sh < wdw:
                    dst, b_dst = (ta, b_ta) if tgl == 0 else (tb_, b_tb)
                    d_ap = dst[:, :, :ew]
                    T.op(dve, [b_cur], [b_dst], lambda: nc.vector.tensor_tensor(
                        out=d_ap[:, :, lo + sh:ew], in0=cur[:, :, lo + sh:ew], in1=cur[:, :, lo:ew - sh], op=ALU.add))
                    cur, b_cur = d_ap, b_dst
                    lo += sh; sh *= 2; tgl ^= 1
                o = out_fn(ch0, PGC)
                T.op(dve, [b_cur, b_ext], [b_pld], lambda: nc.vector.scalar_tensor_tensor(
                    out=o, in0=cur[:, :, 15:ew], scalar=1.0 / wdw, in1=e[:, :, 15:ew], op0=ALU.mult, op1=ALU.subtract))
                if fix:
                    for cc in range(PGC):
                        T.op(dve, [b_cur, b_invc], [b_tr2], lambda: nc.vector.tensor_tensor(
                            out=tr2[:, 0:16], in0=cur[:, cc, 15:31], in1=invc[:, g, :], op=ALU.mult))
                        T.op(dve, [b_tr2, b_ext], [b_pld], lambda: nc.vector.tensor_tensor(
                            out=o[:, cc, 0:16], in0=tr2[:, 0:16], in1=e[:, cc, 15:31], op=ALU.subtract))

        tr2, b_tr2 = sb("tr2", [128, 16], F32, stack)
        pool_one(lambda ch0, n: u[:, ch0:ch0 + n, :], b_u, EW, lambda ch0, n: pld[:, ch0:ch0 + n, 0:TP], p == 0)
        if has_s:
            for s_ in range(c.NS):
                pool_one(lambda ch0, n: us[:, s_, ch0:ch0 + n, :], b_us, 15 + DS,
                         lambda ch0, n: pld[:, ch0:ch0 + n, TP + s_ * DS:TP + (s_ + 1) * DS], False)
        for g in range(4):
            for dc in range(PGC):
                for ti, (t0, w) in enumerate(tiles):
                    if ti == 0:
                        pst, bps = PS[(g + dc) % 2]; pa = pst[:, :w]
                    else:
                        pa, bps = ps_small(w)
                    mm_group(pa, bps, [(wp[:, g, cc, dc * 128:(dc + 1) * 128], pld[:, g * PGC + cc, t0:t0 + w])
                                       for cc in range(PGC)], [b_wp, b_pld])
                    ch = g * PGC + dc
                    T.op(act, [bps, b_gains], [b_hm], lambda: nc.scalar.activation(
                        out=hm[:, c.HM + ch, t0:t0 + w], in_=pa, func=AF.Copy,
                        scale=gains[:, l, GO_PS + ch:GO_PS + ch + 1]))
        outs = []
        if p == c.NPASS - 1:
            outs.append((lambda j: u[:, j, TP:TP + 15], b_u, o_pool_p[l]))
            for s_ in range(c.NS):
                outs.append((lambda j, s_=s_: us[:, s_, j, DS:DS + 15], b_us, o_pool_s[l, s_ * 15:(s_ + 1) * 15, :]))
        for (srcf, b_s, dst) in outs:
            for j in range(PWC):
                pa, bps = PS[7][0][:, :128], PS[7][1]
                T.op(pe, [b_s, b_ident], [bps], lambda: nc.tensor.transpose(out=pa[:15, :128], in_=srcf(j),
                                                                           identity=ident[:, :]))
                T.op(act, [bps], [b_ho], lambda: nc.scalar.copy(out=ho[:15, j * 128:(j + 1) * 128], in_=pa[:15, :128]))
            T.dma(sp, dst, ho[:15, :], [b_ho], [d_yout], b_ho)

    def attention(l, qn, b_qn, qp, b_qp, qs, b_qs, qc0, N, segs, diag_last, stack_bufs, ssq):
        B = stack_bufs
        (ssq_m, ssq_s) = ssq
        for g in range(c.NG):
            hm0 = g * c.HMG; hs0 = g * c.HSG
            T.dma(pool, B.wuk[:, :, :], w_uk[l].rearrange("(kc p) f -> p kc f", p=128)[:, :, hm0 * 128:(hm0 + c.HMG) * 128],
                  [], [B.b_wuk], B.b_wuk)
            T.dma(pool, B.wuv[:, :, :], w_uv[l].rearrange("(kc p) f -> p kc f", p=128)[:, :, hm0 * 128:(hm0 + c.HMG) * 128],
                  [], [B.b_wuv], B.b_wuv)
            T.op(dve, [], [B.b_accm], lambda: nc.vector.memset(B.accm[:, :, :], 0.0))
            T.op(dve, [], [B.b_accd], lambda: nc.vector.memset(B.accd[:, :, :], 0.0))
            T.op(pool, [], [B.b_accs], lambda: nc.gpsimd.memset(B.accs[:, :, :], 0.0))
            T.op(pool, [], [B.b_spsum], lambda: nc.gpsimd.memset(B.spsum[:, :, :], 0.0))
            groups = []
            for (srcs, ntok, deps) in segs:
                for k0 in range(0, ntok, TP):
                    groups.append((srcs, k0, min(TP, ntok - k0), deps))
            first = True
            for gi_ in range(len(groups) - 1, -1, -1):
                srcs, k0, nk, deps = groups[gi_]
                diag = diag_last and gi_ == len(groups) - 1
                nkb = (nk + 127) // 128
                blk = [(kb, min(128, nk - kb * 128)) for kb in range(nkb)]
                for kb, n in blk:
                    r0 = k0 + kb * 128
                    T.dma(pool, B.rckv[:n, kb, :], srcs["ckv"][r0:r0 + n, :], deps, [B.b_rckv], B.b_rckv)
                    for e in range(2):
                        T.dma(pool, B.rkpe[:n, kb, e * 64:(e + 1) * 64], srcs["kpe"][r0:r0 + n, :], deps, [B.b_rkpe], B.b_rkpe)
                    T.dma(pool, B.rsbk[:n, kb, :], srcs["sbk"][r0:r0 + n, hs0 * 128:(hs0 + c.HSG) * 128], deps,
                          [B.b_rsbk], B.b_rsbk)
                    T.dma(pool, B.vsb[:n, kb, :], srcs["sbv"][r0:r0 + n, hs0 * 128:(hs0 + c.HSG) * 128], deps,
                          [B.b_vsb], B.b_vsb)
                tpb = PS[4][0][:].bitcast(BF16)
                bt4 = PS[4][1]
                def transp(src_fn, b_src, dst_ap_fn, b_dst, alt):
                    def fn():
                        ins = None
                        for kb, n in blk:
                            ins = nc.tensor.transpose(out=tpb[:, kb * 128:kb * 128 + n], in_=src_fn(kb, n),
                                                      identity=identb[:n, :n])
                        return ins
                    T.op(pe, [b_src, b_identb], [bt4], fn)
                    if alt % 2:
                        T.op(act, [bt4], [b_dst], lambda: nc.scalar.copy(out=dst_ap_fn(), in_=tpb[:, :nk]))
                    else:
                        T.op(dve, [bt4], [b_dst], lambda: nc.vector.tensor_copy(out=dst_ap_fn(), in_=tpb[:, :nk]))
                for cc in range(c.KVC):
                    transp(lambda kb, n, cc=cc: B.rckv[:n, kb, cc * 128:(cc + 1) * 128], B.b_rckv,
                           lambda cc=cc: B.ckvT[:, cc, :nk], B.b_ckvT, cc)
                transp(lambda kb, n: B.rkpe[:n, kb, :], B.b_rkpe, lambda: B.kpeT[:, :nk], B.b_kpeT, 1)
                for hh in range(c.HSG):
                    transp(lambda kb, n, hh=hh: B.rsbk[:n, kb, hh * 128:(hh + 1) * 128], B.b_rsbk,
                           lambda hh=hh: B.ksbT[:, hh, :nk], B.b_ksbT, hh)
                for hh in range(c.HMG):
                    pst, bps = PS[5]
                    mm_group(pst[:, :nk], bps, [(B.wuk[:, cc, hh * 128:(hh + 1) * 128], B.ckvT[:, cc, :nk])
                                                for cc in range(c.KVC)], [B.b_wuk, B.b_ckvT])
                    if hh % 2:
                        T.op(act, [bps], [B.b_kn], lambda: nc.scalar.copy(out=B.kn[:, hh, :nk], in_=pst[:, :nk]))
                    else:
                        T.op(dve, [bps], [B.b_kn], lambda: nc.vector.tensor_copy(out=B.kn[:, hh, :nk], in_=pst[:, :nk]))
                VW = c.HMG * 128
                for kb, n in blk:
                    for v0 in range(0, VW, 512):
                        vw = min(512, VW - v0)
                        pst, bps = PS[5]
                        mm_group(pst[:n, :vw], bps, [(B.ckvT[:, cc, kb * 128:kb * 128 + n], B.wuv[:, cc, v0:v0 + vw])
                                                     for cc in range(c.KVC)], [B.b_wuv, B.b_ckvT])
                        if kb % 2:
                            T.op(act, [bps], [B.b_vm], lambda: nc.scalar.copy(out=B.vm[:n, kb, v0:v0 + vw], in_=pst[:n, :vw]))
                        else:
                            T.op(dve, [bps], [B.b_vm], lambda: nc.vector.tensor_copy(out=B.vm[:n, kb, v0:v0 + vw],
                                                                                     in_=pst[:n, :vw]))
                for hh in range(c.HMG):
                    h = hm0 + hh
                    po = 64 * (h % 2)
                    ops_, bo = PS[2]; dps, bdn = PS[3]
                    pts = []
                    for kb, n in blk:
                        sps, bs_ = PS[kb % 2]
                        mm_group(sps[:n, :N], bs_,
                                 [(B.kn[:, hh, kb * 128:kb * 128 + n], qn[:, h, qc0:qc0 + N]),
                                  (B.kpeT[po:po + 64, kb * 128:kb * 128 + n], qp[po:po + 64, h // 2, qc0:qc0 + N])],
                                 [B.b_kn, B.b_kpeT, b_qn, b_qp])
                        pt, b_pt = B.pt[kb % 2]
                        T.op(act, [bs_], [b_pt], lambda: nc.scalar.activation(out=pt[:n, :N], in_=sps[:n, :N], func=AF.Exp,
                                                                             scale=c.mla_scale))
                        if diag:
                            T.op(dve, [b_pt, b_mmla], [b_pt], lambda: nc.vector.tensor_tensor(
                                out=pt[:n, :N], in0=pt[:n, :N], in1=mmla[:n, kb, :N], op=ALU.mult))
                        pts.append((pt, b_pt, kb, n))
                        mm_group(ops_[:, :N], bo, [(B.vm[:n, kb, hh * 128:(hh + 1) * 128], pt[:n, :N])], [B.b_vm, b_pt])
                        T.op(dve, [bo, B.b_accm], [B.b_accm], lambda: nc.vector.tensor_tensor(
                            out=B.accm[:, hh, :N], in0=ops_[:, :N], in1=B.accm[:, hh, :N], op=ALU.add))
                        mm_group(dps[:, :N], bdn, [(onesb[:n, :], pt[:n, :N])], [b_onesb, b_pt])
                        T.op(pool if False else dve, [bdn, B.b_accd], [B.b_accd], lambda: nc.vector.tensor_tensor(
                            out=B.accd[:, hh, :N], in0=dps[:, :N], in1=B.accd[:, hh, :N], op=ALU.add))
                for hh in range(c.HSG):
                    h = hs0 + hh
                    for kb, n in reversed(blk):
                        zps, bz = PS[kb % 2]
                        mm_group(zps[:n, :N], bz, [(B.ksbT[:, hh, kb * 128:kb * 128 + n], qs[:, h, qc0:qc0 + N])],
                                 [B.b_ksbT, b_qs])
                        T.op(act, [bz], [B.b_e], lambda: nc.scalar.activation(out=B.e[:n, :N], in_=zps[:n, :N], func=AF.Exp,
                                                                             scale=c.sb_scale))
                        T.op(act, [B.b_e], [B.b_sp], lambda: nc.scalar.activation(out=B.sp[:n, :N], in_=B.e[:n, :N],
                                                                                  func=AF.Ln, bias=1.0, scale=1.0))
                        if diag:
                            T.op(dve, [B.b_sp, b_msb], [B.b_sp], lambda: nc.vector.tensor_tensor(
                                out=B.sp[:n, :N], in0=B.sp[:n, :N], in1=msb[:n, kb, :N], op=ALU.mult))
                        ips, bi_ = PS[3]
                        pairs = [(ut[:n, :n], B.sp[:n, :N])]
                        rd = [b_ut, B.b_sp]
                        if not (first and kb == blk[-1][0]):
                            pairs.append((onesf[:, :n], B.spsum[:, hh, :N])); rd += [b_onesf, B.b_spsum]
                        mm_group(ips[:n, :N], bi_, pairs, rd)
                        T.op(act, [bi_], [B.b_ex], lambda: nc.scalar.activation(out=B.ex[:n, :N], in_=ips[:n, :N],
                                                                               func=AF.Exp, scale=-1.0))
                        at, b_at = B.pt[kb % 2]
                        if diag:
                            T.op(dve, [B.b_e, B.b_ex], [B.b_e], lambda: nc.vector.tensor_tensor(
                                out=B.e[:n, :N], in0=B.e[:n, :N], in1=B.ex[:n, :N], op=ALU.mult))
                            T.op(dve, [B.b_e, b_msb], [b_at], lambda: nc.vector.tensor_tensor(
                                out=at[:n, :N], in0=B.e[:n, :N], in1=msb[:n, kb, :N], op=ALU.mult))
                        else:
                            T.op(dve, [B.b_e, B.b_ex], [b_at], lambda: nc.vector.tensor_tensor(
                                out=at[:n, :N], in0=B.e[:n, :N], in1=B.ex[:n, :N], op=ALU.mult))
                        T.op(pool, [B.b_sp, B.b_spsum], [B.b_spsum], lambda: nc.gpsimd.tensor_tensor(
                            out=B.spsum[:n, hh, :N], in0=B.spsum[:n, hh, :N], in1=B.sp[:n, :N], op=ALU.add))
                        ops_, bo = PS[2]
                        mm_group(ops_[:, :N], bo, [(B.vsb[:n, kb, hh * 128:(hh + 1) * 128], at[:n, :N])], [B.b_vsb, b_at])
                        T.op(dve, [bo, B.b_accs], [B.b_accs], lambda: nc.vector.tensor_tensor(
                            out=B.accs[:, hh, :N], in0=ops_[:, :N], in1=B.accs[:, hh, :N], op=ALU.add))
                first = False
            for hh in range(c.HMG):
                T.op(dve, [B.b_accd], [B.b_accd], lambda: nc.vector.reciprocal(out=B.accd[:, hh, :N], in_=B.accd[:, hh, :N]))
                T.op(dve, [B.b_accd, B.b_accm], [B.b_accm], lambda: nc.vector.tensor_tensor(
                    out=B.accm[:, hh, :N], in0=B.accm[:, hh, :N], in1=B.accd[:, hh, :N], op=ALU.mult))
                T.op(act, [B.b_accm], [B.b_sqb], lambda: nc.scalar.activation(out=B.sqb[:, hh, :N], in_=B.accm[:, hh, :N],
                                                                              func=AF.Square))
            pst, bps = PS[7]
            mm_group(pst[:, :N], bps, [(onesb[:, :], B.sqb[:, hh, :N]) for hh in range(c.HMG)], [b_onesb, B.b_sqb])
            T.op(dve, [bps, B.b_ssq], [B.b_ssq], lambda: nc.vector.tensor_tensor(
                out=ssq_m[:, qc0:qc0 + N], in0=pst[:, :N], in1=ssq_m[:, qc0:qc0 + N], op=ALU.add))
            for hh in range(c.HSG):
                T.op(act, [B.b_accs], [B.b_sqb], lambda: nc.scalar.activation(out=B.sqb[:, hh, :N], in_=B.accs[:, hh, :N],
                                                                              func=AF.Square))
            mm_group(pst[:, :N], bps, [(onesb[:, :], B.sqb[:, hh, :N]) for hh in range(c.HSG)], [b_onesb, B.b_sqb])
            T.op(dve, [bps, B.b_ssq], [B.b_ssq], lambda: nc.vector.tensor_tensor(
                out=ssq_s[:, qc0:qc0 + N], in0=pst[:, :N], in1=ssq_s[:, qc0:qc0 + N], op=ALU.add))
            for hh in range(c.HMG):
                T.op(act, [B.b_accm], [b_hm], lambda: nc.scalar.copy(out=hm[:, hm0 + hh, qc0:qc0 + N], in_=B.accm[:, hh, :N]))
            for hh in range(c.HSG):
                T.op(act, [B.b_accs], [b_hm], lambda: nc.scalar.copy(out=hm[:, c.HM + c.PWC + hs0 + hh, qc0:qc0 + N],
                                                                    in_=B.accs[:, hh, :N]))

    class NS_:
        pass

    def attn_stage(l, p, tiles, qn, b_qn, qp, b_qp, qs, b_qs, stack):
        B = NS_()
        KGB = c.KGB
        def mk(name, shape, dt):
            t, b = sb(name, shape, dt, stack); setattr(B, name, t); setattr(B, "b_" + name, b)
        mk("wuk", [128, c.KVC, c.HMG * 128], BF16); mk("wuv", [128, c.KVC, c.HMG * 128], BF16)
        mk("rckv", [128, KGB, c.KVL], BF16); mk("rkpe", [128, KGB, 128], BF16)
        mk("rsbk", [128, KGB, c.HSG * 128], BF16); mk("vsb", [128, KGB, c.HSG * 128], BF16)
        mk("ckvT", [128, c.KVC, TP], BF16); mk("kpeT", [128, TP], BF16); mk("ksbT", [128, c.HSG, TP], BF16)
        mk("kn", [128, c.HMG, TP], BF16); mk("vm", [128, KGB, c.HMG * 128], BF16)
        mk("accm", [128, c.HMG, TP], F32); mk("accd", [128, c.HMG, TP], F32)
        mk("accs", [128, c.HSG, TP], F32); mk("spsum", [128, c.HSG, TP], F32)
        mk("e", [128, TP], F32); mk("sp", [128, TP], F32); mk("ex", [128, TP], F32)
        B.sqb, B.b_sqb = B.kn, B.b_kn
        mk("ssq", [128, 2, NW], F32)
        B.pt = [sb("pt%d" % i, [128, TP], BF16, stack) for i in range(2)]
        T.op(dve, [], [B.b_ssq], lambda: nc.vector.memset(B.ssq[:, :, :], 0.0))
        ssq = (B.ssq[:, 0, :], B.ssq[:, 1, :])
        srcs = {"ckv": o_ckv_p[l], "kpe": o_kpe_p[l], "sbk": o_sbk_p[l], "sbv": o_sbv_p[l]}
        attention(l, qn, b_qn, qp, b_qp, qs, b_qs, 0, TP, [(srcs, (p + 1) * TP, [d_kout[l]])], True, B, ssq)
        if len(tiles) > 1:
            for s_ in range(c.NS):
                past = {"ckv": c_ckv[l, s_], "kpe": c_kpe[l, s_], "sbk": c_sbk[l, s_], "sbv": c_sbv[l, s_]}
                new = {"ckv": o_ckv_s[l, s_ * DS:(s_ + 1) * DS, :], "kpe": o_kpe_s[l, s_ * DS:(s_ + 1) * DS, :],
                       "sbk": o_sbk_s[l, s_ * DS:(s_ + 1) * DS, :], "sbv": o_sbv_s[l, s_ * DS:(s_ + 1) * DS, :]}
                attention(l, qn, b_qn, qp, b_qp, qs, b_qs, TP + s_ * DS, DS,
                          [(past, c.PAST, []), (new, DS, [d_kout[l]])], True, B, ssq)
        for (which, nh, goff, ch0) in ((0, c.HM, GO_MO, 0), (1, c.HS, GO_SO, c.HM + c.PWC)):
            for (t0, w) in tiles:
                T.op(act, [B.b_ssq], [B.b_ssq], lambda: nc.scalar.activation(
                    out=B.ssq[:, which, t0:t0 + w], in_=B.ssq[:, which, t0:t0 + w], func=AF.Sqrt, bias=EPS,
                    scale=1.0 / (nh * 128)))
                T.op(dve, [B.b_ssq], [B.b_ssq], lambda: nc.vector.reciprocal(out=B.ssq[:, which, t0:t0 + w],
                                                                            in_=B.ssq[:, which, t0:t0 + w]))
                for hh in range(nh):
                    eng, q = (nc.vector, dve)
                    T.op(q, [B.b_ssq, b_gains, b_hm], [b_hm], lambda: eng.scalar_tensor_tensor(
                        out=hm[:, ch0 + hh, t0:t0 + w], in0=hm[:, ch0 + hh, t0:t0 + w],
                        scalar=gains[:, l, goff + hh:goff + hh + 1], in1=B.ssq[:, which, t0:t0 + w],
                        op0=ALU.mult, op1=ALU.mult))

    for p in range(c.NPASS):
        tiles = pass_tiles(p)
        for l in range(L):
            with ExitStack() as st:
                x, b_x = sb("x", [128, KC, NW], F32, st)
                if l == 0:
                    with ExitStack() as st2:
                        load_x(p, x, b_x, tiles, st2)
                        T.barrier()
                else:
                    for (t0, w) in tiles:
                        T.dma(sp, x[:, :, t0:t0 + w], xscr[:, :, t0:t0 + w], [d_xscr], [b_x], b_x)
                ringA = Ring("ra", 4, [128, KC, 128], st); ringD = Ring("rd", 4, [128, c.D], st)
                actb = [sb("act%d" % i, [128, NW], BF16, st) for i in range(6)]
                silb = [sb("sil%d" % i, [128, TP], F32, st) for i in range(2)]
                rstd, b_rstd = sb("rstd", [128, NW], F32, st)
                rmsnorm_feat(x, b_x, KC, tiles, GO_F1, l, hm, b_hm, rstd, b_rstd)
                ffn(l, w1g, w1u, w1d, x, b_x, tiles, ringA, ringD, actb, silb)
                rmsnorm_feat(x, b_x, KC, tiles, GO_MIX, l, hm, b_hm, rstd, b_rstd)
                for (t0, w) in tiles:
                    T.dma(sp, xscr[:, :, t0:t0 + w], x[:, :, t0:t0 + w], [b_x], [d_xscr], b_x)
                T.barrier()
            with ExitStack() as st:
                qn, b_qn = sb("qn", [128, c.HM, NW], BF16, st)
                qp, b_qp = sb("qp", [128, c.HM // 2, NW], BF16, st)
                qs, b_qs = sb("qs", [128, c.HS, NW], BF16, st)
                with ExitStack() as st2:
                    ring = Ring("rq", 4, [128, KC, 128], st2)
                    mix_q(l, p, tiles, ring, qn, b_qn, qp, b_qp, qs, b_qs, st2)
                    T.barrier()
                with ExitStack() as st2:
                    ring = Ring("rk", 4, [128, KC, 128], st2)
                    mix_k(l, p, tiles, ring, st2)
                    mix_pool(l, p, tiles, ring, st2)
                    T.barrier()
                with ExitStack() as st2:
                    attn_stage(l, p, tiles, qn, b_qn, qp, b_qp, qs, b_qs, st2)
                    T.barrier()
            with ExitStack() as st:
                x, b_x = sb("x", [128, KC, NW], F32, st)
                for (t0, w) in tiles:
                    T.dma(sp, x[:, :, t0:t0 + w], xscr[:, :, t0:t0 + w], [d_xscr], [b_x], b_x)
                with ExitStack() as st2:
                    ring = Ring("rw", 4, [128, c.MIXC, 128], st2)
                    for m in range(KC):
                        def ev(pa, bps, t0, w, m=m):
                            T.op(dve, [bps, b_x], [b_x], lambda: nc.vector.tensor_tensor(
                                out=x[:, m, t0:t0 + w], in0=pa, in1=x[:, m, t0:t0 + w], op=ALU.add))
                        proj_feat(ring, w_out[l], c.MIXC, m * 128, 128, hm, b_hm, tiles, m % 2, ev, kind="w_out")
                    T.barrier()
                with ExitStack() as st2:
                    ringA = Ring("ra", 4, [128, KC, 128], st2); ringD = Ring("rd", 4, [128, c.D], st2)
                    actb = [sb("act%d" % i, [128, NW], BF16, st2) for i in range(6)]
                    silb = [sb("sil%d" % i, [128, TP], F32, st2) for i in range(2)]
                    rstd, b_rstd = sb("rstd", [128, NW], F32, st2)
                    rmsnorm_feat(x, b_x, KC, tiles, GO_F2, l, hm, b_hm, rstd, b_rstd)
                    ffn(l, w2g, w2u, w2d, x, b_x, tiles, ringA, ringD, actb, silb)
                    if l == L - 1:
                        rmsnorm_feat(x, b_x, KC, tiles, 0, l, x, b_x, rstd, b_rstd, gt=gfin, sq=hm, b_sq=b_hm)
                    T.barrier()
                if l == L - 1:
                    with ExitStack() as st2:
                        store_y(p, x, b_x, tiles, st2)
                        T.barrier()
                else:
                    for (t0, w) in tiles:
                        T.dma(sp, xscr[:, :, t0:t0 + w], x[:, :, t0:t0 + w], [b_x], [d_xscr], b_x)
                    T.barrier()
    T.barrier()
    es.close()
    return nc, T


def host_consts(c: Cfg):
    k = {}
    k["k_ident"] = np.eye(128, dtype=np.float32)
    j = np.arange(128)
    k["k_ut"] = (j[:, None] >= j[None, :]).astype(np.float32)
    r = np.zeros((64, 64), np.float32)
    for d in range(32):
        r[d, d + 32] = -1.0; r[d + 32, d] = 1.0
    rt = r.T.copy()
    R128 = np.zeros((128, 128), np.float32); R128[:64, :64] = rt; R128[64:, 64:] = rt
    k["k_r"] = R128
    kk = np.arange(128)[:, None]; q = np.arange(c.TP)[None, :]
    mm = np.zeros((c.KGB, 128, c.TP), np.float32); ms = np.zeros((c.KGB, 128, c.TP), np.float32)
    for kb in range(c.KGB):
        key = kb * 128 + kk
        mm[kb] = ((key // 64) <= (q // 64)).astype(np.float32)
        ms[kb] = (key < q).astype(np.float32)
    k["k_mmla"] = mm; k["k_msb"] = ms
    half = 32
    inv = (1.0 / (np.float32(ROPE_THETA) ** (np.arange(half, dtype=np.float32) / np.float32(half)))).astype(np.float32)
    pos = np.concatenate([np.arange(c.SEQ), c.PAST + np.tile(np.arange(c.DS), c.NS)]).astype(np.float32)
    ang = (pos[:, None] * inv[None, :]).astype(np.float32)
    cos = np.cos(ang).astype(np.float32); sin = np.sin(ang).astype(np.float32)
    k["k_cosk"] = cos; k["k_sink"] = sin
    cq = np.concatenate([cos, cos], axis=1).T
    sq = np.concatenate([sin, sin], axis=1).T
    k["k_cosq"] = np.ascontiguousarray(np.concatenate([cq, cq], axis=0), dtype=np.float32)
    k["k_sinq"] = np.ascontiguousarray(np.concatenate([sq, sq], axis=0), dtype=np.float32)
    ic = np.zeros((128, 4, 16), np.float32)
    for g, w in enumerate((2, 4, 8, 16)):
        ic[:, g, :] = 1.0 / np.minimum(np.arange(16) + 1, w).astype(np.float32)
    k["k_invc"] = ic
    return k


_CACHE = {}


def run(c: Cfg, inputs, n_cores=8, runner=None):
    key = (c.D, c.DFF, c.SEQ, c.TP, c.PAST)
    if key not in _CACHE:
        _CACHE[key] = build_program(c)
    nc, T = _CACHE[key]
    f = lambda a: np.ascontiguousarray(np.asarray(a, dtype=np.float32))
    I = {k: f(v) for k, v in inputs.items()}
    L = c.L
    CHL = chunk_lists(c)
    pk = lambda name, kind: pack_slabs(I[name], CHL[kind])
    shared = {
        "g_ffn1": I["g_ffn1"], "w1_gate": pk("w1_gate", "ffn"), "w1_up": pk("w1_up", "ffn"), "w1_down": I["w1_down"],
        "g_mix": I["g_mix"], "w_in": pk("w_in", "w_in"), "g_qnorm": I["g_qnorm"], "w_uq": pk("w_uq", "w_uq"),
        "g_kvnorm": I["g_kvnorm"],
        "w_uk": I["w_uk"].reshape(L, c.KVL, c.HM * 128), "w_uv": I["w_uv"].reshape(L, c.KVL, c.HM * 128),
        "w_pool": I["w_pool"], "pool_scale": I["pool_scale"], "g_mla_out": I["g_mla_out"], "g_sb_out": I["g_sb_out"],
        "w_out": pk("w_out", "w_out"), "g_ffn2": I["g_ffn2"], "w2_gate": pk("w2_gate", "ffn"), "w2_up": pk("w2_up", "ffn"),
        "w2_down": I["w2_down"],
        "g_final": I["g_final"],
    }
    shared.update(host_consts(c))
    nb = I["x_prompt"].shape[0]
    in_maps = []
    for core in range(n_cores):
        s0 = core * c.NS
        m = dict(shared)
        m["xp"] = I["x_prompt"][core % nb]
        m["xs"] = f(I["x_sample"][s0:s0 + c.NS].reshape(c.NST, c.D))
        m["c_ckv"] = f(I["cache_ckv"][:, s0:s0 + c.NS]); m["c_kpe"] = f(I["cache_kpe"][:, s0:s0 + c.NS])
        m["c_pool"] = f(I["state_pool"][:, s0:s0 + c.NS])
        m["c_sbk"] = f(I["cache_sb_k"][:, s0:s0 + c.NS].reshape(L, c.NS, c.PAST, c.SBW))
        m["c_sbv"] = f(I["cache_sb_v"][:, s0:s0 + c.NS].reshape(L, c.NS, c.PAST, c.SBW))
        in_maps.append(m)
    if runner is None:
        res = run_bass_kernel_spmd(nc, in_maps, core_ids=list(range(n_cores))).results
    else:
        res = runner(nc, in_maps)
    R = res
    st = lambda name, cores: np.stack([R[i][name] for i in cores], axis=0)
    pc = list(range(nb)); ac = list(range(n_cores))
    nsq = n_cores * c.NS
    y_prompt = st("y_p", pc)
    y_sample = st("y_s", ac).reshape(nsq, c.DS, c.D)
    ckv_p = st("o_ckv_p", pc).transpose(1, 0, 2, 3)
    kpe_p = st("o_kpe_p", pc).transpose(1, 0, 2, 3)
    pool_p = st("o_pool_p", pc).transpose(1, 0, 2, 3)
    sbk_p = st("o_sbk_p", pc).transpose(1, 0, 2, 3).reshape(L, nb, c.SEQ, c.HS, 128)
    sbv_p = st("o_sbv_p", pc).transpose(1, 0, 2, 3).reshape(L, nb, c.SEQ, c.HS, 128)
    def samp(name, rows, width):
        a = st(name, ac)
        a = a.reshape(n_cores, L, c.NS, rows, width).transpose(1, 0, 2, 3, 4).reshape(L, nsq, rows, width)
        return a
    ckv_s = samp("o_ckv_s", c.DS, c.KVL); kpe_s = samp("o_kpe_s", c.DS, 64)
    pool_s = samp("o_pool_s", 15, c.PW)
    sbk_s = samp("o_sbk_s", c.DS, c.SBW).reshape(L, nsq, c.DS, c.HS, 128)
    sbv_s = samp("o_sbv_s", c.DS, c.SBW).reshape(L, nsq, c.DS, c.HS, 128)
    outs = (y_prompt, y_sample, ckv_p, kpe_p, pool_p, sbk_p, sbv_p, ckv_s, kpe_s, pool_s, sbk_s, sbv_s)
    return tuple(np.ascontiguousarray(o, dtype=np.float32) for o in outs)


def kernel(**inputs):
    return run(Cfg(), inputs)
```

```python
import math
from contextlib import ExitStack
import numpy as np
import ml_dtypes
import concourse.bass as bass
import concourse.mybir as mybir
from concourse.bass_utils import run_bass_kernel_spmd

F32 = mybir.dt.float32
BF16 = mybir.dt.bfloat16
AF = mybir.ActivationFunctionType
ALU = mybir.AluOpType
EPS = 1e-6
ROPE_THETA = 10000.0


class Cfg:
    def __init__(s, D=4096, DFF=11008, SEQ=2048, TP=512, PAST=4096, QL=1024, KVL=512, HM=16, HS=8,
                 PG=256, L=2, NS=2, DS=16):
        s.D, s.DFF, s.SEQ, s.TP, s.PAST, s.QL, s.KVL, s.HM, s.HS, s.PG, s.L, s.NS, s.DS = \
            D, DFF, SEQ, TP, PAST, QL, KVL, HM, HS, PG, L, NS, DS
        s.KC = D // 128; s.FC = DFF // 128; s.NPASS = SEQ // TP
        s.QLC = QL // 128; s.KVC = KVL // 128
        s.PW = 4 * PG; s.PGC = PG // 128; s.PWC = s.PW // 128
        s.MLAW = HM * 128; s.SBW = HS * 128
        s.MIX = s.MLAW + s.PW + s.SBW; s.MIXC = s.MIX // 128
        s.oq = 0; s.okv = QL; s.ope = QL + KVL; s.ou = s.ope + 64
        s.oqs = s.ou + s.PW; s.oks = s.oqs + s.SBW; s.ovs = s.oks + s.SBW
        s.INW = s.ovs + s.SBW
        s.NST = NS * DS
        s.NW = TP + s.NST
        s.KGB = TP // 128
        s.NG = 2
        s.HMG = HM // 2; s.HSG = HS // 2
        s.mla_scale = 1.0 / math.sqrt(192.0); s.sb_scale = 1.0 / math.sqrt(128.0)


def chunk_lists(c):
    ch = {}
    ch["ffn"] = [[(j * 128, 128, 0)] for j in range(c.FC)]
    win = []
    for j in range(c.QLC): win.append([(c.oq + j * 128, 128, 0)])
    for j in range(c.KVC): win.append([(c.okv + j * 128, 128, 0)])
    win.append([(c.ope, 64, 0)])
    for off, n in ((c.ou, c.PWC), (c.oqs, c.HS), (c.oks, c.HS), (c.ovs, c.HS)):
        for j in range(n): win.append([(off + j * 128, 128, 0)])
    ch["w_in"] = win
    ch["w_out"] = [[(j * 128, 128, 0)] for j in range(c.KC)]
    wuq = [[(h * 192, 128, 0)] for h in range(c.HM)]
    for hp in range(c.HM // 2):
        wuq.append([((2 * hp) * 192 + 128, 64, 0), ((2 * hp + 1) * 192 + 128, 64, 64)])
    ch["w_uq"] = wuq
    return ch


def pack_slabs(W, chunks):
    L_, K_, F_ = W.shape
    kc = K_ // 128
    aligned = all(len(p) == 1 and p[0][1] == 128 and p[0][2] == 0 for p in chunks) and \
        [p[0][0] for p in chunks] == [i * 128 for i in range(len(chunks))] and len(chunks) * 128 == F_
    if aligned:
        return np.ascontiguousarray(W.reshape(L_, kc, 128, len(chunks), 128).transpose(0, 3, 2, 1, 4))
    Wv = W.reshape(L_, kc, 128, F_)
    out = np.zeros((L_, len(chunks), 128, kc, 128), np.float32)
    for i, pieces in enumerate(chunks):
        for (c0, m, d0) in pieces:
            out[:, i, :, :, d0:d0 + m] = Wv[:, :, :, c0:c0 + m].transpose(0, 2, 1, 3)
    return out


class Q:
    def __init__(s, nc, eng, name):
        s.eng = eng; s.sem = nc.alloc_semaphore(name); s.n = 0; s.waited = {}


class Buf:
    def __init__(s, name):
        s.name = name; s.w = {}; s.r = {}; s.ds = {}


class DSem:
    def __init__(s, sem):
        s.sem = sem; s.n = 0


class Tr:
    def __init__(s, nc):
        s.nc = nc
        s.pe = Q(nc, nc.tensor, "q_pe"); s.act = Q(nc, nc.scalar, "q_act")
        s.dve = Q(nc, nc.vector, "q_dve"); s.pool = Q(nc, nc.gpsimd, "q_pool")
        s.sp = Q(nc, nc.sync, "q_sp")
        s.queues = [s.pe, s.act, s.dve, s.pool, s.sp]
        s.dbufs = []
        s.dpool = {}
        s.dall = []
        s.ninst = 0

    def _wait(s, q, ev):
        if ev[0] == 'e':
            sem, val, key = ev[1].sem, ev[2], id(ev[1])
        else:
            sem, val, key = ev[1].sem, 16 * ev[1].n, id(ev[1])
        if q.waited.get(key, 0) >= val:
            return
        q.eng.wait_ge(sem, val)
        q.waited[key] = val

    def _deps(s, q, reads, writes):
        for b in reads:
            for ev in b.w.values():
                s._wait(q, ev)
        for b in writes:
            for ev in b.w.values():
                s._wait(q, ev)
            for ev in b.r.values():
                s._wait(q, ev)

    def _reg(s, ev, key, reads, writes):
        for b in reads:
            b.r[key] = ev
        for b in writes:
            b.w = {key: ev}; b.r = {}

    def op(s, q, reads, writes, fn):
        s._deps(q, reads, writes)
        ins = fn()
        q.n += 1
        ins.then_inc(q.sem, 1)
        s._reg(('e', q, q.n), id(q), reads, writes)
        s.ninst += 1
        return ins

    def dma(s, q, out, in_, reads, writes, sb):
        qk = id(q)
        if qk not in sb.ds:
            pl = s.dpool.setdefault(qk, [])
            if pl:
                sb.ds[qk] = pl.pop()
            else:
                sb.ds[qk] = DSem(s.nc.alloc_semaphore("dsem%d" % len(s.dall))); s.dall.append(sb.ds[qk])
            s.dbufs.append((sb, qk))
        d = sb.ds[qk]
        s._deps(q, reads, writes)
        ins = q.eng.dma_start(out=out, in_=in_)
        d.n += 1
        ins.then_inc(d.sem, 16)
        s._reg(('d', d), id(d), reads, writes)
        s.ninst += 1

    def barrier(s):
        for q in s.queues:
            for q2 in s.queues:
                if q2 is not q and q2.n > 0:
                    s._wait(q, ('e', q2, q2.n))
            for d in s.dall:
                if d.n > 0:
                    s._wait(q, ('d', d))
        for (b, qk) in s.dbufs:
            s.dpool[qk].append(b.ds.pop(qk))
        s.dbufs = []


def build_program(c: Cfg):
    nc = bass.Bass("TRN2", target_bir_lowering=False)
    T = Tr(nc)
    pe, act, dve, pool, sp = T.pe, T.act, T.dve, T.pool, T.sp
    L, KC, FC, TP, NW, NST, DS = c.L, c.KC, c.FC, c.TP, c.NW, c.NST, c.DS

    def din(name, shape):
        return nc.dram_tensor(name, list(shape), F32, kind="ExternalInput").ap()

    def dout(name, shape):
        return nc.dram_tensor(name, list(shape), F32, kind="ExternalOutput").ap()

    xp = din("xp", [c.SEQ, c.D]); xs = din("xs", [NST, c.D])
    c_ckv = din("c_ckv", [L, c.NS, c.PAST, c.KVL]); c_kpe = din("c_kpe", [L, c.NS, c.PAST, 64])
    c_pool = din("c_pool", [L, c.NS, 15, c.PW])
    c_sbk = din("c_sbk", [L, c.NS, c.PAST, c.SBW]); c_sbv = din("c_sbv", [L, c.NS, c.PAST, c.SBW])
    g_ffn1 = din("g_ffn1", [L, c.D]); w1g = din("w1_gate", [L, FC, 128, KC, 128]); w1u = din("w1_up", [L, FC, 128, KC, 128])
    w1d = din("w1_down", [L, c.DFF, c.D]); g_mix = din("g_mix", [L, c.D]); CH = chunk_lists(c)
    w_in = din("w_in", [L, len(CH["w_in"]), 128, KC, 128])
    g_qn = din("g_qnorm", [L, c.QL]); w_uq = din("w_uq", [L, len(CH["w_uq"]), 128, c.QLC, 128])
    g_kvn = din("g_kvnorm", [L, c.KVL]); w_uk = din("w_uk", [L, c.KVL, c.HM * 128])
    w_uv = din("w_uv", [L, c.KVL, c.HM * 128]); w_pool = din("w_pool", [L, 4, c.PG, c.PG])
    pscale = din("pool_scale", [L, c.PW]); g_mo = din("g_mla_out", [L, c.MLAW]); g_so = din("g_sb_out", [L, c.SBW])
    w_out = din("w_out", [L, KC, 128, c.MIXC, 128]); g_ffn2 = din("g_ffn2", [L, c.D])
    w2g = din("w2_gate", [L, FC, 128, KC, 128]); w2u = din("w2_up", [L, FC, 128, KC, 128]); w2d = din("w2_down", [L, c.DFF, c.D])
    g_fin = din("g_final", [c.D])
    k_ident = din("k_ident", [128, 128]); k_ut = din("k_ut", [128, 128]); k_r = din("k_r", [128, 128])
    k_mmla = din("k_mmla", [c.KGB, 128, TP]); k_msb = din("k_msb", [c.KGB, 128, TP])
    k_cosq = din("k_cosq", [128, c.SEQ + NST]); k_sinq = din("k_sinq", [128, c.SEQ + NST])
    k_cosk = din("k_cosk", [c.SEQ + NST, 32]); k_sink = din("k_sink", [c.SEQ + NST, 32])
    k_invc = din("k_invc", [128, 4, 16])

    y_p = dout("y_p", [c.SEQ, c.D]); y_s = dout("y_s", [NST, c.D])
    o_ckv_p = dout("o_ckv_p", [L, c.SEQ, c.KVL]); o_kpe_p = dout("o_kpe_p", [L, c.SEQ, 64])
    o_pool_p = dout("o_pool_p", [L, 15, c.PW]); o_sbk_p = dout("o_sbk_p", [L, c.SEQ, c.SBW])
    o_sbv_p = dout("o_sbv_p", [L, c.SEQ, c.SBW])
    o_ckv_s = dout("o_ckv_s", [L, NST, c.KVL]); o_kpe_s = dout("o_kpe_s", [L, NST, 64])
    o_pool_s = dout("o_pool_s", [L, c.NS * 15, c.PW]); o_sbk_s = dout("o_sbk_s", [L, NST, c.SBW])
    o_sbv_s = dout("o_sbv_s", [L, NST, c.SBW])
    xscr = nc.dram_tensor("xscr", [128, KC, NW], F32).ap()
    d_xscr = Buf("xscr")
    d_kout = [Buf("kout%d" % l) for l in range(L)]
    d_yout = Buf("yout")

    es = ExitStack()

    uid = [0]

    def sb(name, shape, dt, stack=None):
        uid[0] += 1
        nm = "%s_%d" % (name, uid[0])
        t = (stack or es).enter_context(nc.sbuf_tensor(nm, list(shape), dt))
        return t, Buf(nm)

    PS = []
    for i in range(8):
        t = es.enter_context(nc.psum_tensor("ps%d" % i, [128, 512], F32))
        PS.append((t, Buf("ps%d" % i)))

    ident, b_ident = sb("ident", [128, 128], F32)
    identb, b_identb = sb("identb", [128, 128], BF16)
    ut, b_ut = sb("ut", [128, 128], F32)
    rr, b_rr = sb("rr", [128, 128], F32)
    onesf, b_onesf = sb("onesf", [128, 128], F32)
    onesb, b_onesb = sb("onesb", [128, 128], BF16)
    mmla, b_mmla = sb("mmla", [128, c.KGB, TP], BF16)
    msb, b_msb = sb("msb", [128, c.KGB, TP], BF16)
    invc, b_invc = sb("invc", [128, 4, 16], F32)
    gains, b_gains = sb("gains", [128, L, 3 * KC + c.QLC + c.HM + c.HS + c.PWC], F32)
    gfin, b_gfin = sb("gfin", [128, KC], F32)
    gkv, b_gkv = sb("gkv", [128, L, c.KVL], F32)
    uh, b_uh = sb("uh", [128, L, c.PWC, 15], F32)
    hm, b_hm = sb("hm", [128, max(KC, c.MIXC), NW], BF16)

    GO_F1, GO_MIX, GO_F2 = 0, KC, 2 * KC
    GO_QN = 3 * KC; GO_MO = GO_QN + c.QLC; GO_SO = GO_MO + c.HM; GO_PS = GO_SO + c.HS

    def load_consts():
        T.dma(sp, ident[:], k_ident[:, :], [], [b_ident], b_ident)
        T.dma(sp, ut[:], k_ut[:, :], [], [b_ut], b_ut)
        T.dma(sp, rr[:], k_r[:, :], [], [b_rr], b_rr)
        T.op(dve, [b_ident], [b_identb], lambda: nc.vector.tensor_copy(out=identb[:], in_=ident[:]))
        T.op(dve, [], [b_onesf], lambda: nc.vector.memset(onesf[:], 1.0))
        T.op(dve, [], [b_onesb], lambda: nc.vector.memset(onesb[:], 1.0))
        T.op(dve, [], [b_uh], lambda: nc.vector.memset(uh[:], 0.0))
        for kb in range(c.KGB):
            T.dma(sp, cst[:, 0, :], k_mmla[kb], [], [b_cst], b_cst)
            T.op(dve, [b_cst], [b_mmla], lambda: nc.vector.tensor_copy(out=mmla[:, kb, :], in_=cst[:, 0, :]))
            T.dma(sp, cst[:, 1, :], k_msb[kb], [], [b_cst], b_cst)
            T.op(dve, [b_cst], [b_msb], lambda: nc.vector.tensor_copy(out=msb[:, kb, :], in_=cst[:, 1, :]))
        T.dma(sp, invc[:], k_invc[:, :, :], [], [b_invc], b_invc)
        T.dma(sp, gfin[:], g_fin.rearrange("(k p) -> p k", p=128), [], [b_gfin], b_gfin)
        for l in range(L):
            for (src, off, n) in ((g_ffn1, GO_F1, KC), (g_mix, GO_MIX, KC), (g_ffn2, GO_F2, KC),
                                  (g_qn, GO_QN, c.QLC), (g_mo, GO_MO, c.HM), (g_so, GO_SO, c.HS),
                                  (pscale, GO_PS, c.PWC)):
                T.dma(sp, gains[:, l, off:off + n], src[l].rearrange("(k p) -> p k", p=128), [], [b_gains], b_gains)
            T.dma(sp, gkv[:, l, :], g_kvn[l].partition_broadcast(128), [], [b_gkv], b_gkv)

    with ExitStack() as st0:
        cst, b_cst = sb("cst", [128, 2, TP], F32, st0)
        with nc.allow_non_contiguous_dma(reason="tiny gain-vector loads"):
            load_consts()
        T.barrier()

    def pass_tiles(p):
        tl = [(0, TP)]
        if p == c.NPASS - 1:
            tl.append((TP, NST))
        return tl

    class Ring:
        def __init__(s, name, n, shape, stack):
            s.slots = [sb("%s%d" % (name, i), shape, BF16, stack) for i in range(n)]
            s.i = 0

        def next(s):
            t = s.slots[s.i % len(s.slots)]; s.i += 1
            return t

    CHIDX = {nm: {tuple(pc): i for i, pc in enumerate(lst)} for nm, lst in CH.items()}

    def load_slab(ring, Wl, kchunks, c0, m, kind="ffn"):
        t, b = ring.next()
        idx = CHIDX[kind][((c0, m, 0),)]
        T.dma(pool, t[:, :kchunks, :], Wl[idx], [], [b], b)
        return t, b

    def load_rows(ring, W2d, r0, ncols):
        t, b = ring.next()
        T.dma(pool, t[:, :ncols], W2d[r0:r0 + 128, 0:ncols], [], [b], b)
        return t, b

    small_slot = [0]

    def ps_small(w):
        i = small_slot[0] % 8; small_slot[0] += 1
        return PS[6][0][:, i * 64:i * 64 + w], PS[6][1]

    def mm_group(out_ap, b_out, pairs, reads):
        def fn():
            n = len(pairs); ins = None
            for i, (l_, r_) in enumerate(pairs):
                ins = nc.tensor.matmul(out_ap, lhsT=l_, rhs=r_, start=(i == 0), stop=(i == n - 1))
            return ins
        return T.op(pe, reads, [b_out], fn)

    def rstd_from_ps(ps_ap, b_ps, out_ap, b_o, dim, eng_alt=0):
        T.op(act, [b_ps], [b_o], lambda: nc.scalar.activation(out=out_ap, in_=ps_ap, func=AF.Sqrt,
                                                               bias=EPS, scale=1.0 / dim))
        T.op(dve, [b_o], [b_o], lambda: nc.vector.reciprocal(out=out_ap, in_=out_ap))

    def rmsnorm_feat(src, b_src, nch, tiles, goff, l, dst, b_dst, rstd, b_rstd, gt=None, sq=None, b_sq=None):
        gt = gains if gt is None else gt
        sq = dst if sq is None else sq; b_sq = b_dst if b_sq is None else b_sq
        for ti, (c0, w) in enumerate(tiles):
            for k in range(nch):
                T.op(act, [b_src], [b_sq], lambda: nc.scalar.activation(out=sq[:, k, c0:c0 + w], in_=src[:, k, c0:c0 + w],
                                                                      func=AF.Square))
            pst, bps = PS[7]
            mm_group(pst[:, :w], bps, [(onesb[:, :], sq[:, k, c0:c0 + w]) for k in range(nch)], [b_sq, b_onesb])
            rstd_from_ps(pst[:, :w], bps, rstd[:, c0:c0 + w], b_rstd, nch * 128)
            for k in range(nch):
                g_ap = gt[:, k:k + 1] if gt is gfin else gt[:, l, goff + k:goff + k + 1]
                eng, q = (nc.vector, dve)
                T.op(q, [b_src, b_rstd, b_gains, b_gfin], [b_dst],
                     lambda: eng.scalar_tensor_tensor(out=dst[:, k, c0:c0 + w], in0=src[:, k, c0:c0 + w],
                                                      scalar=g_ap, in1=rstd[:, c0:c0 + w],
                                                      op0=ALU.mult, op1=ALU.mult))

    def ffn(l, Wg, Wu, Wd, x, b_x, tiles, ringA, ringD, actb, silb):
        G = 3
        groups = [list(range(g0, min(g0 + G, FC))) for g0 in range(0, FC, G)]
        gi = 0
        for grp in groups:
            acts = []
            dslabs = []
            for j in grp:
                sg, bsg = load_slab(ringA, Wg[l], KC, j * 128, 128)
                su, bsu = load_slab(ringA, Wu[l], KC, j * 128, 128)
                at, b_at = actb[gi % len(actb)]; gi += 1
                acts.append((at, b_at))
                for ti, (c0, w) in enumerate(tiles):
                    if ti == 0:
                        gps, bg = PS[j % 2]; ups, bu = PS[2 + j % 2]
                        gp, up = gps[:, :w], ups[:, :w]
                    else:
                        gp, bg = ps_small(w); up, bu = ps_small(w)
                    mm_group(gp, bg, [(sg[:, k, :], hm[:, k, c0:c0 + w]) for k in range(KC)], [bsg, b_hm])
                    mm_group(up, bu, [(su[:, k, :], hm[:, k, c0:c0 + w]) for k in range(KC)], [bsu, b_hm])
                    st, b_st = silb[(j + ti) % len(silb)]
                    T.op(act, [bg], [b_st], lambda: nc.scalar.activation(out=st[:, :w], in_=gp, func=AF.Silu))
                    T.op(dve, [b_st, bu], [b_at], lambda: nc.vector.tensor_tensor(out=at[:, c0:c0 + w], in0=st[:, :w],
                                                                                 in1=up, op=ALU.mult))
            for j in grp:
                dslabs.append(load_rows(ringD, Wd[l], j * 128, c.D))
            for m in range(KC):
                for ti, (c0, w) in enumerate(tiles):
                    if ti == 0:
                        dps, bd = PS[4 + m % 2]; dp = dps[:, :w]
                    else:
                        dp, bd = ps_small(w)
                    mm_group(dp, bd, [(dslabs[i][0][:, m * 128:(m + 1) * 128], acts[i][0][:, c0:c0 + w])
                                      for i in range(len(grp))],
                             [dslabs[i][1] for i in range(len(grp))] + [a[1] for a in acts])
                    T.op(dve, [bd, b_x], [b_x], lambda: nc.vector.scalar_tensor_tensor(
                        out=x[:, m, c0:c0 + w], in0=dp, scalar=0.5, in1=x[:, m, c0:c0 + w],
                        op0=ALU.mult, op1=ALU.add))

    def load_x(p, x, b_x, tiles, stack):
        xin, b_xin = sb("xin", [128, c.D], F32, stack)
        for (c0, w) in tiles:
            for tb in range((w + 127) // 128):
                n = min(128, w - tb * 128)
                src = xp[p * TP + tb * 128: p * TP + tb * 128 + n, :] if c0 == 0 else xs[0:n, :]
                T.dma(sp, xin[:n, :], src, [], [b_xin], b_xin)
                for k0 in range(0, KC, 4):
                    kk = min(4, KC - k0)
                    pst, bps = PS[7]
                    def fn():
                        ins = None
                        for j in range(kk):
                            ins = nc.tensor.transpose(out=pst[:, j * 128:j * 128 + n],
                                                      in_=xin[:n, (k0 + j) * 128:(k0 + j + 1) * 128],
                                                      identity=ident[:n, :n])
                        return ins
                    T.op(pe, [b_xin, b_ident], [bps], fn)
                    T.op(act, [bps], [b_x], lambda: nc.scalar.copy(
                        out=x[:, k0:k0 + kk, c0 + tb * 128:c0 + tb * 128 + n],
                        in_=pst[:, :kk * 128].rearrange("p (j t) -> p j t", t=128)[:, :, :n]))

    def store_y(p, x, b_x, tiles, stack):
        yo, b_yo = sb("yo", [128, 2, 512], F32, stack)
        cnt = 0
        for (c0, w) in tiles:
            for tb in range((w + 127) // 128):
                n = min(128, w - tb * 128)
                for k0 in range(0, KC, 4):
                    kk = min(4, KC - k0)
                    pst, bps = PS[7]
                    def fn():
                        ins = None
                        for j in range(kk):
                            ins = nc.tensor.transpose(out=pst[:n, j * 128:(j + 1) * 128],
                                                      in_=x[:, k0 + j, c0 + tb * 128:c0 + tb * 128 + n],
                                                      identity=ident[:, :])
                        return ins
                    T.op(pe, [b_x, b_ident], [bps], fn)
                    s_ = cnt % 2; cnt += 1
                    T.op(act, [bps], [b_yo], lambda: nc.scalar.copy(out=yo[:n, s_, :kk * 128], in_=pst[:n, :kk * 128]))
                    dst = (y_p[p * TP + tb * 128: p * TP + tb * 128 + n, k0 * 128:(k0 + kk) * 128] if c0 == 0
                           else y_s[0:n, k0 * 128:(k0 + kk) * 128])
                    T.dma(sp, dst, yo[:n, s_, :kk * 128], [b_yo], [d_yout], b_yo)

    def proj_feat(ring, W2d, kch, c0, m, src, b_src, tiles, big_bank, evac, kind="w_in"):
        s_, bs = load_slab(ring, W2d, kch, c0, m, kind)
        for ti, (t0, w) in enumerate(tiles):
            if ti == 0:
                pst, bps = PS[big_bank]; pa = pst[:m, :w]
            else:
                pa, bps = ps_small(w); pa = pa[:m, :]
            mm_group(pa, bps, [(s_[:, k, :m], src[:, k, t0:t0 + w]) for k in range(kch)], [bs, b_src])
            evac(pa, bps, t0, w)

    def copy_evac(dst_fn, b_dst, alt=[0]):
        def ev(pa, bps, t0, w):
            alt[0] += 1
            if alt[0] % 2:
                T.op(act, [bps], [b_dst], lambda: nc.scalar.copy(out=dst_fn(t0, w), in_=pa))
            else:
                T.op(dve, [bps], [b_dst], lambda: nc.vector.tensor_copy(out=dst_fn(t0, w), in_=pa))
        return ev

    def mix_q(l, p, tiles, ring, qn, b_qn, qp, b_qp, qs, b_qs, stack):
        cq, b_cq = sb("cq", [128, c.QLC, NW], F32, stack)
        cqn, b_cqn = sb("cqn", [128, c.QLC, NW], BF16, stack)
        rs, b_rs = sb("rs_q", [128, NW], F32, stack)
        tq, b_tq = sb("tq", [128, 2, TP], F32, stack)
        cs, b_cs = sb("cs_q", [128, 2, NW], F32, stack)
        W = w_in[l]
        for j in range(c.QLC):
            proj_feat(ring, W, KC, c.oq + j * 128, 128, hm, b_hm, tiles, j % 2,
                      copy_evac(lambda t0, w, j=j: cq[:, j, t0:t0 + w], b_cq))
        rmsnorm_feat(cq, b_cq, c.QLC, tiles, GO_QN, l, cqn, b_cqn, rs, b_rs)
        for h in range(c.HM):
            proj_feat(ring, w_uq[l], c.QLC, h * 192, 128, cqn, b_cqn, tiles, h % 2,
                      copy_evac(lambda t0, w, h=h: qn[:, h, t0:t0 + w], b_qn), kind="w_uq")
        T.dma(sp, cs[:, 0, :TP], k_cosq[:, p * TP:(p + 1) * TP], [], [b_cs], b_cs)
        T.dma(sp, cs[:, 1, :TP], k_sinq[:, p * TP:(p + 1) * TP], [], [b_cs], b_cs)
        if len(tiles) > 1:
            T.dma(sp, cs[:, 0, TP:NW], k_cosq[:, c.SEQ:c.SEQ + NST], [], [b_cs], b_cs)
            T.dma(sp, cs[:, 1, TP:NW], k_sinq[:, c.SEQ:c.SEQ + NST], [], [b_cs], b_cs)
        for hp in range(c.HM // 2):
            t_, bt = ring.next()
            T.dma(pool, t_[:, :c.QLC, :], w_uq[l][c.HM + hp], [], [bt], bt)
            for ti, (t0, w) in enumerate(tiles):
                if ti == 0:
                    pst, bps = PS[hp % 2]; pa = pst[:, :w]
                else:
                    pa, bps = ps_small(w)
                mm_group(pa, bps, [(t_[:, k, :], cqn[:, k, t0:t0 + w]) for k in range(c.QLC)], [bt, b_cqn])
                T.op(act, [bps], [b_tq], lambda: nc.scalar.copy(out=tq[:, 0, :w], in_=pa))
                if ti == 0:
                    ps2, bp2 = PS[2 + hp % 2]; pb = ps2[:, :w]
                else:
                    pb, bp2 = ps_small(w)
                mm_group(pb, bp2, [(rr[:, :], tq[:, 0, :w])], [b_rr, b_tq])
                T.op(dve, [b_tq, b_cs], [b_tq], lambda: nc.vector.tensor_tensor(out=tq[:, 0, :w], in0=tq[:, 0, :w],
                                                                              in1=cs[:, 0, t0:t0 + w], op=ALU.mult))
                T.op(dve, [bp2, b_cs], [b_tq], lambda: nc.vector.tensor_tensor(out=tq[:, 1, :w], in0=pb,
                                                                              in1=cs[:, 1, t0:t0 + w], op=ALU.mult))
                T.op(dve, [b_tq], [b_qp], lambda: nc.vector.tensor_tensor(out=qp[:, hp, t0:t0 + w], in0=tq[:, 0, :w],
                                                                          in1=tq[:, 1, :w], op=ALU.add))
        for h in range(c.HS):
            proj_feat(ring, W, KC, c.oqs + h * 128, 128, hm, b_hm, tiles, h % 2,
                      copy_evac(lambda t0, w, h=h: qs[:, h, t0:t0 + w], b_qs))

    def tok_blocks(tiles):
        bl = []
        for (t0, w) in tiles:
            for tb in range((w + 127) // 128):
                bl.append((t0 + tb * 128, min(128, w - tb * 128), t0 != 0))
        return bl

    def mix_k(l, p, tiles, ring, stack):
        NB = c.KGB + 1
        stg, b_stg = sb("stg", [128, NB, 512], F32, stack)
        sm, b_sm = sb("sm_k", [128, NB, 8], F32, stack)
        ck, b_ck = sb("ck", [128, NB, 2, 32], F32, stack)
        junk, b_junk = sb("junk", [128, 512], F32, stack)
        tr_, b_tr = sb("tr_k", [128, 4, 32], F32, stack)
        W = w_in[l]
        blocks = tok_blocks(tiles)
        for bi, (t0, n, is_s) in enumerate(blocks):
            r0 = (c.SEQ + t0 - TP) if is_s else (p * TP + t0)
            T.dma(sp, ck[:n, bi, 0, :], k_cosk[r0:r0 + n, :], [], [b_ck], b_ck)
            T.dma(sp, ck[:n, bi, 1, :], k_sink[r0:r0 + n, :], [], [b_ck], b_ck)

        def tok_proj(c0, width):
            for j0 in range(0, width, 128):
                m = min(128, width - j0)
                s_, bs = load_slab(ring, W, KC, c0 + j0, m, "w_in")
                for bi, (t0, n, is_s) in enumerate(blocks):
                    pa, bps = ps_small(64) if False else (PS[bi % 2][0][:n, :m], PS[bi % 2][1])
                    mm_group(pa, bps, [(hm[:, k, t0:t0 + n], s_[:, k, :m]) for k in range(KC)], [bs, b_hm])
                    if (bi + j0 // 128) % 2:
                        T.op(act, [bps], [b_stg], lambda: nc.scalar.copy(out=stg[:n, bi, j0:j0 + m], in_=pa))
                    else:
                        T.op(dve, [bps], [b_stg], lambda: nc.vector.tensor_copy(out=stg[:n, bi, j0:j0 + m], in_=pa))

        def out_rows(dst_p, dst_s, col0, width):
            for bi, (t0, n, is_s) in enumerate(blocks):
                if is_s:
                    dst = dst_s[l, t0 - TP:t0 - TP + n, col0:col0 + width]
                else:
                    dst = dst_p[l, p * TP + t0:p * TP + t0 + n, col0:col0 + width]
                T.dma(sp, dst, stg[:n, bi, :width], [b_stg], [d_kout[l]], b_stg)

        tok_proj(c.okv, c.KVL)
        for bi, (t0, n, is_s) in enumerate(blocks):
            T.op(act, [b_stg], [b_junk, b_sm], lambda: nc.scalar.activation(
                out=junk[:n, :c.KVL], in_=stg[:n, bi, :c.KVL], func=AF.Square, accum_out=sm[:n, bi, 0:1]))
            T.op(act, [b_sm], [b_sm], lambda: nc.scalar.activation(out=sm[:n, bi, 1:2], in_=sm[:n, bi, 0:1],
                                                                  func=AF.Sqrt, bias=EPS, scale=1.0 / c.KVL))
            T.op(dve, [b_sm], [b_sm], lambda: nc.vector.reciprocal(out=sm[:n, bi, 2:3], in_=sm[:n, bi, 1:2]))
            T.op(dve, [b_stg, b_sm, b_gkv], [b_stg], lambda: nc.vector.scalar_tensor_tensor(
                out=stg[:n, bi, :c.KVL], in0=stg[:n, bi, :c.KVL], scalar=sm[:n, bi, 2:3], in1=gkv[:n, l, :],
                op0=ALU.mult, op1=ALU.mult))
        out_rows(o_ckv_p, o_ckv_s, 0, c.KVL)
        tok_proj(c.ope, 64)
        for bi, (t0, n, is_s) in enumerate(blocks):
            x1 = stg[:n, bi, 0:32]; x2 = stg[:n, bi, 32:64]
            cs_, sn_ = ck[:n, bi, 0, :], ck[:n, bi, 1, :]
            for (o_, a_, b2_) in ((0, x1, cs_), (1, x2, sn_), (2, x2, cs_), (3, x1, sn_)):
                T.op(dve, [b_stg, b_ck], [b_tr], lambda: nc.vector.tensor_tensor(out=tr_[:n, o_, :], in0=a_, in1=b2_,
                                                                              op=ALU.mult))
            T.op(dve, [b_tr], [b_stg], lambda: nc.vector.tensor_tensor(out=x1, in0=tr_[:n, 0, :], in1=tr_[:n, 1, :],
                                                                      op=ALU.subtract))
            T.op(dve, [b_tr], [b_stg], lambda: nc.vector.tensor_tensor(out=x2, in0=tr_[:n, 2, :], in1=tr_[:n, 3, :],
                                                                      op=ALU.add))
        out_rows(o_kpe_p, o_kpe_s, 0, 64)
        for (off, dp_, ds_) in ((c.oks, o_sbk_p, o_sbk_s), (c.ovs, o_sbv_p, o_sbv_s)):
            for c0 in range(0, c.SBW, 512):
                wd = min(512, c.SBW - c0)
                tok_proj(off + c0, wd)
                out_rows(dp_, ds_, c0, wd)

    def mix_pool(l, p, tiles, ring, stack):
        PWC, PGC = c.PWC, c.PGC
        EW = 15 + TP
        u, b_u = sb("u", [128, PWC, EW], F32, stack)
        us, b_us = sb("u_s", [128, c.NS, PWC, 15 + DS], F32, stack)
        ta, b_ta = sb("pl_a", [128, PGC, EW], F32, stack)
        tb_, b_tb = sb("pl_b", [128, PGC, EW], F32, stack)
        pld, b_pld = sb("pld", [128, PWC, NW], BF16, stack)
        wp, b_wp = sb("wp", [128, 4, PGC, c.PG], BF16, stack)
        hs, b_hs = sb("hs", [16, c.PW], F32, stack)
        ho, b_ho = sb("ho", [16, c.PW], F32, stack)
        W = w_in[l]
        for g in range(4):
            T.dma(pool, wp[:, g, :, :], w_pool[l, g].rearrange("(cc p) d -> p cc d", p=128), [], [b_wp], b_wp)
        T.op(dve, [b_uh], [b_u], lambda: nc.vector.tensor_copy(out=u[:, :, 0:15], in_=uh[:, l, :, :]))
        for j in range(PWC):
            def ev(pa, bps, t0, w, j=j):
                if t0 == 0:
                    T.op(act, [bps], [b_u], lambda: nc.scalar.copy(out=u[:, j, 15:15 + w], in_=pa))
                else:
                    for s_ in range(c.NS):
                        T.op(act, [bps], [b_us], lambda: nc.scalar.copy(out=us[:, s_, j, 15:15 + DS],
                                                                       in_=pa[:, s_ * DS:(s_ + 1) * DS]))
            proj_feat(ring, W, KC, c.ou + j * 128, 128, hm, b_hm, tiles, j % 2, ev)
        has_s = len(tiles) > 1
        if has_s:
            for s_ in range(c.NS):
                T.dma(sp, hs[:15, :], c_pool[l, s_], [], [b_hs], b_hs)
                for j in range(PWC):
                    pa, bps = ps_small(16)
                    T.op(pe, [b_hs, b_ident], [bps], lambda: nc.tensor.transpose(
                        out=pa[:, :15], in_=hs[:15, j * 128:(j + 1) * 128], identity=ident[:15, :15]))
                    T.op(act, [bps], [b_us], lambda: nc.scalar.copy(out=us[:, s_, j, 0:15], in_=pa[:, :15]))
        T.op(dve, [b_u], [b_uh], lambda: nc.vector.tensor_copy(out=uh[:, l, :, :], in_=u[:, :, TP:TP + 15]))

        def pool_one(ext_fn, b_ext, ew, out_fn, fix):
            for g, wdw in enumerate((2, 4, 8, 16)):
                ch0 = g * PGC
                e = ext_fn(ch0, PGC)
                cur, b_cur = e, b_ext
                sh = 1; tgl = 0; lo = 0
                while sh < wdw:
                    dst, b_dst = (ta, b_ta) if tgl == 0 else (tb_, b_tb)
                    d_ap = dst[:, :, :ew]
                    T.op(dve, [b_cur], [b_dst], lambda: nc.vector.tensor_tensor(
                        out=d_ap[:, :, lo + sh:ew], in0=cur[:, :, lo + sh:ew], in1=cur[:, :, lo:ew - sh], op=ALU.add))
                    cur, b_cur = d_ap, b_dst
                    lo += sh; sh *= 2; tgl ^= 1
                o = out_fn(ch0, PGC)
                T.op(dve, [b_cur, b_ext], [b_pld], lambda: nc.vector.scalar_tensor_tensor(
                    out=o, in0=cur[:, :, 15:ew], scalar=1.0 / wdw, in1=e[:, :, 15:ew], op0=ALU.mult, op1=ALU.subtract))
                if fix:
                    for cc in range(PGC):
                        T.op(dve, [b_cur, b_invc], [b_tr2], lambda: nc.vector.tensor_tensor(
                            out=tr2[:, 0:16], in0=cur[:, cc, 15:31], in1=invc[:, g, :], op=ALU.mult))
                        T.op(dve, [b_tr2, b_ext], [b_pld], lambda: nc.vector.tensor_tensor(
                            out=o[:, cc, 0:16], in0=tr2[:, 0:16], in1=e[:, cc, 15:31], op=ALU.subtract))

        tr2, b_tr2 = sb("tr2", [128, 16], F32, stack)
        pool_one(lambda ch0, n: u[:, ch0:ch0 + n, :], b_u, EW, lambda ch0, n: pld[:, ch0:ch0 + n, 0:TP], p == 0)
        if has_s:
            for s_ in range(c.NS):
                pool_one(lambda ch0, n: us[:, s_, ch0:ch0 + n, :], b_us, 15 + DS,
                         lambda ch0, n: pld[:, ch0:ch0 + n, TP + s_ * DS:TP + (s_ + 1) * DS], False)
        for g in range(4):
            for dc in range(PGC):
                for ti, (t0, w) in enumerate(tiles):
                    if ti == 0:
                        pst, bps = PS[(g + dc) % 2]; pa = pst[:, :w]
                    else:
                        pa, bps = ps_small(w)
                    mm_group(pa, bps, [(wp[:, g, cc, dc * 128:(dc + 1) * 128], pld[:, g * PGC + cc, t0:t0 + w])
                                       for cc in range(PGC)], [b_wp, b_pld])
                    ch = g * PGC + dc
                    T.op(act, [bps, b_gains], [b_hm], lambda: nc.scalar.activation(
                        out=hm[:, c.HM + ch, t0:t0 + w], in_=pa, func=AF.Copy,
                        scale=gains[:, l, GO_PS + ch:GO_PS + ch + 1]))
        outs = []
        if p == c.NPASS - 1:
            outs.append((lambda j: u[:, j, TP:TP + 15], b_u, o_pool_p[l]))
            for s_ in range(c.NS):
                outs.append((lambda j, s_=s_: us[:, s_, j, DS:DS + 15], b_us, o_pool_s[l, s_ * 15:(s_ + 1) * 15, :]))
        for (srcf, b_s, dst) in outs:
            for j in range(PWC):
                pa, bps = PS[7][0][:, :128], PS[7][1]
                T.op(pe, [b_s, b_ident], [bps], lambda: nc.tensor.transpose(out=pa[:15, :128], in_=srcf(j),
                                                                           identity=ident[:, :]))
                T.op(act, [bps], [b_ho], lambda: nc.scalar.copy(out=ho[:15, j * 128:(j + 1) * 128], in_=pa[:15, :128]))
            T.dma(sp, dst, ho[:15, :], [b_ho], [d_yout], b_ho)

    def attention(l, qn, b_qn, qp, b_qp, qs, b_qs, qc0, N, segs, diag_last, stack_bufs, ssq):
        B = stack_bufs
        (ssq_m, ssq_s) = ssq
        for g in range(c.NG):
            hm0 = g * c.HMG; hs0 = g * c.HSG
            T.dma(pool, B.wuk[:, :, :], w_uk[l].rearrange("(kc p) f -> p kc f", p=128)[:, :, hm0 * 128:(hm0 + c.HMG) * 128],
                  [], [B.b_wuk], B.b_wuk)
            T.dma(pool, B.wuv[:, :, :], w_uv[l].rearrange("(kc p) f -> p kc f", p=128)[:, :, hm0 * 128:(hm0 + c.HMG) * 128],
                  [], [B.b_wuv], B.b_wuv)
            T.op(dve, [], [B.b_accm], lambda: nc.vector.memset(B.accm[:, :, :], 0.0))
            T.op(dve, [], [B.b_accd], lambda: nc.vector.memset(B.accd[:, :, :], 0.0))
            T.op(pool, [], [B.b_accs], lambda: nc.gpsimd.memset(B.accs[:, :, :], 0.0))
            T.op(pool, [], [B.b_spsum], lambda: nc.gpsimd.memset(B.spsum[:, :, :], 0.0))
            groups = []
            for (srcs, ntok, deps) in segs:
                for k0 in range(0, ntok, TP):
                    groups.append((srcs, k0, min(TP, ntok - k0), deps))
            first = True
            for gi_ in range(len(groups) - 1, -1, -1):
                srcs, k0, nk, deps = groups[gi_]
                diag = diag_last and gi_ == len(groups) - 1
                nkb = (nk + 127) // 128
                blk = [(kb, min(128, nk - kb * 128)) for kb in range(nkb)]
                for kb, n in blk:
                    r0 = k0 + kb * 128
                    T.dma(pool, B.rckv[:n, kb, :], srcs["ckv"][r0:r0 + n, :], deps, [B.b_rckv], B.b_rckv)
                    for e in range(2):
                        T.dma(pool, B.rkpe[:n, kb, e * 64:(e + 1) * 64], srcs["kpe"][r0:r0 + n, :], deps, [B.b_rkpe], B.b_rkpe)
                    T.dma(pool, B.rsbk[:n, kb, :], srcs["sbk"][r0:r0 + n, hs0 * 128:(hs0 + c.HSG) * 128], deps,
                          [B.b_rsbk], B.b_rsbk)
                    T.dma(pool, B.vsb[:n, kb, :], srcs["sbv"][r0:r0 + n, hs0 * 128:(hs0 + c.HSG) * 128], deps,
                          [B.b_vsb], B.b_vsb)
                tpb = PS[4][0][:].bitcast(BF16)
                bt4 = PS[4][1]
                def transp(src_fn, b_src, dst_ap_fn, b_dst, alt):
                    def fn():
                        ins = None
                        for kb, n in blk:
                            ins = nc.tensor.transpose(out=tpb[:, kb * 128:kb * 128 + n], in_=src_fn(kb, n),
                                                      identity=identb[:n, :n])
                        return ins
                    T.op(pe, [b_src, b_identb], [bt4], fn)
                    if alt % 2:
                        T.op(act, [bt4], [b_dst], lambda: nc.scalar.copy(out=dst_ap_fn(), in_=tpb[:, :nk]))
                    else:
                        T.op(dve, [bt4], [b_dst], lambda: nc.vector.tensor_copy(out=dst_ap_fn(), in_=tpb[:, :nk]))
                for cc in range(c.KVC):
                    transp(lambda kb, n, cc=cc: B.rckv[:n, kb, cc * 128:(cc + 1) * 128], B.b_rckv,
                           lambda cc=cc: B.ckvT[:, cc, :nk], B.b_ckvT, cc)
                transp(lambda kb, n: B.rkpe[:n, kb, :], B.b_rkpe, lambda: B.kpeT[:, :nk], B.b_kpeT, 1)
                for hh in range(c.HSG):
                    transp(lambda kb, n, hh=hh: B.rsbk[:n, kb, hh * 128:(hh + 1) * 128], B.b_rsbk,
                           lambda hh=hh: B.ksbT[:, hh, :nk], B.b_ksbT, hh)
                for hh in range(c.HMG):
                    pst, bps = PS[5]
                    mm_group(pst[:, :nk], bps, [(B.wuk[:, cc, hh * 128:(hh + 1) * 128], B.ckvT[:, cc, :nk])
                                                for cc in range(c.KVC)], [B.b_wuk, B.b_ckvT])
                    if hh % 2:
                        T.op(act, [bps], [B.b_kn], lambda: nc.scalar.copy(out=B.kn[:, hh, :nk], in_=pst[:, :nk]))
                    else:
                        T.op(dve, [bps], [B.b_kn], lambda: nc.vector.tensor_copy(out=B.kn[:, hh, :nk], in_=pst[:, :nk]))
                VW = c.HMG * 128
                for kb, n in blk:
                    for v0 in range(0, VW, 512):
                        vw = min(512, VW - v0)
                        pst, bps = PS[5]
                        mm_group(pst[:n, :vw], bps, [(B.ckvT[:, cc, kb * 128:kb * 128 + n], B.wuv[:, cc, v0:v0 + vw])
                                                     for cc in range(c.KVC)], [B.b_wuv, B.b_ckvT])
                        if kb % 2:
                            T.op(act, [bps], [B.b_vm], lambda: nc.scalar.copy(out=B.vm[:n, kb, v0:v0 + vw], in_=pst[:n, :vw]))
                        else:
                            T.op(dve, [bps], [B.b_vm], lambda: nc.vector.tensor_copy(out=B.vm[:n, kb, v0:v0 + vw],
                                                                                     in_=pst[:n, :vw]))
                for hh in range(c.HMG):
                    h = hm0 + hh
                    po = 64 * (h % 2)
                    ops_, bo = PS[2]; dps, bdn = PS[3]
                    pts = []
                    for kb, n in blk:
                        sps, bs_ = PS[kb % 2]
                        mm_group(sps[:n, :N], bs_,
                                 [(B.kn[:, hh, kb * 128:kb * 128 + n], qn[:, h, qc0:qc0 + N]),
                                  (B.kpeT[po:po + 64, kb * 128:kb * 128 + n], qp[po:po + 64, h // 2, qc0:qc0 + N])],
                                 [B.b_kn, B.b_kpeT, b_qn, b_qp])
                        pt, b_pt = B.pt[kb % 2]
                        T.op(act, [bs_], [b_pt], lambda: nc.scalar.activation(out=pt[:n, :N], in_=sps[:n, :N], func=AF.Exp,
                                                                             scale=c.mla_scale))
                        if diag:
                            T.op(dve, [b_pt, b_mmla], [b_pt], lambda: nc.vector.tensor_tensor(
                                out=pt[:n, :N], in0=pt[:n, :N], in1=mmla[:n, kb, :N], op=ALU.mult))
                        pts.append((pt, b_pt, kb, n))
                        mm_group(ops_[:, :N], bo, [(B.vm[:n, kb, hh * 128:(hh + 1) * 128], pt[:n, :N])], [B.b_vm, b_pt])
                        T.op(dve, [bo, B.b_accm], [B.b_accm], lambda: nc.vector.tensor_tensor(
                            out=B.accm[:, hh, :N], in0=ops_[:, :N], in1=B.accm[:, hh, :N], op=ALU.add))
                        mm_group(dps[:, :N], bdn, [(onesb[:n, :], pt[:n, :N])], [b_onesb, b_pt])
                        T.op(pool if False else dve, [bdn, B.b_accd], [B.b_accd], lambda: nc.vector.tensor_tensor(
                            out=B.accd[:, hh, :N], in0=dps[:, :N], in1=B.accd[:, hh, :N], op=ALU.add))
                for hh in range(c.HSG):
                    h = hs0 + hh
                    for kb, n in reversed(blk):
                        zps, bz = PS[kb % 2]
                        mm_group(zps[:n, :N], bz, [(B.ksbT[:, hh, kb * 128:kb * 128 + n], qs[:, h, qc0:qc0 + N])],
                                 [B.b_ksbT, b_qs])
                        T.op(act, [bz], [B.b_e], lambda: nc.scalar.activation(out=B.e[:n, :N], in_=zps[:n, :N], func=AF.Exp,
                                                                             scale=c.sb_scale))
                        T.op(act, [B.b_e], [B.b_sp], lambda: nc.scalar.activation(out=B.sp[:n, :N], in_=B.e[:n, :N],
                                                                                  func=AF.Ln, bias=1.0, scale=1.0))
                        if diag:
                            T.op(dve, [B.b_sp, b_msb], [B.b_sp], lambda: nc.vector.tensor_tensor(
                                out=B.sp[:n, :N], in0=B.sp[:n, :N], in1=msb[:n, kb, :N], op=ALU.mult))
                        ips, bi_ = PS[3]
                        pairs = [(ut[:n, :n], B.sp[:n, :N])]
                        rd = [b_ut, B.b_sp]
                        if not (first and kb == blk[-1][0]):
                            pairs.append((onesf[:, :n], B.spsum[:, hh, :N])); rd += [b_onesf, B.b_spsum]
                        mm_group(ips[:n, :N], bi_, pairs, rd)
                        T.op(act, [bi_], [B.b_ex], lambda: nc.scalar.activation(out=B.ex[:n, :N], in_=ips[:n, :N],
                                                                               func=AF.Exp, scale=-1.0))
                        at, b_at = B.pt[kb % 2]
                        if diag:
                            T.op(dve, [B.b_e, B.b_ex], [B.b_e], lambda: nc.vector.tensor_tensor(
                                out=B.e[:n, :N], in0=B.e[:n, :N], in1=B.ex[:n, :N], op=ALU.mult))
                            T.op(dve, [B.b_e, b_msb], [b_at], lambda: nc.vector.tensor_tensor(
                                out=at[:n, :N], in0=B.e[:n, :N], in1=msb[:n, kb, :N], op=ALU.mult))
                        else:
                            T.op(dve, [B.b_e, B.b_ex], [b_at], lambda: nc.vector.tensor_tensor(
                                out=at[:n, :N], in0=B.e[:n, :N], in1=B.ex[:n, :N], op=ALU.mult))
                        T.op(pool, [B.b_sp, B.b_spsum], [B.b_spsum], lambda: nc.gpsimd.tensor_tensor(
                            out=B.spsum[:n, hh, :N], in0=B.spsum[:n, hh, :N], in1=B.sp[:n, :N], op=ALU.add))
                        ops_, bo = PS[2]
                        mm_group(ops_[:, :N], bo, [(B.vsb[:n, kb, hh * 128:(hh + 1) * 128], at[:n, :N])], [B.b_vsb, b_at])
                        T.op(dve, [bo, B.b_accs], [B.b_accs], lambda: nc.vector.tensor_tensor(
                            out=B.accs[:, hh, :N], in0=ops_[:, :N], in1=B.accs[:, hh, :N], op=ALU.add))
                first = False
            for hh in range(c.HMG):
                T.op(dve, [B.b_accd], [B.b_accd], lambda: nc.vector.reciprocal(out=B.accd[:, hh, :N], in_=B.accd[:, hh, :N]))
                T.op(dve, [B.b_accd, B.b_accm], [B.b_accm], lambda: nc.vector.tensor_tensor(
                    out=B.accm[:, hh, :N], in0=B.accm[:, hh, :N], in1=B.accd[:, hh, :N], op=ALU.mult))
                T.op(act, [B.b_accm], [B.b_sqb], lambda: nc.scalar.activation(out=B.sqb[:, hh, :N], in_=B.accm[:, hh, :N],
                                                                              func=AF.Square))
            pst, bps = PS[7]
            mm_group(pst[:, :N], bps, [(onesb[:, :], B.sqb[:, hh, :N]) for hh in range(c.HMG)], [b_onesb, B.b_sqb])
            T.op(dve, [bps, B.b_ssq], [B.b_ssq], lambda: nc.vector.tensor_tensor(
                out=ssq_m[:, qc0:qc0 + N], in0=pst[:, :N], in1=ssq_m[:, qc0:qc0 + N], op=ALU.add))
            for hh in range(c.HSG):
                T.op(act, [B.b_accs], [B.b_sqb], lambda: nc.scalar.activation(out=B.sqb[:, hh, :N], in_=B.accs[:, hh, :N],
                                                                              func=AF.Square))
            mm_group(pst[:, :N], bps, [(onesb[:, :], B.sqb[:, hh, :N]) for hh in range(c.HSG)], [b_onesb, B.b_sqb])
            T.op(dve, [bps, B.b_ssq], [B.b_ssq], lambda: nc.vector.tensor_tensor(
                out=ssq_s[:, qc0:qc0 + N], in0=pst[:, :N], in1=ssq_s[:, qc0:qc0 + N], op=ALU.add))
            for hh in range(c.HMG):
                T.op(act, [B.b_accm], [b_hm], lambda: nc.scalar.copy(out=hm[:, hm0 + hh, qc0:qc0 + N], in_=B.accm[:, hh, :N]))
            for hh in range(c.HSG):
                T.op(act, [B.b_accs], [b_hm], lambda: nc.scalar.copy(out=hm[:, c.HM + c.PWC + hs0 + hh, qc0:qc0 + N],
                                                                    in_=B.accs[:, hh, :N]))

    class NS_:
        pass

    def attn_stage(l, p, tiles, qn, b_qn, qp, b_qp, qs, b_qs, stack):
        B = NS_()
        KGB = c.KGB
        def mk(name, shape, dt):
            t, b = sb(name, shape, dt, stack); setattr(B, name, t); setattr(B, "b_" + name, b)
        mk("wuk", [128, c.KVC, c.HMG * 128], BF16); mk("wuv", [128, c.KVC, c.HMG * 128], BF16)
        mk("rckv", [128, KGB, c.KVL], BF16); mk("rkpe", [128, KGB, 128], BF16)
        mk("rsbk", [128, KGB, c.HSG * 128], BF16); mk("vsb", [128, KGB, c.HSG * 128], BF16)
        mk("ckvT", [128, c.KVC, TP], BF16); mk("kpeT", [128, TP], BF16); mk("ksbT", [128, c.HSG, TP], BF16)
        mk("kn", [128, c.HMG, TP], BF16); mk("vm", [128, KGB, c.HMG * 128], BF16)
        mk("accm", [128, c.HMG, TP], F32); mk("accd", [128, c.HMG, TP], F32)
        mk("accs", [128, c.HSG, TP], F32); mk("spsum", [128, c.HSG, TP], F32)
        mk("e", [128, TP], F32); mk("sp", [128, TP], F32); mk("ex", [128, TP], F32)
        B.sqb, B.b_sqb = B.kn, B.b_kn
        mk("ssq", [128, 2, NW], F32)
        B.pt = [sb("pt%d" % i, [128, TP], BF16, stack) for i in range(2)]
        T.op(dve, [], [B.b_ssq], lambda: nc.vector.memset(B.ssq[:, :, :], 0.0))
        ssq = (B.ssq[:, 0, :], B.ssq[:, 1, :])
        srcs = {"ckv": o_ckv_p[l], "kpe": o_kpe_p[l], "sbk": o_sbk_p[l], "sbv": o_sbv_p[l]}
        attention(l, qn, b_qn, qp, b_qp, qs, b_qs, 0, TP, [(srcs, (p + 1) * TP, [d_kout[l]])], True, B, ssq)
        if len(tiles) > 1:
            for s_ in range(c.NS):
                past = {"ckv": c_ckv[l, s_], "kpe": c_kpe[l, s_], "sbk": c_sbk[l, s_], "sbv": c_sbv[l, s_]}
                new = {"ckv": o_ckv_s[l, s_ * DS:(s_ + 1) * DS, :], "kpe": o_kpe_s[l, s_ * DS:(s_ + 1) * DS, :],
                       "sbk": o_sbk_s[l, s_ * DS:(s_ + 1) * DS, :], "sbv": o_sbv_s[l, s_ * DS:(s_ + 1) * DS, :]}
                attention(l, qn, b_qn, qp, b_qp, qs, b_qs, TP + s_ * DS, DS,
                          [(past, c.PAST, []), (new, DS, [d_kout[l]])], True, B, ssq)
        for (which, nh, goff, ch0) in ((0, c.HM, GO_MO, 0), (1, c.HS, GO_SO, c.HM + c.PWC)):
            for (t0, w) in tiles:
                T.op(act, [B.b_ssq], [B.b_ssq], lambda: nc.scalar.activation(
                    out=B.ssq[:, which, t0:t0 + w], in_=B.ssq[:, which, t0:t0 + w], func=AF.Sqrt, bias=EPS,
                    scale=1.0 / (nh * 128)))
                T.op(dve, [B.b_ssq], [B.b_ssq], lambda: nc.vector.reciprocal(out=B.ssq[:, which, t0:t0 + w],
                                                                            in_=B.ssq[:, which, t0:t0 + w]))
                for hh in range(nh):
                    eng, q = (nc.vector, dve)
                    T.op(q, [B.b_ssq, b_gains, b_hm], [b_hm], lambda: eng.scalar_tensor_tensor(
                        out=hm[:, ch0 + hh, t0:t0 + w], in0=hm[:, ch0 + hh, t0:t0 + w],
                        scalar=gains[:, l, goff + hh:goff + hh + 1], in1=B.ssq[:, which, t0:t0 + w],
                        op0=ALU.mult, op1=ALU.mult))

    for p in range(c.NPASS):
        tiles = pass_tiles(p)
        for l in range(L):
            with ExitStack() as st:
                x, b_x = sb("x", [128, KC, NW], F32, st)
                if l == 0:
                    with ExitStack() as st2:
                        load_x(p, x, b_x, tiles, st2)
                        T.barrier()
                else:
                    for (t0, w) in tiles:
                        T.dma(sp, x[:, :, t0:t0 + w], xscr[:, :, t0:t0 + w], [d_xscr], [b_x], b_x)
                ringA = Ring("ra", 4, [128, KC, 128], st); ringD = Ring("rd", 4, [128, c.D], st)
                actb = [sb("act%d" % i, [128, NW], BF16, st) for i in range(6)]
                silb = [sb("sil%d" % i, [128, TP], F32, st) for i in range(2)]
                rstd, b_rstd = sb("rstd", [128, NW], F32, st)
                rmsnorm_feat(x, b_x, KC, tiles, GO_F1, l, hm, b_hm, rstd, b_rstd)
                ffn(l, w1g, w1u, w1d, x, b_x, tiles, ringA, ringD, actb, silb)
                rmsnorm_feat(x, b_x, KC, tiles, GO_MIX, l, hm, b_hm, rstd, b_rstd)
                for (t0, w) in tiles:
                    T.dma(sp, xscr[:, :, t0:t0 + w], x[:, :, t0:t0 + w], [b_x], [d_xscr], b_x)
                T.barrier()
            with ExitStack() as st:
                qn, b_qn = sb("qn", [128, c.HM, NW], BF16, st)
                qp, b_qp = sb("qp", [128, c.HM // 2, NW], BF16, st)
                qs, b_qs = sb("qs", [128, c.HS, NW], BF16, st)
                with ExitStack() as st2:
                    ring = Ring("rq", 4, [128, KC, 128], st2)
                    mix_q(l, p, tiles, ring, qn, b_qn, qp, b_qp, qs, b_qs, st2)
                    T.barrier()
                with ExitStack() as st2:
                    ring = Ring("rk", 4, [128, KC, 128], st2)
                    mix_k(l, p, tiles, ring, st2)
                    mix_pool(l, p, tiles, ring, st2)
                    T.barrier()
                with ExitStack() as st2:
                    attn_stage(l, p, tiles, qn, b_qn, qp, b_qp, qs, b_qs, st2)
                    T.barrier()
            with ExitStack() as st:
                x, b_x = sb("x", [128, KC, NW], F32, st)
                for (t0, w) in tiles:
                    T.dma(sp, x[:, :, t0:t0 + w], xscr[:, :, t0:t0 + w], [d_xscr], [b_x], b_x)
                with ExitStack() as st2:
                    ring = Ring("rw", 4, [128, c.MIXC, 128], st2)
                    for m in range(KC):
                        def ev(pa, bps, t0, w, m=m):
                            T.op(dve, [bps, b_x], [b_x], lambda: nc.vector.tensor_tensor(
                                out=x[:, m, t0:t0 + w], in0=pa, in1=x[:, m, t0:t0 + w], op=ALU.add))
                        proj_feat(ring, w_out[l], c.MIXC, m * 128, 128, hm, b_hm, tiles, m % 2, ev, kind="w_out")
                    T.barrier()
                with ExitStack() as st2:
                    ringA = Ring("ra", 4, [128, KC, 128], st2); ringD = Ring("rd", 4, [128, c.D], st2)
                    actb = [sb("act%d" % i, [128, NW], BF16, st2) for i in range(6)]
                    silb = [sb("sil%d" % i, [128, TP], F32, st2) for i in range(2)]
                    rstd, b_rstd = sb("rstd", [128, NW], F32, st2)
                    rmsnorm_feat(x, b_x, KC, tiles, GO_F2, l, hm, b_hm, rstd, b_rstd)
                    ffn(l, w2g, w2u, w2d, x, b_x, tiles, ringA, ringD, actb, silb)
                    if l == L - 1:
                        rmsnorm_feat(x, b_x, KC, tiles, 0, l, x, b_x, rstd, b_rstd, gt=gfin, sq=hm, b_sq=b_hm)
                    T.barrier()
                if l == L - 1:
                    with ExitStack() as st2:
                        store_y(p, x, b_x, tiles, st2)
                        T.barrier()
                else:
                    for (t0, w) in tiles:
                        T.dma(sp, xscr[:, :, t0:t0 + w], x[:, :, t0:t0 + w], [b_x], [d_xscr], b_x)
                    T.barrier()
    T.barrier()
    es.close()
    return nc, T


def host_consts(c: Cfg):
    k = {}
    k["k_ident"] = np.eye(128, dtype=np.float32)
    j = np.arange(128)
    k["k_ut"] = (j[:, None] >= j[None, :]).astype(np.float32)
    r = np.zeros((64, 64), np.float32)
    for d in range(32):
        r[d, d + 32] = -1.0; r[d + 32, d] = 1.0
    rt = r.T.copy()
    R128 = np.zeros((128, 128), np.float32); R128[:64, :64] = rt; R128[64:, 64:] = rt
    k["k_r"] = R128
    kk = np.arange(128)[:, None]; q = np.arange(c.TP)[None, :]
    mm = np.zeros((c.KGB, 128, c.TP), np.float32); ms = np.zeros((c.KGB, 128, c.TP), np.float32)
    for kb in range(c.KGB):
        key = kb * 128 + kk
        mm[kb] = ((key // 64) <= (q // 64)).astype(np.float32)
        ms[kb] = (key < q).astype(np.float32)
    k["k_mmla"] = mm; k["k_msb"] = ms
    half = 32
    inv = (1.0 / (np.float32(ROPE_THETA) ** (np.arange(half, dtype=np.float32) / np.float32(half)))).astype(np.float32)
    pos = np.concatenate([np.arange(c.SEQ), c.PAST + np.tile(np.arange(c.DS), c.NS)]).astype(np.float32)
    ang = (pos[:, None] * inv[None, :]).astype(np.float32)
    cos = np.cos(ang).astype(np.float32); sin = np.sin(ang).astype(np.float32)
    k["k_cosk"] = cos; k["k_sink"] = sin
    cq = np.concatenate([cos, cos], axis=1).T
    sq = np.concatenate([sin, sin], axis=1).T
    k["k_cosq"] = np.ascontiguousarray(np.concatenate([cq, cq], axis=0), dtype=np.float32)
    k["k_sinq"] = np.ascontiguousarray(np.concatenate([sq, sq], axis=0), dtype=np.float32)
    ic = np.zeros((128, 4, 16), np.float32)
    for g, w in enumerate((2, 4, 8, 16)):
        ic[:, g, :] = 1.0 / np.minimum(np.arange(16) + 1, w).astype(np.float32)
    k["k_invc"] = ic
    return k


_CACHE = {}


def run(c: Cfg, inputs, n_cores=8, runner=None):
    key = (c.D, c.DFF, c.SEQ, c.TP, c.PAST)
    if key not in _CACHE:
        _CACHE[key] = build_program(c)
    nc, T = _CACHE[key]
    f = lambda a: np.ascontiguousarray(np.asarray(a, dtype=np.float32))
    I = {k: f(v) for k, v in inputs.items()}
    L = c.L
    CHL = chunk_lists(c)
    pk = lambda name, kind: pack_slabs(I[name], CHL[kind])
    shared = {
        "g_ffn1": I["g_ffn1"], "w1_gate": pk("w1_gate", "ffn"), "w1_up": pk("w1_up", "ffn"), "w1_down": I["w1_down"],
        "g_mix": I["g_mix"], "w_in": pk("w_in", "w_in"), "g_qnorm": I["g_qnorm"], "w_uq": pk("w_uq", "w_uq"),
        "g_kvnorm": I["g_kvnorm"],
        "w_uk": I["w_uk"].reshape(L, c.KVL, c.HM * 128), "w_uv": I["w_uv"].reshape(L, c.KVL, c.HM * 128),
        "w_pool": I["w_pool"], "pool_scale": I["pool_scale"], "g_mla_out": I["g_mla_out"], "g_sb_out": I["g_sb_out"],
        "w_out": pk("w_out", "w_out"), "g_ffn2": I["g_ffn2"], "w2_gate": pk("w2_gate", "ffn"), "w2_up": pk("w2_up", "ffn"),
        "w2_down": I["w2_down"],
        "g_final": I["g_final"],
    }
    shared.update(host_consts(c))
    nb = I["x_prompt"].shape[0]
    in_maps = []
    for core in range(n_cores):
        s0 = core * c.NS
        m = dict(shared)
        m["xp"] = I["x_prompt"][core % nb]
        m["xs"] = f(I["x_sample"][s0:s0 + c.NS].reshape(c.NST, c.D))
        m["c_ckv"] = f(I["cache_ckv"][:, s0:s0 + c.NS]); m["c_kpe"] = f(I["cache_kpe"][:, s0:s0 + c.NS])
        m["c_pool"] = f(I["state_pool"][:, s0:s0 + c.NS])
        m["c_sbk"] = f(I["cache_sb_k"][:, s0:s0 + c.NS].reshape(L, c.NS, c.PAST, c.SBW))
        m["c_sbv"] = f(I["cache_sb_v"][:, s0:s0 + c.NS].reshape(L, c.NS, c.PAST, c.SBW))
        in_maps.append(m)
    if runner is None:
        res = run_bass_kernel_spmd(nc, in_maps, core_ids=list(range(n_cores))).results
    else:
        res = runner(nc, in_maps)
    R = res
    st = lambda name, cores: np.stack([R[i][name] for i in cores], axis=0)
    pc = list(range(nb)); ac = list(range(n_cores))
    nsq = n_cores * c.NS
    y_prompt = st("y_p", pc)
    y_sample = st("y_s", ac).reshape(nsq, c.DS, c.D)
    ckv_p = st("o_ckv_p", pc).transpose(1, 0, 2, 3)
    kpe_p = st("o_kpe_p", pc).transpose(1, 0, 2, 3)
    pool_p = st("o_pool_p", pc).transpose(1, 0, 2, 3)
    sbk_p = st("o_sbk_p", pc).transpose(1, 0, 2, 3).reshape(L, nb, c.SEQ, c.HS, 128)
    sbv_p = st("o_sbv_p", pc).transpose(1, 0, 2, 3).reshape(L, nb, c.SEQ, c.HS, 128)
    def samp(name, rows, width):
        a = st(name, ac)
        a = a.reshape(n_cores, L, c.NS, rows, width).transpose(1, 0, 2, 3, 4).reshape(L, nsq, rows, width)
        return a
    ckv_s = samp("o_ckv_s", c.DS, c.KVL); kpe_s = samp("o_kpe_s", c.DS, 64)
    pool_s = samp("o_pool_s", 15, c.PW)
    sbk_s = samp("o_sbk_s", c.DS, c.SBW).reshape(L, nsq, c.DS, c.HS, 128)
    sbv_s = samp("o_sbv_s", c.DS, c.SBW).reshape(L, nsq, c.DS, c.HS, 128)
    outs = (y_prompt, y_sample, ckv_p, kpe_p, pool_p, sbk_p, sbv_p, ckv_s, kpe_s, pool_s, sbk_s, sbv_s)
    return tuple(np.ascontiguousarray(o, dtype=np.float32) for o in outs)


def kernel(**inputs):
    return run(Cfg(), inputs)
```
